# Optimizing a Trainium2 kernel written in Bass

```python
import jax, jax.numpy as jnp
from jax import lax
import numpy as np

D_MODEL = 1024
BATCH = 8
SEQ = 2048
DEPTH = 2
DEC_BATCH = 128
DEC_SEQ = 8
PAST_LEN = 16384
PAGE_SIZE = 128

H_A = 6
DK_A = 64
DV_A = 64
QK_A = H_A * DK_A
W_A = H_A * DV_A
CONV_A = 4
W_B = 256
LRU_BLOCKS = 4
LRU_BLOCK = W_B // LRU_BLOCKS
CONV_B = 4
LRU_C = 8.0
H_C = 6
DK_C = 32
DV_C = 64
QK_C = H_C * DK_C
W_C = H_C * DV_C
GLA_RANK = 16
GLA_TAU = 16.0
D_MIX = W_A + W_B + W_C
CHUNK = 64
D_FF = 2816
CONV_F = 3
ALPHA = (2.0 * DEPTH) ** 0.25
BETA_INIT = (8.0 * DEPTH) ** -0.25
EPS = 1e-6

SPLIT_SIZES = (QK_A, QK_A, W_A, W_A, H_A, H_A,
               W_B, W_B,
               QK_C, QK_C, W_C, W_C, GLA_RANK)
D_IN = sum(SPLIT_SIZES)
SPLIT_POINTS = tuple(int(s) for s in np.cumsum(SPLIT_SIZES)[:-1])
CONV_A_WIDTH = 2 * QK_A + W_A

kernel_name = 'hybrid_deltanet_rglru_gla_step'


def layer_norm(x, g, b):
    xf = x.astype(jnp.float32)
    mu = xf.mean(-1, keepdims=True)
    var = jnp.square(xf - mu).mean(-1, keepdims=True)
    return ((xf - mu) * lax.rsqrt(var + EPS) * g.astype(jnp.float32) + b.astype(jnp.float32)).astype(x.dtype)


def head_rms_norm(o, g):
    return o * lax.rsqrt(jnp.mean(jnp.square(o), -1, keepdims=True) + EPS) * g.astype(jnp.float32)


def l2norm(x):
    xf = x.astype(jnp.float32)
    return xf * lax.rsqrt(jnp.sum(jnp.square(xf), -1, keepdims=True) + EPS)


def causal_dwconv(x, buf, w):
    width = w.shape[0]
    T = x.shape[1]
    xp = jnp.concatenate([buf.astype(x.dtype), x], axis=1)
    y = xp[:, 0:T] * w[0]
    for i in range(1, width):
        y = y + xp[:, i:i + T] * w[i]
    return y, xp[:, -(width - 1):]


def chunk_size(T):
    return CHUNK if T % CHUNK == 0 else T


def to_chunks(x, c):
    B, T, H, D = x.shape
    return x.reshape(B, T // c, c, H, D).transpose(1, 0, 3, 2, 4)


def from_chunks(x):
    n, B, H, c, D = x.shape
    return x.transpose(1, 0, 3, 2, 4).reshape(B, n * c, H, D)


def gated_delta_chunked(q, k, v, beta, g, s0):
    f32 = jnp.float32
    T = q.shape[1]
    dv = v.shape[-1]
    c = chunk_size(T)
    qc, kc, vc = (to_chunks(t.astype(f32), c) for t in (q, k, v))
    bc = to_chunks(beta.astype(f32)[..., None], c)[..., 0]
    gc = to_chunks(g.astype(f32)[..., None], c)[..., 0]
    incl = jnp.tril(jnp.ones((c, c), bool))
    strict = jnp.tril(jnp.ones((c, c), bool), -1)
    eye = jnp.eye(c, dtype=f32)

    def step(S, inp):
        qi, ki, vi, bi, gi = inp
        gcum = jnp.cumsum(gi, axis=-1)
        diff = gcum[..., :, None] - gcum[..., None, :]
        dec_incl = jnp.exp(jnp.where(incl, diff, -jnp.inf))
        dec_strict = jnp.where(strict, dec_incl, 0.0)
        kk = jnp.einsum('bhid,bhjd->bhij', ki, ki)
        lhs = eye + bi[..., :, None] * kk * dec_strict
        rhs = jnp.concatenate([bi[..., None] * vi, (bi * jnp.exp(gcum))[..., None] * ki], axis=-1)
        sol = lax.linalg.triangular_solve(lhs, rhs, left_side=True, lower=True, unit_diagonal=True)
        u, wk = sol[..., :dv], sol[..., dv:]
        w = u - jnp.einsum('bhik,bhkv->bhiv', wk, S)
        qk = jnp.einsum('bhid,bhjd->bhij', qi, ki) * dec_incl
        o = jnp.exp(gcum)[..., None] * jnp.einsum('bhik,bhkv->bhiv', qi, S) + jnp.einsum('bhij,bhjv->bhiv', qk, w)
        g_last = gcum[..., -1:]
        k_dec = ki * jnp.exp(g_last - gcum)[..., None]
        S_new = jnp.exp(g_last)[..., None] * S + jnp.einsum('bhjk,bhjv->bhkv', k_dec, w)
        return S_new, o

    S, o = lax.scan(step, s0.astype(f32), (qc, kc, vc, bc, gc))
    return from_chunks(o), S


def gla_chunked(q, k, v, logf, s0):
    f32 = jnp.float32
    T = q.shape[1]
    c = chunk_size(T)
    qc, kc, vc, fc = (to_chunks(t.astype(f32), c) for t in (q, k, v, logf))
    incl = jnp.tril(jnp.ones((c, c), bool))[..., None]

    def step(S, inp):
        qi, ki, vi, fi = inp
        b = jnp.cumsum(fi, axis=-2)
        diff = b[..., :, None, :] - b[..., None, :, :]
        dec = jnp.exp(jnp.where(incl, diff, -jnp.inf))
        att = jnp.einsum('bhik,bhjk,bhijk->bhij', qi, ki, dec)
        o = jnp.einsum('bhik,bhkv->bhiv', qi * jnp.exp(b), S) + jnp.einsum('bhij,bhjv->bhiv', att, vi)
        b_last = b[..., -1:, :]
        S_new = jnp.exp(b_last[..., 0, :])[..., None] * S + jnp.einsum('bhjk,bhjv->bhkv', ki * jnp.exp(b_last - b), vi)
        return S_new, o

    S, o = lax.scan(step, s0.astype(f32), (qc, kc, vc, fc))
    return from_chunks(o), S


def rg_lru(xc, h0, w_r, b_r, w_i, b_i, lam):
    f32 = jnp.float32
    B, T, _ = xc.shape
    xf = xc.astype(f32)
    xb = xf.reshape(B, T, LRU_BLOCKS, LRU_BLOCK)
    r = jax.nn.sigmoid(jnp.einsum('btnd,nde->btne', xb, w_r.astype(f32)).reshape(B, T, W_B) + b_r.astype(f32))
    i = jax.nn.sigmoid(jnp.einsum('btnd,nde->btne', xb, w_i.astype(f32)).reshape(B, T, W_B) + b_i.astype(f32))
    log_a = -LRU_C * r * jax.nn.softplus(-lam.astype(f32))
    a = jnp.exp(log_a)
    bx = jnp.sqrt(-jnp.expm1(2.0 * log_a)) * (i * xf)
    bx = bx.at[:, 0].add(a[:, 0] * h0.astype(f32))

    def combine(left, right):
        a1, b1 = left
        a2, b2 = right
        return a1 * a2, a2 * b1 + b2

    _, h = lax.associative_scan(combine, (a, bx), axis=1)
    return h, h[:, -1]


def conv_ffn(x, buf, w_up, conv_w, conv_b, w_down):
    up = x @ w_up
    gate, val = jnp.split(up, [D_FF], axis=-1)
    gate, buf_new = causal_dwconv(gate, buf, conv_w)
    h = jax.nn.gelu(gate + conv_b) * val
    return h @ w_down, buf_new


def _layer(x, st_dconv, st_delta, st_lconv, st_lru, st_gla, st_fconv,
           w_in, conv_a_w, a_log, dt_bias, norm_a_w, conv_b_w, conv_b_b,
           lru_w_r, lru_b_r, lru_w_i, lru_b_i, lru_lambda, gla_w2, gla_b2, norm_c_w,
           w_out, ln1_g, ln1_b, ffn_w_up, ffn_conv_w, ffn_conv_b, ffn_w_down, ln2_g, ln2_b):
    f32 = jnp.float32
    B, T, _ = x.shape
    proj = x @ w_in
    (qa, ka, va, za, ba, aa, xb, gb, qc, kc, vc, zc, lc) = jnp.split(proj, SPLIT_POINTS, axis=-1)

    qkv, dconv_new = causal_dwconv(jnp.concatenate([qa, ka, va], axis=-1), st_dconv, conv_a_w)
    qkv = jax.nn.silu(qkv)
    qa, ka, va = jnp.split(qkv, [QK_A, 2 * QK_A], axis=-1)
    qa = l2norm(qa.reshape(B, T, H_A, DK_A)) * DK_A ** -0.5
    ka = l2norm(ka.reshape(B, T, H_A, DK_A))
    va = va.reshape(B, T, H_A, DV_A)
    beta = jax.nn.sigmoid(ba.astype(f32))
    g = -jnp.exp(a_log.astype(f32)) * jax.nn.softplus(aa.astype(f32) + dt_bias.astype(f32))
    oa, delta_new = gated_delta_chunked(qa, ka, va, beta, g, st_delta)
    oa = head_rms_norm(oa, norm_a_w) * jax.nn.silu(za.reshape(B, T, H_A, DV_A).astype(f32))

    xb, lconv_new = causal_dwconv(xb, st_lconv, conv_b_w)
    ob, lru_new = rg_lru(xb + conv_b_b, st_lru, lru_w_r, lru_b_r, lru_w_i, lru_b_i, lru_lambda)
    ob = ob * jax.nn.gelu(gb.astype(f32))

    qc = qc.reshape(B, T, H_C, DK_C).astype(f32) * DK_C ** -0.5
    kc = kc.reshape(B, T, H_C, DK_C)
    vc = vc.reshape(B, T, H_C, DV_C)
    logf = jax.nn.log_sigmoid((lc @ gla_w2 + gla_b2).astype(f32)) / GLA_TAU
    oc, gla_new = gla_chunked(qc, kc, vc, logf.reshape(B, T, H_C, DK_C), st_gla)
    oc = head_rms_norm(oc, norm_c_w) * jax.nn.silu(zc.reshape(B, T, H_C, DV_C).astype(f32))

    heads = jnp.concatenate([oa.reshape(B, T, W_A), ob, oc.reshape(B, T, W_C)], axis=-1).astype(x.dtype)
    x = layer_norm(ALPHA * x + heads @ w_out, ln1_g, ln1_b)
    f, fconv_new = conv_ffn(x, st_fconv, ffn_w_up, ffn_conv_w, ffn_conv_b, ffn_w_down)
    x = layer_norm(ALPHA * x + f, ln2_g, ln2_b)
    return x, (dconv_new, delta_new, lconv_new, lru_new, gla_new, fconv_new)


def _trunk(x, states, weights):
    per_layer = []
    for l in range(DEPTH):
        x, st = _layer(x, *(s[l] for s in states), *(w[l] for w in weights))
        per_layer.append(st)
    stacked = tuple(jnp.stack([st[i] for st in per_layer]).astype(x.dtype) for i in range(len(states)))
    return x, stacked


def setup_inputs(seed: int = 0) -> dict:
    key = jax.random.key(seed)
    ks = iter(jax.random.split(key, 40))
    nrm = lambda shape, s: jax.random.normal(next(ks), shape, jnp.float32) * s
    L = DEPTH
    u_decay = jax.random.uniform(next(ks), (L, H_A), jnp.float32, 1.0, 16.0)
    dt = jnp.exp(jax.random.uniform(next(ks), (L, H_A), jnp.float32, np.log(1e-3), np.log(1e-1)))
    a8 = jax.random.uniform(next(ks), (L, W_B), jnp.float32, 0.9, 0.999)
    a1 = a8 ** (1.0 / LRU_C)
    return {
        'x_prompt': nrm((BATCH, SEQ, D_MODEL), 1.0),
        'x_sample': nrm((DEC_BATCH, DEC_SEQ, D_MODEL), 1.0),
        'state_delta_conv': nrm((L, DEC_BATCH, CONV_A - 1, CONV_A_WIDTH), 1.0),
        'state_delta': nrm((L, DEC_BATCH, H_A, DK_A, DV_A), 0.1),
        'state_lru_conv': nrm((L, DEC_BATCH, CONV_B - 1, W_B), 1.0),
        'state_lru': nrm((L, DEC_BATCH, W_B), 0.5),
        'state_gla': nrm((L, DEC_BATCH, H_C, DK_C, DV_C), 0.5),
        'state_ffn_conv': nrm((L, DEC_BATCH, CONV_F - 1, D_FF), 1.0),
        'w_in': nrm((L, D_MODEL, D_IN), D_MODEL ** -0.5),
        'conv_a_w': nrm((L, CONV_A, CONV_A_WIDTH), 0.5),
        'a_log': jnp.log(u_decay),
        'dt_bias': dt + jnp.log(-jnp.expm1(-dt)),
        'norm_a_w': 1.0 + nrm((L, DV_A), 0.1),
        'conv_b_w': nrm((L, CONV_B, W_B), 0.5),
        'conv_b_b': nrm((L, W_B), 0.02),
        'lru_w_r': nrm((L, LRU_BLOCKS, LRU_BLOCK, LRU_BLOCK), LRU_BLOCK ** -0.5),
        'lru_b_r': nrm((L, W_B), 0.02),
        'lru_w_i': nrm((L, LRU_BLOCKS, LRU_BLOCK, LRU_BLOCK), LRU_BLOCK ** -0.5),
        'lru_b_i': nrm((L, W_B), 0.02),
        'lru_lambda': jnp.log(a1) - jnp.log1p(-a1),
        'gla_w2': nrm((L, GLA_RANK, QK_C), GLA_RANK ** -0.5),
        'gla_b2': nrm((L, QK_C), 0.1),
        'norm_c_w': 1.0 + nrm((L, DV_C), 0.1),
        'w_out': nrm((L, D_MIX, D_MODEL), D_MIX ** -0.5 * BETA_INIT),
        'ln1_g': 1.0 + nrm((L, D_MODEL), 0.05),
        'ln1_b': nrm((L, D_MODEL), 0.02),
        'ffn_w_up': nrm((L, D_MODEL, 2 * D_FF), D_MODEL ** -0.5),
        'ffn_conv_w': nrm((L, CONV_F, D_FF), 0.5),
        'ffn_conv_b': nrm((L, D_FF), 0.02),
        'ffn_w_down': nrm((L, D_FF, D_MODEL), D_FF ** -0.5 * BETA_INIT),
        'ln2_g': 1.0 + nrm((L, D_MODEL), 0.05),
        'ln2_b': nrm((L, D_MODEL), 0.02),
    }


def reference(x_prompt, x_sample, state_delta_conv, state_delta, state_lru_conv, state_lru, state_gla, state_ffn_conv,
              w_in, conv_a_w, a_log, dt_bias, norm_a_w, conv_b_w, conv_b_b,
              lru_w_r, lru_b_r, lru_w_i, lru_b_i, lru_lambda, gla_w2, gla_b2, norm_c_w,
              w_out, ln1_g, ln1_b, ffn_w_up, ffn_conv_w, ffn_conv_b, ffn_w_down, ln2_g, ln2_b):
    weights = (w_in, conv_a_w, a_log, dt_bias, norm_a_w, conv_b_w, conv_b_b,
               lru_w_r, lru_b_r, lru_w_i, lru_b_i, lru_lambda, gla_w2, gla_b2, norm_c_w,
               w_out, ln1_g, ln1_b, ffn_w_up, ffn_conv_w, ffn_conv_b, ffn_w_down, ln2_g, ln2_b)
    sample_states = (state_delta_conv, state_delta, state_lru_conv, state_lru, state_gla, state_ffn_conv)
    prompt_states = tuple(jnp.zeros((DEPTH, BATCH) + s.shape[2:], x_prompt.dtype) for s in sample_states)
    y_prompt, (p_dconv, p_delta, p_lconv, p_lru, p_gla, p_fconv) = _trunk(x_prompt, prompt_states, weights)
    y_sample, (s_dconv, s_delta, s_lconv, s_lru, s_gla, s_fconv) = _trunk(x_sample, sample_states, weights)
    return (y_prompt, y_sample,
            p_dconv, p_delta, p_lconv, p_lru, p_gla, p_fconv,
            s_dconv, s_delta, s_lconv, s_lru, s_gla, s_fconv)
```

```python
import bisect
import contextlib
import numpy as np
import concourse.bass as bass
import concourse.mybir as mybir
from concourse.bass_utils import run_bass_kernel_spmd

F32 = mybir.dt.float32
F32R = mybir.dt.float32r
AF = mybir.ActivationFunctionType
ALU = mybir.AluOpType
AX = mybir.AxisListType

D = 1024
DFF = 2816
ALPHA = 4.0 ** 0.25
EPS = 1e-6
NCORES = 8
EPOCH = 8192
STRICT_SAME_ENGINE = True
ENGS = ("pe", "dve", "act", "pool", "sp")


class Prog:
    def __init__(self, nc):
        self.nc = nc
        self.ops = []

    def op(self, eng, fn, reads=(), writes=(), group=None):
        self.ops.append((eng, fn, tuple(reads), tuple(writes), group))

    def build(self):
        nc = self.nc
        ops = self.ops
        n = len(ops)
        last_w = {}
        readers = {}
        deps = [None] * n
        for i, (eng, fn, rd, wr, grp) in enumerate(ops):
            d = set()
            for k in rd:
                j = last_w.get(k)
                if j is not None:
                    d.add((j, True))
                if k.startswith("ps"):
                    for r in readers.get(k, ()):
                        if ops[r][0] != eng:
                            d.add((r, False))
            for k in wr:
                j = last_w.get(k)
                if j is not None:
                    d.add((j, False))
                for r in readers.get(k, ()):
                    d.add((r, False))
            deps[i] = d
            for k in rd:
                readers.setdefault(k, []).append(i)
            for k in wr:
                last_w[k] = i
                readers[k] = []
        has_consumer = [False] * n
        fdeps = [None] * n
        for i, (eng, fn, rd, wr, grp) in enumerate(ops):
            raw = {j for (j, r) in deps[i] if r}
            nd = set()
            for (j, r) in deps[i]:
                if j == i:
                    continue
                pe, _, _, _, pg = ops[j]
                if pe == eng and pg is None:
                    if eng == "pe":
                        continue
                    if j not in raw and not STRICT_SAME_ENGINE:
                        continue
                nd.add(j)
            fdeps[i] = nd
            for j in nd:
                has_consumer[j] = True
        eng_cnt = {e: 0 for e in ENGS}
        grp_ops = {}
        sig = [None] * n
        for i, (eng, fn, rd, wr, grp) in enumerate(ops):
            if grp is not None:
                grp_ops.setdefault(grp, []).append(i)
                sig[i] = ("g", grp, 0)
            elif has_consumer[i]:
                c = eng_cnt[eng]
                eng_cnt[eng] = c + 1
                sig[i] = ("e", (eng, c // EPOCH), (c % EPOCH) + 1)
        sem_keys = []
        seen_keys = set()
        for s in sig:
            if s is not None and (s[0], s[1]) not in seen_keys:
                seen_keys.add((s[0], s[1]))
                sem_keys.append((s[0], s[1]))
        stack = contextlib.ExitStack()
        sems = {}
        for num, k in enumerate(sem_keys):
            sems[k] = stack.enter_context(nc.semaphore("s%d" % num))
        per_eng = {e: [] for e in ENGS}
        for i, o in enumerate(ops):
            per_eng[o[0]].append(i)
        group_owner = {}
        for i, o in enumerate(ops):
            if o[4] is not None:
                group_owner.setdefault(o[4], o[0])
        self.stats = dict(n_ops=n, n_sems=len(sem_keys), per_eng={e: len(v) for e, v in per_eng.items()})

        def emit_engine(eng_name, eobj):
            seen = {}
            for i in per_eng[eng_name]:
                eng, fn, rd, wr, grp = ops[i]
                need = {}
                for j in fdeps[i]:
                    kind, key, val = sig[j]
                    if kind == "g":
                        val = 16 * bisect.bisect_left(grp_ops[key], i)
                    kk = (kind, key)
                    if val > need.get(kk, 0):
                        need[kk] = val
                for kk, val in need.items():
                    if seen.get(kk, 0) >= val:
                        continue
                    seen[kk] = val
                    eobj.wait_ge(sems[kk], val)
                inst = fn(eobj)
                s = sig[i]
                if s is not None:
                    inst.then_inc(sems[(s[0], s[1])], 16 if s[0] == "g" else 1)
            for g, lst in grp_ops.items():
                if group_owner[g] == eng_name:
                    eobj.wait_ge(sems[("g", g)], 16 * len(lst))

        with stack:
            with nc.Block() as block:
                @block.tensor
                def _(e):
                    emit_engine("pe", e)

                @block.vector
                def _(e):
                    emit_engine("dve", e)

                @block.scalar
                def _(e):
                    emit_engine("act", e)

                @block.gpsimd
                def _(e):
                    emit_engine("pool", e)

                @block.sync
                def _(e):
                    emit_engine("sp", e)
        return self.stats


class V:
    def __init__(self, ap, keys):
        self.ap = ap
        self.k = tuple(keys)

    def __getitem__(self, idx):
        return V(self.ap[idx], self.k)

    def re(self, pat, **kw):
        return V(self.ap.rearrange(pat, **kw), self.k)

    def bc(self, axis, shape):
        return V(self.ap.unsqueeze(axis).to_broadcast(list(shape)), self.k)

    def kk(self, *keys):
        return V(self.ap, keys)

    def r(self):
        return V(self.ap.bitcast(F32R), self.k)

    def f(self):
        return V(self.ap.bitcast(F32), self.k)


def _ks(*vs):
    out = []
    for v in vs:
        if isinstance(v, V):
            out.extend(v.k)
    return out


def _a(v):
    return v.ap if isinstance(v, V) else v


class Builder:
    def __init__(self, nc):
        self.nc = nc
        self.P = Prog(nc)
        self.stack = contextlib.ExitStack()
        self.psn = 0
        self.ps = []
        self.uid = 0

    def sb(self, name, shape, dt=F32, key=None):
        t = self.stack.enter_context(self.nc.sbuf_tensor("sb_" + name, list(shape), dt))
        return V(t[:], [key or name])

    def init_psum(self):
        for i in range(8):
            t = self.stack.enter_context(self.nc.psum_tensor("ps%d" % i, [128, 512], F32))
            self.ps.append(V(t[:], ["ps%d" % i]))

    def nps(self):
        v = self.ps[self.psn % 8]
        self.psn += 1
        return v

    def mm(self, out, lhsT, rhs, start=True, stop=True):
        rd = _ks(lhsT, rhs) + ([] if start else _ks(out))
        self.P.op("pe", lambda e: e.matmul(out.ap, lhsT=lhsT.ap, rhs=rhs.ap, start=start, stop=stop), rd, _ks(out))

    def tr(self, out, in_, ident):
        self.P.op("pe", lambda e: e.transpose(out.ap, in_.ap, ident.ap), _ks(in_, ident), _ks(out))

    def act(self, out, in_, func, bias=None, scale=None, eng="act"):
        kw = {}
        if bias is not None:
            kw["bias"] = _a(bias)
        if scale is not None:
            kw["scale"] = _a(scale)
        self.P.op("act", lambda e: e.activation(out=out.ap, in_=in_.ap, func=func, **kw), _ks(in_, bias, scale), _ks(out))

    def tt(self, out, a, b, op, eng="dve"):
        self.P.op(eng, lambda e: e.tensor_tensor(out=out.ap, in0=a.ap, in1=b.ap, op=op), _ks(a, b), _ks(out))

    def ts(self, out, a, s1, op0, s2=None, op1=None, eng="dve"):
        if op1 is None:
            self.P.op(eng, lambda e: e.tensor_scalar(out=out.ap, in0=a.ap, scalar1=_a(s1), scalar2=None, op0=op0), _ks(a, s1), _ks(out))
        else:
            self.P.op(eng, lambda e: e.tensor_scalar(out=out.ap, in0=a.ap, scalar1=_a(s1), scalar2=_a(s2), op0=op0, op1=op1),
                      _ks(a, s1, s2), _ks(out))

    def stt(self, out, a, s, b, op0, op1):
        self.P.op("dve", lambda e: e.scalar_tensor_tensor(out=out.ap, in0=a.ap, scalar=_a(s), in1=b.ap, op0=op0, op1=op1),
                  _ks(a, s, b), _ks(out))

    def cp(self, out, in_, eng="dve"):
        if eng == "act":
            self.P.op("act", lambda e: e.activation(out=out.ap, in_=in_.ap, func=AF.Copy), _ks(in_), _ks(out))
        else:
            self.P.op(eng, lambda e: e.tensor_copy(out=out.ap, in_=in_.ap), _ks(in_), _ks(out))

    def red(self, out, in_, op=ALU.add):
        self.P.op("dve", lambda e: e.tensor_reduce(out=out.ap, in_=in_.ap, axis=AX.X, op=op), _ks(in_), _ks(out))

    def recip(self, out, in_):
        self.P.op("dve", lambda e: e.reciprocal(out=out.ap, in_=in_.ap), _ks(in_), _ks(out))

    def scan(self, out, d0, d1, init):
        self.P.op("dve", lambda e: e.tensor_tensor_scan(out=out.ap, data0=d0.ap, data1=d1.ap, initial=_a(init), op0=ALU.mult, op1=ALU.add),
                  _ks(d0, d1, init), _ks(out))

    def memset(self, out, val, eng="dve"):
        self.P.op(eng, lambda e: e.memset(out.ap, val), [], _ks(out))

    def dma(self, out, in_, eng="sp", group=None, nc_ok=False):
        self.uid += 1
        g = group or ("d%d" % self.uid)
        if nc_ok:
            self.P.op(eng, lambda e: e.dma_start(out=out.ap, in_=in_.ap, allow_slow_non_contiguous=True), _ks(in_), _ks(out), group=g)
        else:
            self.P.op(eng, lambda e: e.dma_start(out=out.ap, in_=in_.ap), _ks(in_), _ks(out), group=g)


NWI = 18 * 128 + 4 * 512 + 256
FM_SRC = [(128 * b, 128) for b in range(9)] + [(1548, 128), (1676, 128), (1804, 128), (1932, 128),
                                                (2060, 96), (2156, 96), (2252, 96), (2348, 96), (3212, 16)]
TM_SRC = [(1152, 396), (2444, 384), (2828, 384), (2252, 192)]
NPF = 172
PF_CA, PF_CB, PF_CBB, PF_LBR, PF_LBI, PF_LAM, PF_L1G, PF_L1B, PF_L2G, PF_L2B, PF_FCW, PF_FCB = 0, 36, 44, 46, 48, 50, 52, 60, 68, 76, 84, 150
NRB = 140
NEG = -30000.0


def _mask_set(c):
    idx = np.arange(128)
    seg = idx // c
    same = seg[:, None] == seg[None, :]
    ns = 128 // c
    tri = (same & (idx[:, None] <= idx[None, :])).astype(np.float32)
    segm = same.astype(np.float32)
    neg_strit = np.where(same & (idx[None, :] < idx[:, None]), 0.0, NEG).astype(np.float32)
    neg_tri = np.where(same & (idx[:, None] <= idx[None, :]), 0.0, NEG).astype(np.float32)
    seg01 = np.zeros((128, 16), np.float32)
    seg01[idx, seg] = 1.0
    last01 = np.zeros((128, 16), np.float32)
    li = (idx % c) == (c - 1)
    last01[idx[li], seg[li]] = 1.0
    return np.concatenate([tri, segm, neg_strit, neg_tri, seg01, last01], axis=1)


C_ID, C_ONE, C_B64, C_MP, C_MS = 0, 128, 256, 384, 384 + 544
NCONST = 384 + 2 * 544
M_TRI, M_SEGM, M_NSTRIT, M_NTRI, M_SEG, M_LAST = 0, 128, 256, 384, 512, 528


WI_PANELS = [(0, 256), (256, 256), (512, 256), (768, 256), (1024, 128), (2304, 256), (2560, 128), (2688, 12),
             (1152, 256), (1408, 256), (1664, 256), (1920, 256), (2176, 16),
             (2816, 256), (4352, 256), (3328, 256), (3840, 256)]
WO_PANELS = [(0, 256), (256, 256), (512, 256), (768, 256)]
WU_PANELS = [(256 * gb, 256) for gb in range(22)]


def _panel_offsets(panels):
    offs, o = {}, 0
    for (c0, n) in panels:
        offs[(c0, n)] = o
        o += 8 * n
    return offs, o


WI_OFF, WI_TOT = _panel_offsets(WI_PANELS)
WO_OFF, WO_TOT = _panel_offsets(WO_PANELS)
WU_OFF, WU_TOT = _panel_offsets(WU_PANELS)


def _pack_panels(w, panels, tot):
    L = w.shape[0]
    out = np.zeros((L, 128, tot), np.float32)
    o = 0
    for (c0, n) in panels:
        blk = w[:, :, c0:c0 + n].reshape(L, 8, 128, n).transpose(0, 2, 1, 3).reshape(L, 128, 8 * n)
        out[:, :, o:o + 8 * n] = blk
        o += 8 * n
    return out


def make_consts():
    ident = np.eye(128, dtype=np.float32)
    ones = np.ones((128, 128), np.float32)
    b64 = np.zeros((128, 128), np.float32)
    b64[:64, :64] = 1.0
    b64[64:, 64:] = 1.0
    return np.ascontiguousarray(np.concatenate([ident, ones, b64, _mask_set(128), _mask_set(8)], axis=1))


def pack_weights(w):
    L = 2
    wi = np.zeros((L, D, NWI), np.float32)
    for b, (s, n) in enumerate(FM_SRC):
        wi[:, :, 128 * b:128 * b + n] = w["w_in"][:, :, s:s + n]
    for g, (s, n) in enumerate(TM_SRC):
        wi[:, :, 2304 + 512 * g:2304 + 512 * g + n] = w["w_in"][:, :, s:s + n]
    wi[:, :, 4352:4480] = w["w_in"][:, :, 2444 + 256:2444 + 384]
    wi[:, :, 4480:4608] = w["w_in"][:, :, 2828 + 256:2828 + 384]
    wu = np.zeros((L, D, 2 * DFF), np.float32)
    for gb in range(22):
        wu[:, :, 256 * gb:256 * gb + 128] = w["ffn_w_up"][:, :, 128 * gb:128 * gb + 128]
        wu[:, :, 256 * gb + 128:256 * gb + 256] = w["ffn_w_up"][:, :, DFF + 128 * gb:DFF + 128 * gb + 128]
    wd = np.ascontiguousarray(w["ffn_w_down"].reshape(L, 22, 128, 8, 128).transpose(0, 3, 2, 1, 4)).reshape(L, 8, 128, 22 * 128)
    pf = np.zeros((L, 128, NPF), np.float32)

    def pc(a):
        return a.reshape(L, -1, 128).transpose(0, 2, 1)
    for i in range(4):
        pf[:, :, PF_CA + i:PF_CA + 36:4] = pc(w["conv_a_w"][:, i])
        pf[:, :, PF_CB + i:PF_CB + 8:4] = pc(w["conv_b_w"][:, i])
    pf[:, :, PF_CBB:PF_CBB + 2] = pc(w["conv_b_b"])
    pf[:, :, PF_LBR:PF_LBR + 2] = pc(w["lru_b_r"])
    pf[:, :, PF_LBI:PF_LBI + 2] = pc(w["lru_b_i"])
    pf[:, :, PF_LAM:PF_LAM + 2] = pc(w["lru_lambda"])
    pf[:, :, PF_L1G:PF_L1G + 8] = pc(w["ln1_g"])
    pf[:, :, PF_L1B:PF_L1B + 8] = pc(w["ln1_b"])
    pf[:, :, PF_L2G:PF_L2G + 8] = pc(w["ln2_g"])
    pf[:, :, PF_L2B:PF_L2B + 8] = pc(w["ln2_b"])
    for i in range(3):
        pf[:, :, PF_FCW + i:PF_FCW + 66:3] = pc(w["ffn_conv_w"][:, i])
    pf[:, :, PF_FCB:PF_FCB + 22] = pc(w["ffn_conv_b"])
    rb = np.zeros((L, 128, NRB), np.float32)
    rb[:, :, 0:6] = w["a_log"][:, None, :]
    rb[:, :, 6:12] = w["dt_bias"][:, None, :]
    rb[:, :, 12:76] = w["norm_a_w"][:, None, :]
    rb[:, :, 76:140] = w["norm_c_w"][:, None, :]
    lw = np.zeros((L, 128, 4, 128), np.float32)
    for gi, nm in enumerate(["lru_w_r", "lru_w_i"]):
        for blk in range(2):
            for q in range(2):
                lw[:, 64 * q:64 * q + 64, 2 * gi + blk, 64 * q:64 * q + 64] = w[nm][:, 2 * blk + q]
    w2 = np.zeros((L, 32, 192), np.float32)
    w2[:, 0:16] = w["gla_w2"]
    w2[:, 16] = w["gla_b2"]
    return dict(wi=_pack_panels(wi, WI_PANELS, WI_TOT), wo=_pack_panels(np.ascontiguousarray(w["w_out"]), WO_PANELS, WO_TOT),
                wu=_pack_panels(wu, WU_PANELS, WU_TOT), wd=wd, pf=pf, rb=rb,
                lw=np.ascontiguousarray(lw.reshape(L, 128, 512)), w2=w2, cst=make_consts())


PW = 516
NPAGES = 32
NSLOT = 3
SLOTW = 2048


class TileCtx:
    def __init__(self, kind, ti=0, nprompt=4):
        self.kind = kind
        self.ti = ti
        self.samp = kind == "s"
        self.T = 128 if self.samp else 512
        self.NB = self.T // 128
        self.S = 16 if self.samp else 1
        self.Tt = 8 if self.samp else 512
        self.NS = 16 if self.samp else 1
        self.K = 2 if self.samp else 6
        self.mb = C_MS if self.samp else C_MP
        self.first = (not self.samp) and ti == 0
        self.last = (not self.samp) and ti == nprompt - 1
        self.state_out = self.samp or self.last


def build_program(cfg):
    nc = bass.Bass("TRN2", target_bir_lowering=False)
    B = Builder(nc)
    L = cfg.get("depth", 2)
    NPT = cfg.get("nprompt", 4)
    tiles = cfg.get("tiles", [("p", i) for i in range(NPT)] + [("s", 0)])
    dbg = cfg.get("dbg", {})

    def din(name, shape):
        return V(nc.dram_tensor(name, list(shape), F32, kind="ExternalInput").ap(), ())

    def dout(name, shape):
        return V(nc.dram_tensor(name, list(shape), F32, kind="ExternalOutput").ap(), ())

    xp = din("xp", [2048, D]); xs = din("xs", [128, D])
    sdc = din("sdc", [2, 16, 3, 1152]); sdl = din("sdl", [2, 16, 6, 64, 64]); slc = din("slc", [2, 16, 3, 256])
    slr = din("slr", [2, 16, 256]); sgl = din("sgl", [2, 16, 6, 32, 64]); sfc = din("sfc", [2, 16, 2, DFF])
    wi_p = din("wi", [2, 128, WI_TOT]); wo_p = din("wo", [2, 128, WO_TOT]); wu_p = din("wu", [2, 128, WU_TOT]); wd = din("wd", [2, 8, 128, DFF])

    class _Panels:
        def __init__(self, packed, offs):
            self.packed, self.offs, self.l = packed, offs, None

        def __getitem__(self, l):
            p = _Panels(self.packed, self.offs)
            p.l = l
            return p

        def cols(self, c0, n):
            o = self.offs[(c0, n)]
            return self.packed[self.l][:, o:o + 8 * n]
    wi = _Panels(wi_p, WI_OFF); wo = _Panels(wo_p, WO_OFF); wu = _Panels(wu_p, WU_OFF)
    pfd = din("pf", [2, 128, NPF]); rbd = din("rb", [2, 128, NRB]); lwd = din("lw", [2, 128, 512]); w2d = din("w2", [2, 32, 192])
    cstd = din("cst", [128, NCONST])
    yp = dout("yp", [2048, D]); ys = dout("ys", [128, D])
    o_pdc = dout("o_pdc", [2, 3, 1152]); o_pdl = dout("o_pdl", [2, 6, 64, 64]); o_plc = dout("o_plc", [2, 3, 256])
    o_plr = dout("o_plr", [2, 256]); o_pgl = dout("o_pgl", [2, 6, 32, 64]); o_pfc = dout("o_pfc", [2, 2, DFF])
    o_sdc = dout("o_sdc", [2, 16, 3, 1152]); o_sdl = dout("o_sdl", [2, 16, 6, 64, 64]); o_slc = dout("o_slc", [2, 16, 3, 256])
    o_slr = dout("o_slr", [2, 16, 256]); o_sgl = dout("o_sgl", [2, 16, 6, 32, 64]); o_sfc = dout("o_sfc", [2, 16, 2, DFF])
    dbg_out = {k: dout("dbg_" + k, shp) for k, shp in dbg.items()}

    B.init_psum()
    xT = B.sb("xT", [128, 8, 512])
    slots = [B.sb("slot%d" % i, [128, SLOTW], F32R) for i in range(NSLOT)]
    xin = [B.sb("xin%d" % i, [128, 1024]) for i in range(2)]
    stg = B.sb("stg", [128, 256])
    cst = B.sb("cst", [128, NCONST])
    cstR = B.sb("cstR", [128, 256], F32R)
    pf = [B.sb("pf%d" % l, [128, NPF]) for l in range(2)]
    rb = [B.sb("rb%d" % l, [128, NRB]) for l in range(2)]
    lw = [B.sb("lw%d" % l, [128, 512]) for l in range(2)]
    w2 = [B.sb("w2_%d" % l, [32, 192]) for l in range(2)]
    nea = [B.sb("nea%d" % l, [128, 6]) for l in range(2)]
    lc12 = [B.sb("lc12_%d" % l, [128, 4]) for l in range(2)]
    histA = [B.sb("histA%d" % l, [128, 9, 3]) for l in range(2)]
    histB = [B.sb("histB%d" % l, [128, 2, 3]) for l in range(2)]
    histF = [B.sb("histF%d" % l, [128, 22, 2]) for l in range(2)]
    SAp = [B.sb("SAp%d" % l, [128, 3, 64]) for l in range(2)]
    SCp = [B.sb("SCp%d" % l, [128, 2, 64]) for l in range(2)]
    hlp = [B.sb("hlp%d" % l, [128, 2]) for l in range(2)]
    small = B.sb("small", [128, 256])
    HG = 3
    SOLVE_R = cfg.get("solve_r", False)
    SDT = F32R if SOLVE_R else F32
    hbtR = B.sb("hbtR", [128, 5 * HG, 128], SDT)
    hbt2R = B.sb("hbt2R", [128, 2 * HG, 256], SDT)
    hbt = B.sb("hbt", [128, 4, 128])
    uwb = B.sb("uwb", [128, HG, 320])
    otm_x = B.sb("otm_x", [128, 3, 128])
    gat = B.sb("gat", [128, 12, 24])
    sab = B.sb("sab", [128, 16, 64])
    wxb = B.sb("wxb", [128, 16, 64])
    B.memset(hbt[:, 0, :].kk("hbs0"), 0.0)
    for sl_ in range(HG):
        B.cp(V(hbtR.ap[:, 5 * sl_ + 4, :], ["hbpad%d" % sl_]), hbt[:, 0, :].kk("hbs0"), eng="dve")
    arena_t = B.stack.enter_context(nc.sbuf_tensor("arena", [128, NPAGES, PW], F32))
    arenaR_t = B.stack.enter_context(nc.sbuf_tensor("arenaR", [128, 22, 512], F32R))

    def pg(p0, n=1):
        return V(arena_t[:, p0:p0 + n, :], ["ar%d" % p for p in range(p0, p0 + n)])

    def pgf(p0, n, width):
        assert width <= n * PW
        return V(arena_t[:, p0:p0 + n, :].rearrange("p a b -> p (a b)")[:, 0:width], ["ar%d" % p for p in range(p0, p0 + n)])

    def rpg(p0, n=1):
        return V(arenaR_t[:, p0:p0 + n, :], ["rp%d" % p for p in range(p0, p0 + n)])

    ident = cst[:, C_ID:C_ID + 128]
    ones = cst[:, C_ONE:C_ONE + 128]
    onesR = cstR[:, 0:128]
    b64R = cstR[:, 128:256]

    scnt = [0]

    def sm(n, tag=""):
        o = scnt[0] % 16
        scnt[0] += 1
        return V(small.ap[:, 16 * o:16 * o + n], ["sm%d" % o])

    hcnt = {"a": 0, "b": 0}

    def hbuf(tag=""):
        i = hcnt["a"] % 4
        hcnt["a"] += 1
        return V(hbt.ap[:, i, :], ["hbs%d" % i])

    def hbuf2(tag=""):
        raise RuntimeError("unused")

    B.dma(cst, cstd, group="const")
    for l in range(L):
        B.dma(pf[l], pfd[l], group="const")
        B.dma(rb[l], rbd[l], group="const")
        B.dma(lw[l], lwd[l], group="const")
        B.dma(w2[l], w2d[l], group="const")
    B.cp(cstR, cst[:, C_ONE:C_ONE + 256], eng="dve")
    for l in range(L):
        t = sm(6)
        B.act(t, rb[l][:, 0:6], AF.Exp)
        B.ts(nea[l], t, -1.0, ALU.mult)
        t2 = sm(2)
        B.act(t2, pf[l][:, PF_LAM:PF_LAM + 2], AF.Exp, scale=-1.0)
        t3 = sm(2)
        B.act(t3, t2, AF.Ln, bias=1.0)
        B.ts(lc12[l][:, 0:2], t3, -8.0, ALU.mult)
        B.ts(lc12[l][:, 2:4], t3, -16.0, ALU.mult)
        for tl in (SAp[l], SCp[l], hlp[l], histA[l], histB[l], histF[l]):
            B.memset(tl, 0.0)

    slot_i = [0]

    prefetched = {}

    def prefetch(key, src2d, ncols):
        if key not in prefetched:
            prefetched[key] = fill(src2d, ncols)

    def fill(src2d, ncols, nk=8, key=None):
        if key is not None and key in prefetched:
            return prefetched.pop(key)
        assert nk * ncols <= SLOTW
        s = slots[slot_i[0] % NSLOT]
        slot_i[0] += 1
        sv = s[:, 0:nk * ncols].re("p (c n) -> p c n", c=nk)
        B.dma(s[:, 0:nk * ncols], src2d, eng="pool", group="w:" + s.k[0])
        return sv

    evn = [0]

    def evac(out, in_, scale=None):
        evn[0] += 1
        if scale is not None:
            B.act(out, in_, AF.Copy, scale=scale)
        elif evn[0] % 2:
            B.cp(out, in_, eng="act")
        else:
            B.cp(out, in_, eng="dve")

    def dump(name, view):
        if name in dbg_out:
            B.dma(dbg_out[name], view, group="dbg_" + name)

    xTr = xT.r()

    def proj_fm(sv, j, n, T, rhsT=None):
        ps = B.nps()
        rr = xTr if rhsT is None else rhsT
        TN = max(T, 256)
        for c in range(8):
            B.mm(ps[0:n, 0:TN], sv[:, c, 128 * j:128 * j + n], rr[:, c, 0:TN], start=(c == 0), stop=(c == 7))
        return ps

    def proj_tm(sv, n, blk, c0=0):
        ps = B.nps()
        for c in range(8):
            B.mm(ps[:, 0:n], xTr[:, c, 128 * blk:128 * blk + 128], sv[:, c, c0:c0 + n], start=(c == 0), stop=(c == 7))
        return ps

    def load_x(tc):
        src = xs if tc.samp else xp
        r0 = 0 if tc.samp else tc.ti * 512
        for blk in range(tc.NB):
            xi = xin[blk % 2]
            B.dma(xi, src[r0 + 128 * blk:r0 + 128 * blk + 128, :])
            for half in range(2):
                ps = B.nps()
                for q in range(4):
                    c = 4 * half + q
                    B.tr(ps[:, 128 * q:128 * q + 128], xi[:, 128 * c:128 * c + 128], ident)
                evac(xTr[:, 4 * half:4 * half + 4, 128 * blk:128 * blk + 128], ps.re("p (a b) -> p a b", a=4))

    def store_y(tc):
        dst = ys if tc.samp else yp
        r0 = 0 if tc.samp else tc.ti * 512
        for blk in range(tc.NB):
            xi = xin[blk % 2]
            for half in range(2):
                ps = B.nps()
                for q in range(4):
                    c = 4 * half + q
                    B.tr(ps[:, 128 * q:128 * q + 128], xT[:, c, 128 * blk:128 * blk + 128], ident)
                evac(xi[:, 512 * half:512 * half + 512], ps)
            B.dma(dst[r0 + 128 * blk:r0 + 128 * blk + 128, :], xi)

    def conv_fm(out3, pre3, Tt, wcols, ntap, bias=None):
        if bias is not None:
            B.ts(out3, pre3[:, :, 0:Tt], wcols[0], ALU.mult, bias, ALU.add)
        else:
            B.ts(out3, pre3[:, :, 0:Tt], wcols[0], ALU.mult)
        for i in range(1, ntap):
            B.stt(out3, pre3[:, :, i:i + Tt], wcols[i], out3, ALU.mult, ALU.add)

    def hist_from_state(state2d, nrows, nch, dst_fn):
        R = 16 * nrows
        nb = nch // 128
        for b0 in range(0, nb, 8):
            nbb = min(8, nb - b0)
            xi = xin[(b0 // 8) % 2]
            B.dma(xi[0:R, 0:128 * nbb], state2d[:, 128 * b0:128 * (b0 + nbb)])
            for b in range(nbb):
                ps = B.nps()
                B.tr(ps[:, 0:R], xi[0:R, 128 * b:128 * b + 128], ident[0:R, 0:R])
                evac(dst_fn(b0 + b), ps[:, 0:R].re("p (s r) -> p s r", r=nrows))

    def state_rows_out(tc, ps_tm, ncols, nrows, dst_p, dst_s, col0):
        evac(stg[:, 0:ncols], ps_tm[:, 0:ncols])
        if tc.samp:
            for r in range(nrows):
                base = stg.ap[:, 0:ncols]
                pstep = base.ap[0][0]
                srcv = V(bass.AP(base.tensor, base.offset + (8 - nrows + r) * pstep, [[8 * pstep, 16], [1, ncols]]), stg.k)
                B.dma(dst_s[:, r, col0:col0 + ncols], srcv, group="so")
        else:
            B.dma(dst_p[:, col0:col0 + ncols], stg[128 - nrows:128, 0:ncols], group="so")

    def rms_gate_gen(tc, l, blk, o_tm, z_view, nw, hd0, p_sq=29, p_sz=30):
        o2 = o_tm.re("p h v -> p (h v)")
        sq = pgf(p_sq, 1, 384)
        B.act(sq, o2, AF.Square)
        sz = pgf(p_sz, 1, 384)
        B.act(sz, z_view, AF.Silu)
        yield
        ss = sm(6)
        B.red(ss, sq.re("p (h v) -> p h v", h=6))
        B.ts(ss, ss, 1.0 / 64, ALU.mult, EPS, ALU.add)
        yield
        B.act(ss, ss, AF.Sqrt)
        yield
        rs = sm(6)
        B.recip(rs, ss)
        B.tt(o_tm, o_tm, rs.bc(2, [128, 6, 64]), ALU.mult)
        yield
        B.tt(o_tm, o_tm, nw.bc(1, [128, 6, 64]), ALU.mult)
        yield
        B.tt(o2, o2, sz, ALU.mult)
        ps = B.nps()
        for j in range(3):
            B.tr(ps[:, 128 * j:128 * j + 128], o2[:, 128 * j:128 * j + 128], ident)
        yield
        evac(rpg(hd0, 3)[:, :, 128 * blk:128 * blk + 128], ps[:, 0:384].re("p (a b) -> p a b", a=3))

    def rms_gate_heads(tc, l, blk, o_tm, z_view, nw, hd0, p_sq=29, p_sz=30):
        for _ in rms_gate_gen(tc, l, blk, o_tm, z_view, nw, hd0, p_sq, p_sz):
            pass

    def phase_A(tc, l):
        T, NB, S, Tt, NS, mb = tc.T, tc.NB, tc.S, tc.Tt, tc.NS, tc.mb
        TRI = cst[:, mb + M_TRI:mb + M_TRI + 128]
        SEGM = cst[:, mb + M_SEGM:mb + M_SEGM + 128]
        NSTRIT = cst[:, mb + M_NSTRIT:mb + M_NSTRIT + 128]
        NTRI = cst[:, mb + M_NTRI:mb + M_NTRI + 128]
        SEG = cst[:, mb + M_SEG:mb + M_SEG + 16]
        LAST = cst[:, mb + M_LAST:mb + M_LAST + 16]
        W = 3 + Tt

        def pre(b):
            return pg(b % 3)[:, 0, 0:S * W].re("p (s w) -> p s w", s=S)

        def qk(b):
            return pg(4 + b)[:, 0, 0:T]

        hsA = pgf(3, 1, 9 * 48).re("p (b s r) -> p b s r", b=9, s=16)
        if tc.samp:
            hist_from_state(sdc[l].re("s r n -> (s r) n"), 3, 1152, lambda b: hsA[:, b, :, :])
        for si in range(5):
            c0 = 256 * si
            ncq = 256 if si < 4 else 128
            sv = fill(wi[l].cols(c0, ncq), ncq, key=("A", l, si))
            for j in range(ncq // 128):
                b = 2 * si + j
                if tc.samp:
                    B.cp(pre(b)[:, :, 0:3], hsA[:, b, :, :], eng="dve")
                else:
                    B.cp(pre(b)[:, 0, 0:3], histA[l][:, b, :], eng="dve")
                ps = proj_fm(sv, j, 128, T)
                evac(pre(b)[:, :, 3:3 + Tt], ps[:, 0:T].re("p (s t) -> p s t", s=S))
                if not tc.samp:
                    B.cp(histA[l][:, b, :], pre(b)[:, 0, Tt:Tt + 3], eng="dve")
                o3 = qk(b).re("p (s t) -> p s t", s=S)
                conv_fm(o3, pre(b), Tt, [pf[l][:, PF_CA + 4 * b + i:PF_CA + 4 * b + i + 1] for i in range(4)], 4)
                B.act(qk(b), qk(b), AF.Silu)
            if tc.state_out:
                pt = proj_tm(sv, ncq, tc.NB - 1)
                state_rows_out(tc, pt, ncq, 3, o_pdc[l], o_sdc[l], c0)
        tmA = pgf(13, 4, NB * 396).re("p (b n) -> p b n", b=NB)
        for (zc0, zn) in ((0, 256), (256, 128)):
            sv = fill(wi[l].cols(2304 + zc0, zn), zn)
            for blk in range(NB):
                pt = proj_tm(sv, zn, blk)
                evac(tmA[:, blk, zc0:zc0 + zn], pt[:, 0:zn])
        sv = fill(wi[l].cols(2688, 12), 12)
        for blk in range(NB):
            pt = proj_tm(sv, 12, blk)
            evac(tmA[:, blk, 384:396], pt[:, 0:12])
        dump("qkv_silu_%d" % l, pg(4, 9)[:, :, 0:T])
        if cfg.get("a_stop", 99) <= 1:
            return
        sqs = [rpg(8 + b)[:, 0, 0:T] for b in range(6)]
        rns = [pg(19 + b)[:, 0, 0:T] for b in range(6)]
        pss = []
        for b in range(6):
            B.act(sqs[b], qk(b), AF.Square)
        for b in range(6):
            ps = B.nps()
            pss.append(ps)
            B.mm(ps[:, 0:T], b64R, sqs[b])
        for b in range(6):
            B.act(rns[b], pss[b][:, 0:T], AF.Sqrt, bias=EPS)
        for b in range(6):
            B.recip(rns[b], rns[b])
            if b < 3:
                B.stt(qk(b), rns[b], 0.125, qk(b), ALU.mult, ALU.mult)
            else:
                B.tt(qk(b), qk(b), rns[b], ALU.mult)
        dump("qkn_%d" % l, pg(4, 6)[:, :, 0:T])
        if cfg.get("a_stop", 99) <= 2:
            return

        KL = tc.K
        pending_rms = []
        NG = 6 * NB
        gatv = [V(gat.ap[:, i, 0:NG], ["gat%d" % i]) for i in range(12)]

        def g3(v):
            return v.re("p (b h) -> p b h", h=6)
        B.act(g3(gatv[0]), tmA[:, :, 384:390], AF.Sigmoid)
        B.ts(gatv[1], gatv[0], -1.0, ALU.mult)
        B.tt(g3(gatv[8]), tmA[:, :, 390:396], rb[l][:, 6:12].bc(1, [128, NB, 6]), ALU.add)
        B.act(gatv[8], gatv[8], AF.Exp)
        B.act(gatv[8], gatv[8], AF.Ln, bias=1.0)
        B.tt(g3(gatv[2]), g3(gatv[8]), nea[l].bc(1, [128, NB, 6]), ALU.mult)
        psg = B.nps()
        B.mm(psg[:, 0:NG], TRI, gatv[2])
        B.mm(psg[:, 32:32 + NG], SEGM, gatv[2])
        B.cp(gatv[3], psg[:, 0:NG], eng="dve")
        B.act(gatv[4], psg[:, 0:NG], AF.Exp)
        B.act(gatv[5], psg[:, 32:32 + NG], AF.Exp)
        B.tt(gatv[6], psg[:, 32:32 + NG], gatv[3], ALU.subtract)
        B.act(gatv[6], gatv[6], AF.Exp)
        B.tt(gatv[7], gatv[0], gatv[4], ALU.mult)
        B.ts(gatv[11], gatv[3], -1.0, ALU.mult)
        if NS == 1:
            B.ts(gatv[9], gatv[5], LAST[:, 0:1], ALU.mult)
            psl = B.nps()
            B.mm(psl[:, 0:NG], ones, gatv[9])
            B.cp(gatv[10], psl[:, 0:NG], eng="act")
        for blk in range(NB):
            tc0 = 128 * blk
            za = tmA[:, blk, 0:384]
            ba = tmA[:, blk, 384:390]
            aa = tmA[:, blk, 390:396]
            ktm = pgf(17, 1, 384)
            vtm = pgf(18, 1, 384)
            for (dst, b0) in ((ktm, 3), (vtm, 6)):
                ps = B.nps()
                for j in range(3):
                    B.tr(ps[:, 128 * j:128 * j + 128], qk(b0 + j)[:, tc0:tc0 + 128], ident)
                evac(dst, ps[:, 0:384])
            c6 = slice(6 * blk, 6 * blk + 6)
            beta = gatv[0][:, c6]; nbeta = gatv[1][:, c6]; g = gatv[2][:, c6]; gc = gatv[3][:, c6]; egc = gatv[4][:, c6]
            egl = gatv[5][:, c6]; kdf = gatv[6][:, c6]; bexp = gatv[7][:, c6]
            if blk == 0:
                dump("g_%d" % l, g)
                dump("beta_%d" % l, beta)
            DG = pgf(19, 2, 768).re("p (h f) -> p h f", h=6)
            Dm = pgf(21, 2, 768).re("p (h f) -> p h f", h=6)
            DmT = pgf(23, 2, 768).re("p (h f) -> p h f", h=6)
            PE_ = cfg.get("pool_pre", 1)
            B.tt(DG, ident.bc(1, [128, 6, 128]), gc.bc(2, [128, 6, 128]), ALU.mult, eng=("pool" if PE_ else "dve"))
            ngc = gatv[11][:, c6]
            for hf in range(2):
                psr = B.nps()
                B.mm(psr[:, 0:384], ones, DG[:, 3 * hf:3 * hf + 3, :].re("p h f -> p (h f)"))
                R3 = psr[:, 0:384].re("p (h f) -> p h f", h=3)
                d1 = Dm[:, 3 * hf:3 * hf + 3, :]
                d2 = DmT[:, 3 * hf:3 * hf + 3, :]
                B.tt(d1, R3, NSTRIT.bc(1, [128, 3, 128]), ALU.subtract)
                B.tt(d2, R3, NTRI.bc(1, [128, 3, 128]), ALU.add)
                for hh in range(3):
                    h = 3 * hf + hh
                    B.act(Dm[:, h, :], Dm[:, h, :], AF.Exp, bias=gc[:, h:h + 1], scale=-1.0)
                    B.act(DmT[:, h, :], DmT[:, h, :], AF.Exp, bias=ngc[:, h:h + 1])
            bv = pgf(25, 1, 384).re("p (h v) -> p h v", h=6)
            kb = pgf(26, 1, 384).re("p (h v) -> p h v", h=6)
            kdec = pgf(27, 1, 384).re("p (h v) -> p h v", h=6)
            otm = pgf(28 if blk % 2 == 0 else 3, 1, 384).re("p (h v) -> p h v", h=6)
            v3 = vtm.re("p (h v) -> p h v", h=6)
            k3 = ktm.re("p (h v) -> p h v", h=6)
            pe_ = "pool" if PE_ else "dve"
            B.tt(bv, v3, beta.bc(2, [128, 6, 64]), ALU.mult, eng=pe_)
            B.tt(kb, k3, bexp.bc(2, [128, 6, 64]), ALU.mult, eng=pe_)
            B.tt(kdec, k3, kdf.bc(2, [128, 6, 64]), ALU.mult, eng=pe_)
            if NS == 1:
                glb = gatv[10][:, c6]
            else:
                SEL = pgf(31, 1, 96)
                glb = pgf(31, 1, 192)[:, 96:192].re("p (h s) -> p h s", h=6)
                B.tt(SEL.re("p (h s) -> p h s", h=6), egl.bc(2, [128, 6, 16]), LAST.bc(1, [128, 6, 16]), ALU.mult)
                psl = B.nps()
                B.mm(psl[:, 0:96], ones, SEL)
                B.cp(glb, psl[:, 0:96].re("p (h s) -> p h s", h=6), eng="act")
            SAs = sab

            def head_gen(h, slot):
                hp, po = h // 2, (h % 2) * 64
                kT = qk(3 + hp)[po:po + 64, tc0:tc0 + 128]
                qT = qk(hp)[po:po + 64, tc0:tc0 + 128]
                hb = [V(hbtR.ap[:, 5 * slot + i, :], ["hb%d_%d" % (slot, i)]) for i in range(4)]
                hw = [V(hbt2R.ap[:, 2 * slot + i, :], ["hc%d_%d" % (slot, i)]) for i in range(2)]
                Nm, NmT, Pa, Pb = hb
                Wa, Wb = hw
                NN = V(hbtR.ap[:, 5 * slot:5 * slot + 2, :].rearrange("p a b -> p (a b)"), Nm.k + NmT.k)
                PPa = V(hbtR.ap[:, 5 * slot + 2:5 * slot + 4, :].rearrange("p a b -> p (a b)"), Pa.k + Pb.k)
                PPb = V(hbtR.ap[:, 5 * slot + 3:5 * slot + 5, :].rearrange("p a b -> p (a b)"), Pb.k)
                ps = B.nps()
                B.mm(ps[:, 0:128], kT, kT)
                B.mm(ps[:, 128:256], kT, qT)
                yield
                B.stt(Nm, ps[:, 0:128], nbeta[:, h:h + 1], Dm[:, h, :], ALU.mult, ALU.mult)
                qkmT = V(otm_x.ap[:, slot, :], ["qkm%d" % slot])
                B.tt(qkmT, ps[:, 128:256], DmT[:, h, :], ALU.mult)
                ps = B.nps()
                B.tr(ps[:, 0:128], Nm.f(), ident)
                yield
                B.cp(NmT, ps[:, 0:128], eng="dve")
                B.tt(Wa[:, 128:256], ps[:, 0:128], ident, ALU.add)
                ps = B.nps()
                ps2 = B.nps()
                if SOLVE_R:
                    B.mm(ps[:, 0:256], NmT, NN)
                    B.mm(ps2[:, 0:256], Nm, NN)
                else:
                    B.mm(ps[:, 0:128], NmT, Nm)
                    B.mm(ps2[:, 128:256], Nm, NmT)
                yield
                B.cp(Pa, ps[:, 0:128], eng="act")
                B.cp(Wa[:, 0:128], ps2[:, 128:256], eng="dve")
                Pc, Pn, Wc, Wn, PPc, PPn = Pa, Pb, Wa, Wb, PPa, PPb
                for k in range(1, KL + 1):
                    lastk = k == KL
                    ps = B.nps()
                    if lastk and not SOLVE_R:
                        B.mm(ps[:, 128:256], Pc, Wc[:, 128:256])
                    else:
                        B.mm(ps[:, 0:256], Pc, Wc)
                    if not lastk:
                        ps2 = B.nps()
                        if SOLVE_R:
                            B.mm(ps2[:, 0:256], Wc[:, 0:128], PPc)
                        else:
                            B.mm(ps2[:, 0:128], Wc[:, 0:128], Pc)
                    yield
                    B.tt(Wn[:, 128:256], Wc[:, 128:256].f(), ps[:, 128:256], ALU.add)
                    if not lastk:
                        B.cp(Wn[:, 0:128], ps[:, 0:128], eng="dve")
                        B.cp(Pn, ps2[:, 0:128], eng="act")
                    Pc, Pn, Wc, Wn, PPc, PPn = Pn, Pc, Wn, Wc, PPn, PPc
                AT = Wc[:, 128:256].f()
                u_sb = V(uwb.ap[:, slot, 0:64], ["uw%d_0" % slot])
                w_sb = V(uwb.ap[:, slot, 64:128], ["uw%d_1" % slot])
                qSe = V(uwb.ap[:, slot, 128:192], ["uw%d_2" % slot])
                wkT = V(uwb.ap[:, slot, 192:320], ["uw%d_3" % slot])
                ps = B.nps()
                B.mm(ps[:, 0:64], AT, bv[:, h, :])
                B.mm(ps[po:po + 64, 128:256], kb[:, h, :], AT)
                yield
                B.cp(u_sb, ps[:, 0:64], eng="dve")
                B.cp(wkT[po:po + 64, :], ps[po:po + 64, 128:256], eng="dve")
                if NS == 1:
                    Sh = SAp[l][po:po + 64, hp, :]
                    ps = B.nps()
                    B.mm(ps[:, 0:64], wkT[po:po + 64, :], Sh)
                    B.mm(ps[:, 64:128], qT, Sh)
                    yield
                    B.tt(w_sb, u_sb, ps[:, 0:64], ALU.subtract)
                    B.ts(qSe, ps[:, 64:128], egc[:, h:h + 1], ALU.mult)
                else:
                    if h % 2 == 0:
                        for q in range(2):
                            B.dma(SAs[64 * q:64 * q + 64, :, :], sdl[l][:, h + q, :, :].re("s d v -> d s v"), group="sa")
                    ps = B.nps()
                    for s in range(16):
                        B.mm(ps[0:64, 8 * s:8 * s + 8], SAs[po:po + 64, s, :], wkT[po:po + 64, 8 * s:8 * s + 8])
                        B.mm(ps[0:64, 128 + 8 * s:128 + 8 * s + 8], SAs[po:po + 64, s, :], qT[:, 8 * s:8 * s + 8])
                    yield
                    cTa = hbuf()
                    cTb = hbuf()
                    B.cp(cTa[0:64, :], ps[0:64, 0:128], eng="act")
                    B.cp(cTb[0:64, :], ps[0:64, 128:256], eng="act")
                    ps = B.nps()
                    B.tr(ps[:, 0:64], cTa[0:64, :], ident[0:64, 0:64])
                    B.tr(ps[:, 64:128], cTb[0:64, :], ident[0:64, 0:64])
                    yield
                    B.tt(w_sb, u_sb, ps[:, 0:64], ALU.subtract)
                    B.act(qSe, ps[:, 64:128], AF.Copy, scale=egc[:, h:h + 1])
                ps = B.nps()
                B.mm(ps[:, 0:64], qkmT, w_sb)
                if NS == 1:
                    B.mm(ps[po:po + 64, 128:192], kdec[:, h, :], w_sb)
                    yield
                    B.tt(otm[:, h, :], qSe, ps[:, 0:64], ALU.add)
                    B.stt(Sh, Sh, glb[po:po + 64, h:h + 1], ps[po:po + 64, 128:192], ALU.mult, ALU.add)
                else:
                    yield
                    B.tt(otm[:, h, :], qSe, ps[:, 0:64], ALU.add)
                    Wexp = wxb
                    B.tt(Wexp, w_sb.bc(1, [128, 16, 64]), SEG.bc(2, [128, 16, 64]), ALU.mult)
                    for hf in range(2):
                        psU = B.nps()
                        B.mm(psU[po:po + 64, 0:512], kdec[:, h, :], Wexp[:, 8 * hf:8 * hf + 8, :].re("p s v -> p (s v)"))
                        Sv = SAs[po:po + 64, 8 * hf:8 * hf + 8, :]
                        B.tt(Sv, Sv, glb[po:po + 64, h, 8 * hf:8 * hf + 8].bc(2, [64, 8, 64]), ALU.mult)
                        B.tt(Sv, Sv, psU[po:po + 64, 0:512].re("p (s v) -> p s v", s=8), ALU.add)
                    if h % 2 == 1:
                        for q in range(2):
                            B.dma(o_sdl[l][:, h - 1 + q, :, :].re("s d v -> d s v"), SAs[64 * q:64 * q + 64, :, :], group="sa")

            G = cfg.get('g_prompt', 3) if NS == 1 else 2
            for h0 in range(0, 6, G):
                gens = [head_gen(h0 + i, i) for i in range(G) if h0 + i < 6]
                if h0 == 0 and pending_rms:
                    gens.append(pending_rms.pop())
                while gens:
                    for gen in list(gens):
                        try:
                            next(gen)
                        except StopIteration:
                            gens.remove(gen)
            if blk == 0:
                dump("oa_raw_%d" % l, otm.re("p h v -> p (h v)"))
            pending_rms.append(rms_gate_gen(tc, l, blk, otm, za, rb[l][:, 12:76], 0))
        for gen in pending_rms:
            for _ in gen:
                pass
        if tc.last:
            for h in range(6):
                hp, po = h // 2, (h % 2) * 64
                B.dma(o_pdl[l][h], SAp[l][po:po + 64, hp, :], group="pdl")

    def phase_B(tc, l):
        T, S, Tt = tc.T, tc.S, tc.Tt
        W = 3 + Tt

        def pre(cb):
            return pg(cb)[:, 0, 0:S * W].re("p (s w) -> p s w", s=S)

        if tc.samp:
            hist_from_state(slc[l].re("s r n -> (s r) n"), 3, 256, lambda b: pre(b)[:, :, 0:3])
            xi = xin[0]
            B.dma(xi[0:16, 0:256], slr[l])
            h0 = pgf(16, 1, 32).re("p (c s) -> p c s", c=2)
            for cb in range(2):
                ps = B.nps()
                B.tr(ps[:, 0:16], xi[0:16, 128 * cb:128 * cb + 128], ident[0:16, 0:16])
                evac(h0[:, cb, :], ps[:, 0:16])
            hl_s = pgf(17, 1, 32).re("p (c s) -> p c s", c=2)
        else:
            B.cp(pg(0, 2)[:, :, 0:3], histB[l], eng="dve")
        sv = fill(wi[l].cols(1152, 256), 256)
        if tc.state_out:
            pt = proj_tm(sv, 256, tc.NB - 1)
            state_rows_out(tc, pt, 256, 3, o_plc[l], o_slc[l], 0)
        for cb in range(2):
            ps = proj_fm(sv, cb, 128, T)
            evac(pre(cb)[:, :, 3:3 + Tt], ps[:, 0:T].re("p (s t) -> p s t", s=S))
        if not tc.samp:
            B.cp(histB[l], pg(0, 2)[:, :, 512:515], eng="dve")
        svg = fill(wi[l].cols(1408, 256), 256)
        for cb in range(2):
            xc = pg(2 + cb)[:, 0, 0:T]
            conv_fm(xc.re("p (s t) -> p s t", s=S), pre(cb), Tt,
                    [pf[l][:, PF_CB + 4 * cb + i:PF_CB + 4 * cb + i + 1] for i in range(4)], 4,
                    bias=pf[l][:, PF_CBB + cb:PF_CBB + cb + 1])
            rs = pg(4 + cb)[:, 0, 0:T]
            is_ = pg(6 + cb)[:, 0, 0:T]
            a = pg(8 + cb)[:, 0, 0:T]
            sq = pg(10 + cb)[:, 0, 0:T]
            hh = pg(12 + cb)[:, 0, 0:T]
            psr = B.nps()
            B.mm(psr[:, 0:T], lw[l][:, 128 * cb:128 * cb + 128], xc)
            B.act(rs, psr[:, 0:T], AF.Sigmoid, bias=pf[l][:, PF_LBR + cb:PF_LBR + cb + 1])
            psi = B.nps()
            B.mm(psi[:, 0:T], lw[l][:, 256 + 128 * cb:256 + 128 * cb + 128], xc)
            B.act(is_, psi[:, 0:T], AF.Sigmoid, bias=pf[l][:, PF_LBI + cb:PF_LBI + cb + 1])
            B.act(a, rs, AF.Exp, scale=lc12[l][:, cb:cb + 1])
            B.act(sq, rs, AF.Exp, scale=lc12[l][:, 2 + cb:3 + cb])
            B.act(sq, sq, AF.Sqrt, bias=1.0, scale=-1.0)
            B.tt(is_, is_, xc, ALU.mult)
            B.tt(is_, is_, sq, ALU.mult)
            if tc.samp:
                for s in range(16):
                    B.scan(hh[:, 8 * s:8 * s + 8], a[:, 8 * s:8 * s + 8], is_[:, 8 * s:8 * s + 8], h0[:, cb, s:s + 1])
                B.cp(hl_s[:, cb, :], hh.re("p (s t) -> p s t", t=8)[:, :, 7], eng="dve")
            else:
                B.scan(hh, a, is_, hlp[l][:, cb:cb + 1])
                B.cp(hlp[l][:, cb:cb + 1], hh[:, T - 1:T], eng="dve")
            if cb == 0:
                dump("h_lru_%d" % l, hh)
            psg = proj_fm(svg, cb, 128, T)
            gg = pg(14 + cb)[:, 0, 0:T]
            B.act(gg, psg[:, 0:T], AF.Gelu_apprx_tanh)
            B.tt(rpg(3 + cb)[:, 0, 0:T], hh, gg, ALU.mult)
        if tc.samp:
            for cb in range(2):
                ps = B.nps()
                B.tr(ps[0:16, 0:128], hl_s[:, cb, :], ident)
                evac(stg[0:16, 128 * cb:128 * cb + 128], ps[0:16, 0:128])
            B.dma(o_slr[l], stg[0:16, 0:256], group="so")
        elif tc.last:
            B.dma(o_plr[l].re("(c p) -> p c", p=128), hlp[l], group="plr", nc_ok=True)

    def phase_C(tc, l):
        T, NB, S, Tt, NS, mb = tc.T, tc.NB, tc.S, tc.Tt, tc.NS, tc.mb
        TRI = cst[:, mb + M_TRI:mb + M_TRI + 128]
        SEGM = cst[:, mb + M_SEGM:mb + M_SEGM + 128]
        SEG = cst[:, mb + M_SEG:mb + M_SEG + 16]
        qcT = [pg(0 + g)[:, 0, 0:T] for g in range(2)]
        kcT = [pg(2 + g)[:, 0, 0:T] for g in range(2)]
        lcT = pg(4)[0:32, 0, 0:T]
        vct = pgf(5, 3, NB * 384).re("p (b n) -> p b n", b=NB)
        zct = pgf(8, 3, NB * 384).re("p (b n) -> p b n", b=NB)
        kct = pgf(11, 2, NB * 192).re("p (b n) -> p b n", b=NB)
        sv = fill(wi[l].cols(1664, 256), 256)
        for g in range(2):
            ps = proj_fm(sv, g, 96, T)
            evac(qcT[g][0:96, :], ps[0:96, 0:T], scale=32.0 ** -0.5)
        sv = fill(wi[l].cols(1920, 256), 256)
        for g in range(2):
            ps = proj_fm(sv, g, 96, T)
            evac(kcT[g][0:96, :], ps[0:96, 0:T])
        sv = fill(wi[l].cols(2176, 16), 16)
        B.memset(lcT, 1.0)
        ps = proj_fm(sv, 0, 16, T)
        evac(lcT[0:16, :], ps[0:16, 0:T])
        for (c0, dsts) in ((2816, [(vct, 0, 0, 256)]), (4352, [(vct, 256, 0, 128), (zct, 256, 128, 128)]),
                           (3328, [(zct, 0, 0, 256)]), (3840, [(kct, 0, 0, 192)])):
            sv = fill(wi[l].cols(c0, 256), 256)
            for blk in range(NB):
                pt = proj_tm(sv, 256, blk)
                for (dst, d0, p0, n) in dsts:
                    evac(dst[:, blk, d0:d0 + n], pt[:, p0:p0 + n])
        SCs = sab
        pre_c = {}
        pending_rms_c = []

        def pre_gen_c(blk):
            tc0 = 128 * blk
            pA = pgf(20 + 3 * blk, 1, 384)
            pB = pgf(21 + 3 * blk, 1, 224)
            pC = pgf(22 + 3 * blk, 1, 512)
            logf = pA[:, 0:192]
            b_sb = pA[:, 192:384]
            kdc = pB[:, 0:192]
            ebl = pB[:, 192:224].re("p (g s) -> p g s", g=2)
            qt = pC[:, 0:256].re("p (g t) -> p g t", g=2)
            kt = pC[:, 256:512].re("p (g t) -> p g t", g=2)
            psl = B.nps()
            B.mm(psl[:, 0:192], lcT[:, tc0:tc0 + 128], w2[l])
            yield
            B.act(logf, psl[:, 0:192], AF.Exp, scale=-1.0)
            B.act(logf, logf, AF.Ln, bias=1.0)
            B.ts(logf, logf, -1.0 / 16, ALU.mult)
            if blk == 0:
                dump("logf_%d" % l, logf)
            psb = B.nps()
            B.mm(psb[:, 0:192], TRI, logf)
            B.mm(psb[:, 256:448], SEGM, logf)
            psT = B.nps()
            for g in range(2):
                B.mm(psT[0:96, 128 * g:128 * g + 128], logf[:, 96 * g:96 * g + 96], TRI)
                B.mm(psT[0:96, 256 + 16 * g:256 + 16 * g + 16], logf[:, 96 * g:96 * g + 96], SEG)
            yield
            B.cp(b_sb, psb[:, 0:192], eng="dve")
            B.tt(kdc, psb[:, 256:448], b_sb, ALU.subtract)
            B.act(kdc, kdc, AF.Exp)
            B.tt(kdc, kdc, kct[:, blk, :], ALU.mult)
            B.act(qt[0:96], psT[0:96, 0:256].re("p (g t) -> p g t", g=2), AF.Exp)
            B.act(kt[0:96], psT[0:96, 0:256].re("p (g t) -> p g t", g=2), AF.Exp, scale=-1.0)
            B.act(ebl[0:96], psT[0:96, 256:288].re("p (g s) -> p g s", g=2), AF.Exp)
            for g in range(2):
                B.tt(qt[0:96, g, :], qt[0:96, g, :], qcT[g][0:96, tc0:tc0 + 128], ALU.mult)
                B.tt(kt[0:96, g, :], kt[0:96, g, :], kcT[g][0:96, tc0:tc0 + 128], ALU.mult)
            pre_c[blk] = (kdc, ebl, qt, kt)

        for b0 in range(0, NB, 2):
            gens = [pre_gen_c(b0 + i) for i in range(2) if b0 + i < NB]
            while gens:
                for gen in list(gens):
                    try:
                        next(gen)
                    except StopIteration:
                        gens.remove(gen)
        for blk in range(NB):
            tc0 = 128 * blk
            kdc, ebl, qt, kt = pre_c[blk]
            otm = pgf(19 if blk % 2 == 0 else 15, 1, 384).re("p (h v) -> p h v", h=6)

            def head_gen_c(h):
                g, po = h // 3, (h % 3) * 32
                vh = vct[:, blk, 64 * h:64 * h + 64]
                ps = B.nps()
                B.mm(ps[:, 0:128], kt[po:po + 32, g, :], qt[po:po + 32, g, :])
                yield
                attm = hbuf()
                B.tt(attm, ps[:, 0:128], TRI, ALU.mult)
                if NS == 1:
                    Sh = SCp[l][po:po + 32, g, :]
                    ps = B.nps()
                    B.mm(ps[:, 0:64], qt[po:po + 32, g, :], Sh, start=True, stop=False)
                    B.mm(ps[:, 0:64], attm, vh, start=False, stop=True)
                    B.mm(ps[po:po + 32, 128:192], kdc[:, 32 * h:32 * h + 32], vh)
                    yield
                    B.cp(otm[:, h, :], ps[:, 0:64], eng="dve")
                    B.stt(Sh, Sh, ebl[po:po + 32, g, 0:1], ps[po:po + 32, 128:192], ALU.mult, ALU.add)
                else:
                    if h % 3 == 0:
                        for q in range(3):
                            B.dma(SCs[32 * q:32 * q + 32, :, :], sgl[l][:, h + q, :, :].re("s k v -> k s v"), group="sa")
                    ps = B.nps()
                    for s in range(16):
                        B.mm(ps[0:64, 8 * s:8 * s + 8], SCs[po:po + 32, s, :], qt[po:po + 32, g, 8 * s:8 * s + 8])
                    yield
                    cT = hbuf()
                    B.cp(cT[0:64, :], ps[0:64, 0:128], eng="act")
                    ps = B.nps()
                    B.tr(ps[:, 0:64], cT[0:64, :], ident[0:64, 0:64])
                    B.mm(ps[:, 64:128], attm, vh)
                    yield
                    qS = hbuf()[:, 0:64]
                    B.cp(qS, ps[:, 0:64], eng="act")
                    B.tt(otm[:, h, :], qS, ps[:, 64:128], ALU.add)
                    Vexp = wxb
                    B.tt(Vexp, vh.bc(1, [128, 16, 64]), SEG.bc(2, [128, 16, 64]), ALU.mult)
                    for hf in range(2):
                        psU = B.nps()
                        B.mm(psU[po:po + 32, 0:512], kdc[:, 32 * h:32 * h + 32], Vexp[:, 8 * hf:8 * hf + 8, :].re("p s v -> p (s v)"))
                        Sv = SCs[po:po + 32, 8 * hf:8 * hf + 8, :]
                        B.tt(Sv, Sv, ebl[po:po + 32, g, 8 * hf:8 * hf + 8].bc(2, [32, 8, 64]), ALU.mult)
                        B.tt(Sv, Sv, psU[po:po + 32, 0:512].re("p (s v) -> p s v", s=8), ALU.add)
                    if h % 3 == 2:
                        for q in range(3):
                            B.dma(o_sgl[l][:, h - 2 + q, :, :].re("s k v -> k s v"), SCs[32 * q:32 * q + 32, :, :], group="sa")

            GC = cfg.get("g_c", 3)
            for h0 in range(0, 6, GC):
                gens = [head_gen_c(h0 + i) for i in range(GC) if h0 + i < 6]
                if h0 == 0 and pending_rms_c:
                    gens.append(pending_rms_c.pop())
                while gens:
                    for gen in list(gens):
                        try:
                            next(gen)
                        except StopIteration:
                            gens.remove(gen)
            if blk == 0:
                dump("oc_raw_%d" % l, otm.re("p h v -> p (h v)"))
            pending_rms_c.append(rms_gate_gen(tc, l, blk, otm, zct[:, blk, :], rb[l][:, 76:140], 5, p_sq=13, p_sz=14))
        for gen in pending_rms_c:
            for _ in gen:
                pass
        if tc.last:
            for h in range(6):
                g, po = h // 3, (h % 3) * 32
                B.dma(o_pgl[l][h], SCp[l][po:po + 32, g, :], group="pgl")

    def layer_norm(tc, l, gcol, bcol):
        T = tc.T
        psS = B.nps()
        psQ = B.nps()
        for m in range(8):
            y = pg(m)[:, 0, 0:T]
            ysq = rpg(8 + m % 2)[:, 0, 0:T]
            yr = rpg(10 + m % 2)[:, 0, 0:T]
            B.act(ysq, y, AF.Square)
            B.cp(yr, y, eng="act")
            B.mm(psS[:, 0:T], onesR, yr, start=(m == 0), stop=(m == 7))
            B.mm(psQ[:, 0:T], onesR, ysq, start=(m == 0), stop=(m == 7))
        mean = pg(13)[:, 0, 0:T]
        rstd = pg(14)[:, 0, 0:T]
        msq = pg(15)[:, 0, 0:T]
        B.act(msq, psS[:, 0:T], AF.Square, scale=1.0 / D)
        B.stt(rstd, psQ[:, 0:T], 1.0 / D, msq, ALU.mult, ALU.subtract)
        B.act(rstd, rstd, AF.Sqrt, bias=EPS)
        B.recip(rstd, rstd)
        for m in range(8):
            y = pg(m)[:, 0, 0:T]
            le = "pool" if (m % 2 == 1 and cfg.get("ln_pool", 0)) else "dve"
            B.stt(y, psS[:, 0:T], -1.0 / D, y, ALU.mult, ALU.add)
            B.tt(y, y, rstd, ALU.mult, eng=le)
            B.act(xTr[:, m, 0:T], y, AF.Identity, bias=pf[l][:, bcol + m:bcol + m + 1], scale=pf[l][:, gcol + m:gcol + m + 1])

    def wout_ln1(tc, l):
        T = tc.T
        hd = rpg(0, 8)
        m = 0
        for (c0, ncol) in ((0, 256), (256, 256), (512, 256), (768, 256)):
            sv = fill(wo[l].cols(c0, ncol), ncol)
            for j in range(ncol // 128):
                ps = proj_fm(sv, j, 128, T, rhsT=hd)
                B.stt(pg(m)[:, 0, 0:T], xT[:, m, 0:T], float(ALPHA), ps[:, 0:T], ALU.mult, ALU.add)
                m += 1
        for gb in range(2):
            prefetch(("U", l, gb), wu[l].cols(256 * gb, 256), 256)
        layer_norm(tc, l, PF_L1G, PF_L1B)

    def ffn_ln2(tc, l):
        T, S, Tt = tc.T, tc.S, tc.Tt
        W = 2 + Tt
        hT = rpg(0, 22)
        if tc.samp:
            hs = pgf(16, 2, 22 * 32).re("p (b s r) -> p b s r", b=22, s=16)
            hist_from_state(sfc[l].re("s r n -> (s r) n"), 2, DFF, lambda b: hs[:, b, :, :])
        for gb in range(22):
            sv = fill(wu[l].cols(256 * gb, 256), 256, key=("U", l, gb))
            gpre = pg(8 + gb % 3)[:, 0, 0:S * W].re("p (s w) -> p s w", s=S)
            if tc.samp:
                B.cp(gpre[:, :, 0:2], hs[:, gb, :, :], eng="act")
            else:
                B.cp(gpre[:, 0, 0:2], histF[l][:, gb, :], eng="act")
            psg = proj_fm(sv, 0, 128, T)
            B.cp(gpre[:, :, 2:2 + Tt], psg[:, 0:T].re("p (s t) -> p s t", s=S), eng="act")
            psv = proj_fm(sv, 1, 128, T)
            if not tc.samp:
                B.cp(histF[l][:, gb, :], gpre[:, 0, Tt:Tt + 2], eng="act")
            gcv = pg(11 + gb % 2)[:, 0, 0:T]
            conv_fm(gcv.re("p (s t) -> p s t", s=S), gpre, Tt,
                    [pf[l][:, PF_FCW + 3 * gb + i:PF_FCW + 3 * gb + i + 1] for i in range(3)], 3)
            B.act(gcv, gcv, AF.Gelu_apprx_tanh, bias=pf[l][:, PF_FCB + gb:PF_FCB + gb + 1])
            B.tt(hT[:, gb, 0:T], gcv, psv[:, 0:T], ALU.mult)
            if tc.state_out:
                pt = proj_tm(sv, 128, tc.NB - 1)
                state_rows_out(tc, pt, 128, 2, o_pfc[l], o_sfc[l], 128 * gb)
        for m in range(8):
            ps = B.nps()
            for a in range(2):
                s_ = slots[slot_i[0] % NSLOT]
                slot_i[0] += 1
                B.dma(s_[:, 0:1408], wd[l][m][:, 1408 * a:1408 * a + 1408], eng="pool", group="w:" + s_.k[0])
                sv = s_[:, 0:1408].re("p (c n) -> p c n", c=11)
                TN = max(T, 256)
                for c in range(11):
                    B.mm(ps[:, 0:TN], sv[:, c, :], hT[:, 11 * a + c, 0:TN], start=(a == 0 and c == 0), stop=(a == 1 and c == 10))
            B.stt(pg(m)[:, 0, 0:T], xT[:, m, 0:T], float(ALPHA), ps[:, 0:T], ALU.mult, ALU.add)
        nxt = next_layer.get((tc.kind, tc.ti, l))
        if nxt is not None:
            for si in range(2):
                prefetch(("A", nxt, si), wi[nxt].cols(256 * si, 256), 256)
        layer_norm(tc, l, PF_L2G, PF_L2B)

    stages = cfg.get("stages", "ABCWF")
    next_layer = {}
    seq = [(k, t, l) for (k, t) in tiles for l in range(L)]
    if "A" in stages:
        for a, b in zip(seq[:-1], seq[1:]):
            next_layer[a] = b[2]
    for (kind, ti) in tiles:
        tc = TileCtx(kind, ti, NPT)
        load_x(tc)
        for l in range(L):
            if "A" in stages:
                phase_A(tc, l)
            if "B" in stages:
                phase_B(tc, l)
            if "C" in stages:
                phase_C(tc, l)
            if "W" in stages:
                dump("heads_%d" % l, rpg(0, 8).f()[:, :, 0:tc.T])
                wout_ln1(tc, l)
                dump("x1_%d" % l, xT[:, :, 0:tc.T])
            if "F" in stages:
                ffn_ln2(tc, l)
                dump("x2_%d" % l, xT[:, :, 0:tc.T])
        store_y(tc)
    with B.stack:
        stats = B.P.build()
    return nc, stats


_W_NAMES = ["w_in", "conv_a_w", "a_log", "dt_bias", "norm_a_w", "conv_b_w", "conv_b_b", "lru_w_r", "lru_b_r", "lru_w_i", "lru_b_i",
            "lru_lambda", "gla_w2", "gla_b2", "norm_c_w", "w_out", "ln1_g", "ln1_b", "ffn_w_up", "ffn_conv_w", "ffn_conv_b",
            "ffn_w_down", "ln2_g", "ln2_b"]


def make_in_maps(inputs, cores):
    w = {k: np.asarray(inputs[k], np.float32) for k in _W_NAMES}
    pk = pack_weights(w)
    maps = []
    for c in cores:
        m = dict(pk)
        m["xp"] = np.ascontiguousarray(inputs["x_prompt"][c])
        m["xs"] = np.ascontiguousarray(inputs["x_sample"][16 * c:16 * c + 16].reshape(128, D))
        for nm, key in (("sdc", "state_delta_conv"), ("sdl", "state_delta"), ("slc", "state_lru_conv"), ("slr", "state_lru"),
                        ("sgl", "state_gla"), ("sfc", "state_ffn_conv")):
            m[nm] = np.ascontiguousarray(inputs[key][:, 16 * c:16 * c + 16])
        maps.append(m)
    return maps


def kernel(**inputs):
    inputs = {k: np.asarray(v) for k, v in inputs.items()}
    nc, stats = build_program({})
    maps = make_in_maps(inputs, list(range(NCORES)))
    res = run_bass_kernel_spmd(nc, maps, core_ids=list(range(NCORES)))
    r = res.results
    y_prompt = np.stack([r[c]["yp"] for c in range(NCORES)]).reshape(8, 2048, D)
    y_sample = np.concatenate([r[c]["ys"].reshape(16, 8, D) for c in range(NCORES)], axis=0)
    outs = [y_prompt, y_sample]
    for nm in ["o_pdc", "o_pdl", "o_plc", "o_plr", "o_pgl", "o_pfc"]:
        outs.append(np.stack([r[c][nm] for c in range(NCORES)], axis=1))
    for nm in ["o_sdc", "o_sdl", "o_slc", "o_slr", "o_sgl", "o_sfc"]:
        outs.append(np.concatenate([r[c][nm] for c in range(NCORES)], axis=1))
    return tuple(np.ascontiguousarray(o, dtype=np.float32) for o in outs)
```

```python
import bisect
import contextlib
import numpy as np
import concourse.bass as bass
import concourse.mybir as mybir
from concourse.bass_utils import run_bass_kernel_spmd

F32 = mybir.dt.float32
F32R = mybir.dt.float32r
AF = mybir.ActivationFunctionType
ALU = mybir.AluOpType
AX = mybir.AxisListType

D = 1024
DFF = 2816
ALPHA = 4.0 ** 0.25
EPS = 1e-6
NCORES = 8
EPOCH = 8192
STRICT_SAME_ENGINE = True
ENGS = ("pe", "dve", "act", "pool", "sp")


class Prog:
    def __init__(self, nc):
        self.nc = nc
        self.ops = []

    def op(self, eng, fn, reads=(), writes=(), group=None):
        self.ops.append((eng, fn, tuple(reads), tuple(writes), group))

    def build(self):
        nc = self.nc
        ops = self.ops
        n = len(ops)
        last_w = {}
        readers = {}
        deps = [None] * n
        for i, (eng, fn, rd, wr, grp) in enumerate(ops):
            d = set()
            for k in rd:
                j = last_w.get(k)
                if j is not None:
                    d.add((j, True))
                if k.startswith("ps"):
                    for r in readers.get(k, ()):
                        if ops[r][0] != eng:
                            d.add((r, False))
            for k in wr:
                j = last_w.get(k)
                if j is not None:
                    d.add((j, False))
                for r in readers.get(k, ()):
                    d.add((r, False))
            deps[i] = d
            for k in rd:
                readers.setdefault(k, []).append(i)
            for k in wr:
                last_w[k] = i
                readers[k] = []
        has_consumer = [False] * n
        fdeps = [None] * n
        for i, (eng, fn, rd, wr, grp) in enumerate(ops):
            raw = {j for (j, r) in deps[i] if r}
            nd = set()
            for (j, r) in deps[i]:
                if j == i:
                    continue
                pe, _, _, _, pg = ops[j]
                if pe == eng and pg is None:
                    if eng == "pe":
                        continue
                    if j not in raw and not STRICT_SAME_ENGINE:
                        continue
                nd.add(j)
            fdeps[i] = nd
            for j in nd:
                has_consumer[j] = True
        eng_cnt = {e: 0 for e in ENGS}
        grp_ops = {}
        sig = [None] * n
        for i, (eng, fn, rd, wr, grp) in enumerate(ops):
            if grp is not None:
                grp_ops.setdefault(grp, []).append(i)
                sig[i] = ("g", grp, 0)
            elif has_consumer[i]:
                c = eng_cnt[eng]
                eng_cnt[eng] = c + 1
                sig[i] = ("e", (eng, c // EPOCH), (c % EPOCH) + 1)
        sem_keys = []
        seen_keys = set()
        for s in sig:
            if s is not None and (s[0], s[1]) not in seen_keys:
                seen_keys.add((s[0], s[1]))
                sem_keys.append((s[0], s[1]))
        stack = contextlib.ExitStack()
        sems = {}
        for num, k in enumerate(sem_keys):
            sems[k] = stack.enter_context(nc.semaphore("s%d" % num))
        per_eng = {e: [] for e in ENGS}
        for i, o in enumerate(ops):
            per_eng[o[0]].append(i)
        group_owner = {}
        for i, o in enumerate(ops):
            if o[4] is not None:
                group_owner.setdefault(o[4], o[0])
        self.stats = dict(n_ops=n, n_sems=len(sem_keys), per_eng={e: len(v) for e, v in per_eng.items()})

        def emit_engine(eng_name, eobj):
            seen = {}
            for i in per_eng[eng_name]:
                eng, fn, rd, wr, grp = ops[i]
                need = {}
                for j in fdeps[i]:
                    kind, key, val = sig[j]
                    if kind == "g":
                        val = 16 * bisect.bisect_left(grp_ops[key], i)
                    kk = (kind, key)
                    if val > need.get(kk, 0):
                        need[kk] = val
                for kk, val in need.items():
                    if seen.get(kk, 0) >= val:
                        continue
                    seen[kk] = val
                    eobj.wait_ge(sems[kk], val)
                inst = fn(eobj)
                s = sig[i]
                if s is not None:
                    inst.then_inc(sems[(s[0], s[1])], 16 if s[0] == "g" else 1)
            for g, lst in grp_ops.items():
                if group_owner[g] == eng_name:
                    eobj.wait_ge(sems[("g", g)], 16 * len(lst))

        with stack:
            with nc.Block() as block:
                @block.tensor
                def _(e):
                    emit_engine("pe", e)

                @block.vector
                def _(e):
                    emit_engine("dve", e)

                @block.scalar
                def _(e):
                    emit_engine("act", e)

                @block.gpsimd
                def _(e):
                    emit_engine("pool", e)

                @block.sync
                def _(e):
                    emit_engine("sp", e)
        return self.stats


class V:
    def __init__(self, ap, keys):
        self.ap = ap
        self.k = tuple(keys)

    def __getitem__(self, idx):
        return V(self.ap[idx], self.k)

    def re(self, pat, **kw):
        return V(self.ap.rearrange(pat, **kw), self.k)

    def bc(self, axis, shape):
        return V(self.ap.unsqueeze(axis).to_broadcast(list(shape)), self.k)

    def kk(self, *keys):
        return V(self.ap, keys)

    def r(self):
        return V(self.ap.bitcast(F32R), self.k)

    def f(self):
        return V(self.ap.bitcast(F32), self.k)


def _ks(*vs):
    out = []
    for v in vs:
        if isinstance(v, V):
            out.extend(v.k)
    return out


def _a(v):
    return v.ap if isinstance(v, V) else v


class Builder:
    def __init__(self, nc):
        self.nc = nc
        self.P = Prog(nc)
        self.stack = contextlib.ExitStack()
        self.psn = 0
        self.ps = []
        self.uid = 0

    def sb(self, name, shape, dt=F32, key=None):
        t = self.stack.enter_context(self.nc.sbuf_tensor("sb_" + name, list(shape), dt))
        return V(t[:], [key or name])

    def init_psum(self):
        for i in range(8):
            t = self.stack.enter_context(self.nc.psum_tensor("ps%d" % i, [128, 512], F32))
            self.ps.append(V(t[:], ["ps%d" % i]))

    def nps(self):
        v = self.ps[self.psn % 8]
        self.psn += 1
        return v

    def mm(self, out, lhsT, rhs, start=True, stop=True):
        rd = _ks(lhsT, rhs) + ([] if start else _ks(out))
        self.P.op("pe", lambda e: e.matmul(out.ap, lhsT=lhsT.ap, rhs=rhs.ap, start=start, stop=stop), rd, _ks(out))

    def tr(self, out, in_, ident):
        self.P.op("pe", lambda e: e.transpose(out.ap, in_.ap, ident.ap), _ks(in_, ident), _ks(out))

    def act(self, out, in_, func, bias=None, scale=None, eng="act"):
        kw = {}
        if bias is not None:
            kw["bias"] = _a(bias)
        if scale is not None:
            kw["scale"] = _a(scale)
        self.P.op("act", lambda e: e.activation(out=out.ap, in_=in_.ap, func=func, **kw), _ks(in_, bias, scale), _ks(out))

    def tt(self, out, a, b, op, eng="dve"):
        self.P.op(eng, lambda e: e.tensor_tensor(out=out.ap, in0=a.ap, in1=b.ap, op=op), _ks(a, b), _ks(out))

    def ts(self, out, a, s1, op0, s2=None, op1=None, eng="dve"):
        if op1 is None:
            self.P.op(eng, lambda e: e.tensor_scalar(out=out.ap, in0=a.ap, scalar1=_a(s1), scalar2=None, op0=op0), _ks(a, s1), _ks(out))
        else:
            self.P.op(eng, lambda e: e.tensor_scalar(out=out.ap, in0=a.ap, scalar1=_a(s1), scalar2=_a(s2), op0=op0, op1=op1),
                      _ks(a, s1, s2), _ks(out))

    def stt(self, out, a, s, b, op0, op1):
        self.P.op("dve", lambda e: e.scalar_tensor_tensor(out=out.ap, in0=a.ap, scalar=_a(s), in1=b.ap, op0=op0, op1=op1),
                  _ks(a, s, b), _ks(out))

    def cp(self, out, in_, eng="dve"):
        if eng == "act":
            self.P.op("act", lambda e: e.activation(out=out.ap, in_=in_.ap, func=AF.Copy), _ks(in_), _ks(out))
        else:
            self.P.op(eng, lambda e: e.tensor_copy(out=out.ap, in_=in_.ap), _ks(in_), _ks(out))

    def red(self, out, in_, op=ALU.add):
        self.P.op("dve", lambda e: e.tensor_reduce(out=out.ap, in_=in_.ap, axis=AX.X, op=op), _ks(in_), _ks(out))

    def recip(self, out, in_):
        self.P.op("dve", lambda e: e.reciprocal(out=out.ap, in_=in_.ap), _ks(in_), _ks(out))

    def scan(self, out, d0, d1, init):
        self.P.op("dve", lambda e: e.tensor_tensor_scan(out=out.ap, data0=d0.ap, data1=d1.ap, initial=_a(init), op0=ALU.mult, op1=ALU.add),
                  _ks(d0, d1, init), _ks(out))

    def memset(self, out, val, eng="dve"):
        self.P.op(eng, lambda e: e.memset(out.ap, val), [], _ks(out))

    def dma(self, out, in_, eng="sp", group=None, nc_ok=False):
        self.uid += 1
        g = group or ("d%d" % self.uid)
        if nc_ok:
            self.P.op(eng, lambda e: e.dma_start(out=out.ap, in_=in_.ap, allow_slow_non_contiguous=True), _ks(in_), _ks(out), group=g)
        else:
            self.P.op(eng, lambda e: e.dma_start(out=out.ap, in_=in_.ap), _ks(in_), _ks(out), group=g)


NWI = 18 * 128 + 4 * 512 + 256
FM_SRC = [(128 * b, 128) for b in range(9)] + [(1548, 128), (1676, 128), (1804, 128), (1932, 128),
                                                (2060, 96), (2156, 96), (2252, 96), (2348, 96), (3212, 16)]
TM_SRC = [(1152, 396), (2444, 384), (2828, 384), (2252, 192)]
NPF = 172
PF_CA, PF_CB, PF_CBB, PF_LBR, PF_LBI, PF_LAM, PF_L1G, PF_L1B, PF_L2G, PF_L2B, PF_FCW, PF_FCB = 0, 36, 44, 46, 48, 50, 52, 60, 68, 76, 84, 150
NRB = 140
NEG = -30000.0


def _mask_set(c):
    idx = np.arange(128)
    seg = idx // c
    same = seg[:, None] == seg[None, :]
    ns = 128 // c
    tri = (same & (idx[:, None] <= idx[None, :])).astype(np.float32)
    segm = same.astype(np.float32)
    neg_strit = np.where(same & (idx[None, :] < idx[:, None]), 0.0, NEG).astype(np.float32)
    neg_tri = np.where(same & (idx[:, None] <= idx[None, :]), 0.0, NEG).astype(np.float32)
    seg01 = np.zeros((128, 16), np.float32)
    seg01[idx, seg] = 1.0
    last01 = np.zeros((128, 16), np.float32)
    li = (idx % c) == (c - 1)
    last01[idx[li], seg[li]] = 1.0
    return np.concatenate([tri, segm, neg_strit, neg_tri, seg01, last01], axis=1)


C_ID, C_ONE, C_B64, C_MP, C_MS = 0, 128, 256, 384, 384 + 544
NCONST = 384 + 2 * 544
M_TRI, M_SEGM, M_NSTRIT, M_NTRI, M_SEG, M_LAST = 0, 128, 256, 384, 512, 528


WI_PANELS = [(0, 256), (256, 256), (512, 256), (768, 256), (1024, 128), (2304, 256), (2560, 256),
             (1152, 256), (1408, 256), (1664, 256), (1920, 256), (2176, 16),
             (2816, 256), (4352, 256), (3328, 256), (3840, 256)]
WO_PANELS = [(0, 256), (256, 256), (512, 256), (768, 256)]
WU_PANELS = [(256 * gb, 256) for gb in range(22)]


def _panel_offsets(panels):
    offs, o = {}, 0
    for (c0, n) in panels:
        offs[(c0, n)] = o
        o += 8 * n
    return offs, o


WI_OFF, WI_TOT = _panel_offsets(WI_PANELS)
WO_OFF, WO_TOT = _panel_offsets(WO_PANELS)
WU_OFF, WU_TOT = _panel_offsets(WU_PANELS)


def _pack_panels(w, panels, tot):
    L = w.shape[0]
    out = np.zeros((L, 128, tot), np.float32)
    o = 0
    for (c0, n) in panels:
        blk = w[:, :, c0:c0 + n].reshape(L, 8, 128, n).transpose(0, 2, 1, 3).reshape(L, 128, 8 * n)
        out[:, :, o:o + 8 * n] = blk
        o += 8 * n
    return out


def make_consts():
    ident = np.eye(128, dtype=np.float32)
    ones = np.ones((128, 128), np.float32)
    b64 = np.zeros((128, 128), np.float32)
    b64[:64, :64] = 1.0
    b64[64:, 64:] = 1.0
    return np.ascontiguousarray(np.concatenate([ident, ones, b64, _mask_set(128), _mask_set(8)], axis=1))


def pack_weights(w):
    L = 2
    wi = np.zeros((L, D, NWI), np.float32)
    for b, (s, n) in enumerate(FM_SRC):
        wi[:, :, 128 * b:128 * b + n] = w["w_in"][:, :, s:s + n]
    for g, (s, n) in enumerate(TM_SRC):
        wi[:, :, 2304 + 512 * g:2304 + 512 * g + n] = w["w_in"][:, :, s:s + n]
    wi[:, :, 4352:4480] = w["w_in"][:, :, 2444 + 256:2444 + 384]
    wi[:, :, 4480:4608] = w["w_in"][:, :, 2828 + 256:2828 + 384]
    wu = np.zeros((L, D, 2 * DFF), np.float32)
    for gb in range(22):
        wu[:, :, 256 * gb:256 * gb + 128] = w["ffn_w_up"][:, :, 128 * gb:128 * gb + 128]
        wu[:, :, 256 * gb + 128:256 * gb + 256] = w["ffn_w_up"][:, :, DFF + 128 * gb:DFF + 128 * gb + 128]
    wd = np.ascontiguousarray(w["ffn_w_down"].reshape(L, 22, 128, 8, 128).transpose(0, 3, 2, 1, 4)).reshape(L, 8, 128, 22 * 128)
    pf = np.zeros((L, 128, NPF), np.float32)

    def pc(a):
        return a.reshape(L, -1, 128).transpose(0, 2, 1)
    for i in range(4):
        pf[:, :, PF_CA + i:PF_CA + 36:4] = pc(w["conv_a_w"][:, i])
        pf[:, :, PF_CB + i:PF_CB + 8:4] = pc(w["conv_b_w"][:, i])
    pf[:, :, PF_CBB:PF_CBB + 2] = pc(w["conv_b_b"])
    pf[:, :, PF_LBR:PF_LBR + 2] = pc(w["lru_b_r"])
    pf[:, :, PF_LBI:PF_LBI + 2] = pc(w["lru_b_i"])
    pf[:, :, PF_LAM:PF_LAM + 2] = pc(w["lru_lambda"])
    pf[:, :, PF_L1G:PF_L1G + 8] = pc(w["ln1_g"])
    pf[:, :, PF_L1B:PF_L1B + 8] = pc(w["ln1_b"])
    pf[:, :, PF_L2G:PF_L2G + 8] = pc(w["ln2_g"])
    pf[:, :, PF_L2B:PF_L2B + 8] = pc(w["ln2_b"])
    for i in range(3):
        pf[:, :, PF_FCW + i:PF_FCW + 66:3] = pc(w["ffn_conv_w"][:, i])
    pf[:, :, PF_FCB:PF_FCB + 22] = pc(w["ffn_conv_b"])
    rb = np.zeros((L, 128, NRB), np.float32)
    rb[:, :, 0:6] = w["a_log"][:, None, :]
    rb[:, :, 6:12] = w["dt_bias"][:, None, :]
    rb[:, :, 12:76] = w["norm_a_w"][:, None, :]
    rb[:, :, 76:140] = w["norm_c_w"][:, None, :]
    lw = np.zeros((L, 128, 4, 128), np.float32)
    for gi, nm in enumerate(["lru_w_r", "lru_w_i"]):
        for blk in range(2):
            for q in range(2):
                lw[:, 64 * q:64 * q + 64, 2 * gi + blk, 64 * q:64 * q + 64] = w[nm][:, 2 * blk + q]
    w2 = np.zeros((L, 32, 192), np.float32)
    w2[:, 0:16] = w["gla_w2"]
    w2[:, 16] = w["gla_b2"]
    return dict(wi=_pack_panels(wi, WI_PANELS, WI_TOT), wo=_pack_panels(np.ascontiguousarray(w["w_out"]), WO_PANELS, WO_TOT),
                wu=_pack_panels(wu, WU_PANELS, WU_TOT), wd=wd, pf=pf, rb=rb,
                lw=np.ascontiguousarray(lw.reshape(L, 128, 512)), w2=w2, cst=make_consts())


PW = 516
NPAGES = 32
NSLOT = 3
SLOTW = 2048


class TileCtx:
    def __init__(self, kind, ti=0, nprompt=4):
        self.kind = kind
        self.ti = ti
        self.samp = kind == "s"
        self.T = 128 if self.samp else 512
        self.NB = self.T // 128
        self.S = 16 if self.samp else 1
        self.Tt = 8 if self.samp else 512
        self.NS = 16 if self.samp else 1
        self.K = 2 if self.samp else 6
        self.mb = C_MS if self.samp else C_MP
        self.first = (not self.samp) and ti == 0
        self.last = (not self.samp) and ti == nprompt - 1
        self.state_out = self.samp or self.last


def build_program(cfg):
    nc = bass.Bass("TRN2", target_bir_lowering=False)
    B = Builder(nc)
    L = cfg.get("depth", 2)
    NPT = cfg.get("nprompt", 4)
    tiles = cfg.get("tiles", [("p", i) for i in range(NPT)] + [("s", 0)])
    dbg = cfg.get("dbg", {})

    def din(name, shape):
        return V(nc.dram_tensor(name, list(shape), F32, kind="ExternalInput").ap(), ())

    def dout(name, shape):
        return V(nc.dram_tensor(name, list(shape), F32, kind="ExternalOutput").ap(), ())

    xp = din("xp", [2048, D]); xs = din("xs", [128, D])
    sdc = din("sdc", [2, 16, 3, 1152]); sdl = din("sdl", [2, 16, 6, 64, 64]); slc = din("slc", [2, 16, 3, 256])
    slr = din("slr", [2, 16, 256]); sgl = din("sgl", [2, 16, 6, 32, 64]); sfc = din("sfc", [2, 16, 2, DFF])
    wi_p = din("wi", [2, 128, WI_TOT]); wo_p = din("wo", [2, 128, WO_TOT]); wu_p = din("wu", [2, 128, WU_TOT]); wd = din("wd", [2, 8, 128, DFF])

    class _Panels:
        def __init__(self, packed, offs):
            self.packed, self.offs, self.l = packed, offs, None

        def __getitem__(self, l):
            p = _Panels(self.packed, self.offs)
            p.l = l
            return p

        def cols(self, c0, n):
            o = self.offs[(c0, n)]
            return self.packed[self.l][:, o:o + 8 * n]
    wi = _Panels(wi_p, WI_OFF); wo = _Panels(wo_p, WO_OFF); wu = _Panels(wu_p, WU_OFF)
    pfd = din("pf", [2, 128, NPF]); rbd = din("rb", [2, 128, NRB]); lwd = din("lw", [2, 128, 512]); w2d = din("w2", [2, 32, 192])
    cstd = din("cst", [128, NCONST])
    yp = dout("yp", [2048, D]); ys = dout("ys", [128, D])
    o_pdc = dout("o_pdc", [2, 3, 1152]); o_pdl = dout("o_pdl", [2, 6, 64, 64]); o_plc = dout("o_plc", [2, 3, 256])
    o_plr = dout("o_plr", [2, 256]); o_pgl = dout("o_pgl", [2, 6, 32, 64]); o_pfc = dout("o_pfc", [2, 2, DFF])
    o_sdc = dout("o_sdc", [2, 16, 3, 1152]); o_sdl = dout("o_sdl", [2, 16, 6, 64, 64]); o_slc = dout("o_slc", [2, 16, 3, 256])
    o_slr = dout("o_slr", [2, 16, 256]); o_sgl = dout("o_sgl", [2, 16, 6, 32, 64]); o_sfc = dout("o_sfc", [2, 16, 2, DFF])
    dbg_out = {k: dout("dbg_" + k, shp) for k, shp in dbg.items()}

    B.init_psum()
    xT = B.sb("xT", [128, 8, 512])
    slots = [B.sb("slot%d" % i, [128, SLOTW], F32R) for i in range(NSLOT)]
    xin = [B.sb("xin%d" % i, [128, 1024]) for i in range(2)]
    stg = B.sb("stg", [128, 256])
    cst = B.sb("cst", [128, NCONST])
    cstR = B.sb("cstR", [128, 256], F32R)
    pf = [B.sb("pf%d" % l, [128, NPF]) for l in range(2)]
    rb = [B.sb("rb%d" % l, [128, NRB]) for l in range(2)]
    lw = [B.sb("lw%d" % l, [128, 512]) for l in range(2)]
    w2 = [B.sb("w2_%d" % l, [32, 192]) for l in range(2)]
    nea = [B.sb("nea%d" % l, [128, 6]) for l in range(2)]
    lc12 = [B.sb("lc12_%d" % l, [128, 4]) for l in range(2)]
    histA = [B.sb("histA%d" % l, [128, 9, 3]) for l in range(2)]
    histB = [B.sb("histB%d" % l, [128, 2, 3]) for l in range(2)]
    histF = [B.sb("histF%d" % l, [128, 22, 2]) for l in range(2)]
    SAp = [B.sb("SAp%d" % l, [128, 3, 64]) for l in range(2)]
    SCp = [B.sb("SCp%d" % l, [128, 2, 64]) for l in range(2)]
    hlp = [B.sb("hlp%d" % l, [128, 2]) for l in range(2)]
    small = B.sb("small", [128, 256])
    HG = 3
    SOLVE_R = cfg.get("solve_r", False)
    SDT = F32R if SOLVE_R else F32
    hbtR = B.sb("hbtR", [128, 5 * HG, 128], SDT)
    hbt2R = B.sb("hbt2R", [128, 2 * HG, 256], SDT)
    hbt = B.sb("hbt", [128, 4, 128])
    uwb = B.sb("uwb", [128, HG, 320])
    otm_x = B.sb("otm_x", [128, 3, 128])
    gat = B.sb("gat", [128, 12, 24])
    sab = B.sb("sab", [128, 16, 64])
    wxb = B.sb("wxb", [128, 16, 64])
    B.memset(hbt[:, 0, :].kk("hbs0"), 0.0)
    for sl_ in range(HG):
        B.cp(V(hbtR.ap[:, 5 * sl_ + 4, :], ["hbpad%d" % sl_]), hbt[:, 0, :].kk("hbs0"), eng="dve")
    arena_t = B.stack.enter_context(nc.sbuf_tensor("arena", [128, NPAGES, PW], F32))
    arenaR_t = B.stack.enter_context(nc.sbuf_tensor("arenaR", [128, 22, 512], F32R))

    def pg(p0, n=1):
        return V(arena_t[:, p0:p0 + n, :], ["ar%d" % p for p in range(p0, p0 + n)])

    def pgf(p0, n, width):
        assert width <= n * PW
        return V(arena_t[:, p0:p0 + n, :].rearrange("p a b -> p (a b)")[:, 0:width], ["ar%d" % p for p in range(p0, p0 + n)])

    def rpg(p0, n=1):
        return V(arenaR_t[:, p0:p0 + n, :], ["rp%d" % p for p in range(p0, p0 + n)])

    ident = cst[:, C_ID:C_ID + 128]
    ones = cst[:, C_ONE:C_ONE + 128]
    onesR = cstR[:, 0:128]
    b64R = cstR[:, 128:256]

    scnt = [0]

    def sm(n, tag=""):
        o = scnt[0] % 16
        scnt[0] += 1
        return V(small.ap[:, 16 * o:16 * o + n], ["sm%d" % o])

    hcnt = {"a": 0, "b": 0}

    def hbuf(tag=""):
        i = hcnt["a"] % 4
        hcnt["a"] += 1
        return V(hbt.ap[:, i, :], ["hbs%d" % i])

    def hbuf2(tag=""):
        raise RuntimeError("unused")

    B.dma(cst, cstd, group="const")
    for l in range(L):
        B.dma(pf[l], pfd[l], group="const")
        B.dma(rb[l], rbd[l], group="const")
        B.dma(lw[l], lwd[l], group="const")
        B.dma(w2[l], w2d[l], group="const")
    B.cp(cstR, cst[:, C_ONE:C_ONE + 256], eng="dve")
    for l in range(L):
        t = sm(6)
        B.act(t, rb[l][:, 0:6], AF.Exp)
        B.ts(nea[l], t, -1.0, ALU.mult)
        t2 = sm(2)
        B.act(t2, pf[l][:, PF_LAM:PF_LAM + 2], AF.Exp, scale=-1.0)
        t3 = sm(2)
        B.act(t3, t2, AF.Ln, bias=1.0)
        B.ts(lc12[l][:, 0:2], t3, -8.0, ALU.mult)
        B.ts(lc12[l][:, 2:4], t3, -16.0, ALU.mult)
        for tl in (SAp[l], SCp[l], hlp[l], histA[l], histB[l], histF[l]):
            B.memset(tl, 0.0)

    slot_i = [0]

    prefetched = {}

    def prefetch(key, src2d, ncols):
        if key not in prefetched:
            prefetched[key] = fill(src2d, ncols)

    def fill(src2d, ncols, nk=8, key=None):
        if key is not None and key in prefetched:
            return prefetched.pop(key)
        assert nk * ncols <= SLOTW
        s = slots[slot_i[0] % NSLOT]
        slot_i[0] += 1
        sv = s[:, 0:nk * ncols].re("p (c n) -> p c n", c=nk)
        B.dma(s[:, 0:nk * ncols], src2d, eng="pool", group="w:" + s.k[0])
        return sv

    evn = [0]

    def evac(out, in_, scale=None):
        evn[0] += 1
        if scale is not None:
            B.act(out, in_, AF.Copy, scale=scale)
        elif evn[0] % 2:
            B.cp(out, in_, eng="act")
        else:
            B.cp(out, in_, eng="dve")

    def dump(name, view):
        if name in dbg_out:
            B.dma(dbg_out[name], view, group="dbg_" + name)

    xTr = xT.r()

    def proj_fm(sv, j, n, T, rhsT=None):
        ps = B.nps()
        rr = xTr if rhsT is None else rhsT
        TN = max(T, 256)
        for c in range(8):
            B.mm(ps[0:n, 0:TN], sv[:, c, 128 * j:128 * j + n], rr[:, c, 0:TN], start=(c == 0), stop=(c == 7))
        return ps

    def proj_tm(sv, n, blk, c0=0):
        ps = B.nps()
        for c in range(8):
            B.mm(ps[:, 0:n], xTr[:, c, 128 * blk:128 * blk + 128], sv[:, c, c0:c0 + n], start=(c == 0), stop=(c == 7))
        return ps

    def load_x(tc):
        src = xs if tc.samp else xp
        r0 = 0 if tc.samp else tc.ti * 512
        for blk in range(tc.NB):
            xi = xin[blk % 2]
            B.dma(xi, src[r0 + 128 * blk:r0 + 128 * blk + 128, :])
            for half in range(2):
                ps = B.nps()
                for q in range(4):
                    c = 4 * half + q
                    B.tr(ps[:, 128 * q:128 * q + 128], xi[:, 128 * c:128 * c + 128], ident)
                evac(xTr[:, 4 * half:4 * half + 4, 128 * blk:128 * blk + 128], ps.re("p (a b) -> p a b", a=4))

    def store_y(tc):
        dst = ys if tc.samp else yp
        r0 = 0 if tc.samp else tc.ti * 512
        for blk in range(tc.NB):
            xi = xin[blk % 2]
            for half in range(2):
                ps = B.nps()
                for q in range(4):
                    c = 4 * half + q
                    B.tr(ps[:, 128 * q:128 * q + 128], xT[:, c, 128 * blk:128 * blk + 128], ident)
                evac(xi[:, 512 * half:512 * half + 512], ps)
            B.dma(dst[r0 + 128 * blk:r0 + 128 * blk + 128, :], xi)

    def conv_fm(out3, pre3, Tt, wcols, ntap, bias=None):
        if bias is not None:
            B.ts(out3, pre3[:, :, 0:Tt], wcols[0], ALU.mult, bias, ALU.add)
        else:
            B.ts(out3, pre3[:, :, 0:Tt], wcols[0], ALU.mult)
        for i in range(1, ntap):
            B.stt(out3, pre3[:, :, i:i + Tt], wcols[i], out3, ALU.mult, ALU.add)

    def hist_from_state(state2d, nrows, nch, dst_fn):
        R = 16 * nrows
        nb = nch // 128
        for b0 in range(0, nb, 8):
            nbb = min(8, nb - b0)
            xi = xin[(b0 // 8) % 2]
            B.dma(xi[0:R, 0:128 * nbb], state2d[:, 128 * b0:128 * (b0 + nbb)])
            for b in range(nbb):
                ps = B.nps()
                B.tr(ps[:, 0:R], xi[0:R, 128 * b:128 * b + 128], ident[0:R, 0:R])
                evac(dst_fn(b0 + b), ps[:, 0:R].re("p (s r) -> p s r", r=nrows))

    def state_rows_out(tc, ps_tm, ncols, nrows, dst_p, dst_s, col0):
        evac(stg[:, 0:ncols], ps_tm[:, 0:ncols])
        if tc.samp:
            for r in range(nrows):
                base = stg.ap[:, 0:ncols]
                pstep = base.ap[0][0]
                srcv = V(bass.AP(base.tensor, base.offset + (8 - nrows + r) * pstep, [[8 * pstep, 16], [1, ncols]]), stg.k)
                B.dma(dst_s[:, r, col0:col0 + ncols], srcv, group="so")
        else:
            B.dma(dst_p[:, col0:col0 + ncols], stg[128 - nrows:128, 0:ncols], group="so")

    def rms_gate_gen(tc, l, blk, o_tm, z_view, nw, hd0, p_sq=29, p_sz=30):
        o2 = o_tm.re("p h v -> p (h v)")
        sq = pgf(p_sq, 1, 384)
        B.act(sq, o2, AF.Square)
        sz = pgf(p_sz, 1, 384)
        B.act(sz, z_view, AF.Silu)
        yield
        ss = sm(6)
        B.red(ss, sq.re("p (h v) -> p h v", h=6))
        B.ts(ss, ss, 1.0 / 64, ALU.mult, EPS, ALU.add)
        yield
        B.act(ss, ss, AF.Sqrt)
        yield
        rs = sm(6)
        B.recip(rs, ss)
        B.tt(o_tm, o_tm, rs.bc(2, [128, 6, 64]), ALU.mult)
        yield
        B.tt(o_tm, o_tm, nw.bc(1, [128, 6, 64]), ALU.mult)
        yield
        B.tt(o2, o2, sz, ALU.mult)
        ps = B.nps()
        for j in range(3):
            B.tr(ps[:, 128 * j:128 * j + 128], o2[:, 128 * j:128 * j + 128], ident)
        yield
        evac(rpg(hd0, 3)[:, :, 128 * blk:128 * blk + 128], ps[:, 0:384].re("p (a b) -> p a b", a=3))

    def rms_gate_heads(tc, l, blk, o_tm, z_view, nw, hd0, p_sq=29, p_sz=30):
        for _ in rms_gate_gen(tc, l, blk, o_tm, z_view, nw, hd0, p_sq, p_sz):
            pass

    def phase_A(tc, l):
        T, NB, S, Tt, NS, mb = tc.T, tc.NB, tc.S, tc.Tt, tc.NS, tc.mb
        TRI = cst[:, mb + M_TRI:mb + M_TRI + 128]
        SEGM = cst[:, mb + M_SEGM:mb + M_SEGM + 128]
        NSTRIT = cst[:, mb + M_NSTRIT:mb + M_NSTRIT + 128]
        NTRI = cst[:, mb + M_NTRI:mb + M_NTRI + 128]
        SEG = cst[:, mb + M_SEG:mb + M_SEG + 16]
        LAST = cst[:, mb + M_LAST:mb + M_LAST + 16]
        W = 3 + Tt

        def pre(b):
            return pg(b % 3)[:, 0, 0:S * W].re("p (s w) -> p s w", s=S)

        def qk(b):
            return pg(4 + b)[:, 0, 0:T]

        hsA = pgf(3, 1, 9 * 48).re("p (b s r) -> p b s r", b=9, s=16)
        if tc.samp:
            hist_from_state(sdc[l].re("s r n -> (s r) n"), 3, 1152, lambda b: hsA[:, b, :, :])
        for si in range(5):
            c0 = 256 * si
            ncq = 256 if si < 4 else 128
            sv = fill(wi[l].cols(c0, ncq), ncq, key=("A", l, si))
            for j in range(ncq // 128):
                b = 2 * si + j
                if tc.samp:
                    B.cp(pre(b)[:, :, 0:3], hsA[:, b, :, :], eng="dve")
                else:
                    B.cp(pre(b)[:, 0, 0:3], histA[l][:, b, :], eng="dve")
                ps = proj_fm(sv, j, 128, T)
                evac(pre(b)[:, :, 3:3 + Tt], ps[:, 0:T].re("p (s t) -> p s t", s=S))
                if not tc.samp:
                    B.cp(histA[l][:, b, :], pre(b)[:, 0, Tt:Tt + 3], eng="dve")
                o3 = qk(b).re("p (s t) -> p s t", s=S)
                conv_fm(o3, pre(b), Tt, [pf[l][:, PF_CA + 4 * b + i:PF_CA + 4 * b + i + 1] for i in range(4)], 4)
                B.act(qk(b), qk(b), AF.Silu)
            if tc.state_out:
                pt = proj_tm(sv, ncq, tc.NB - 1)
                state_rows_out(tc, pt, ncq, 3, o_pdc[l], o_sdc[l], c0)
        tmA = pgf(13, 4, NB * 396).re("p (b n) -> p b n", b=NB)
        for (zc0, zn) in ((0, 256), (256, 140)):
            sv = fill(wi[l].cols(2304 + zc0, 256), 256)
            for blk in range(NB):
                pt = proj_tm(sv, 256, blk)
                evac(tmA[:, blk, zc0:zc0 + zn], pt[:, 0:zn])
        dump("qkv_silu_%d" % l, pg(4, 9)[:, :, 0:T])
        if cfg.get("a_stop", 99) <= 1:
            return
        sqs = [rpg(8 + b)[:, 0, 0:T] for b in range(6)]
        rns = [pg(19 + b)[:, 0, 0:T] for b in range(6)]
        pss = []
        for b in range(6):
            B.act(sqs[b], qk(b), AF.Square)
        for b in range(6):
            ps = B.nps()
            pss.append(ps)
            B.mm(ps[:, 0:T], b64R, sqs[b])
        for b in range(6):
            B.act(rns[b], pss[b][:, 0:T], AF.Sqrt, bias=EPS)
        for b in range(6):
            B.recip(rns[b], rns[b])
            if b < 3:
                B.stt(qk(b), rns[b], 0.125, qk(b), ALU.mult, ALU.mult)
            else:
                B.tt(qk(b), qk(b), rns[b], ALU.mult)
        dump("qkn_%d" % l, pg(4, 6)[:, :, 0:T])
        if cfg.get("a_stop", 99) <= 2:
            return

        KL = tc.K
        pending_rms = []
        NG = 6 * NB
        gatv = [V(gat.ap[:, i, 0:NG], ["gat%d" % i]) for i in range(12)]

        def g3(v):
            return v.re("p (b h) -> p b h", h=6)
        B.act(g3(gatv[0]), tmA[:, :, 384:390], AF.Sigmoid)
        B.ts(gatv[1], gatv[0], -1.0, ALU.mult)
        B.tt(g3(gatv[8]), tmA[:, :, 390:396], rb[l][:, 6:12].bc(1, [128, NB, 6]), ALU.add)
        B.act(gatv[8], gatv[8], AF.Exp)
        B.act(gatv[8], gatv[8], AF.Ln, bias=1.0)
        B.tt(g3(gatv[2]), g3(gatv[8]), nea[l].bc(1, [128, NB, 6]), ALU.mult)
        psg = B.nps()
        B.mm(psg[:, 0:NG], TRI, gatv[2])
        B.mm(psg[:, 32:32 + NG], SEGM, gatv[2])
        B.cp(gatv[3], psg[:, 0:NG], eng="dve")
        B.act(gatv[4], psg[:, 0:NG], AF.Exp)
        B.act(gatv[5], psg[:, 32:32 + NG], AF.Exp)
        B.tt(gatv[6], psg[:, 32:32 + NG], gatv[3], ALU.subtract)
        B.act(gatv[6], gatv[6], AF.Exp)
        B.tt(gatv[7], gatv[0], gatv[4], ALU.mult)
        B.ts(gatv[11], gatv[3], -1.0, ALU.mult)
        if NS == 1:
            B.ts(gatv[9], gatv[5], LAST[:, 0:1], ALU.mult)
            psl = B.nps()
            B.mm(psl[:, 0:NG], ones, gatv[9])
            B.cp(gatv[10], psl[:, 0:NG], eng="act")
        for blk in range(NB):
            tc0 = 128 * blk
            za = tmA[:, blk, 0:384]
            ba = tmA[:, blk, 384:390]
            aa = tmA[:, blk, 390:396]
            ktm = pgf(17, 1, 384)
            vtm = pgf(18, 1, 384)
            for (dst, b0) in ((ktm, 3), (vtm, 6)):
                ps = B.nps()
                for j in range(3):
                    B.tr(ps[:, 128 * j:128 * j + 128], qk(b0 + j)[:, tc0:tc0 + 128], ident)
                evac(dst, ps[:, 0:384])
            c6 = slice(6 * blk, 6 * blk + 6)
            beta = gatv[0][:, c6]; nbeta = gatv[1][:, c6]; g = gatv[2][:, c6]; gc = gatv[3][:, c6]; egc = gatv[4][:, c6]
            egl = gatv[5][:, c6]; kdf = gatv[6][:, c6]; bexp = gatv[7][:, c6]
            if blk == 0:
                dump("g_%d" % l, g)
                dump("beta_%d" % l, beta)
            DG = pgf(19, 2, 768).re("p (h f) -> p h f", h=6)
            Dm = pgf(21, 2, 768).re("p (h f) -> p h f", h=6)
            DmT = pgf(23, 2, 768).re("p (h f) -> p h f", h=6)
            PE_ = cfg.get("pool_pre", 1)
            B.tt(DG, ident.bc(1, [128, 6, 128]), gc.bc(2, [128, 6, 128]), ALU.mult, eng=("pool" if PE_ else "dve"))
            ngc = gatv[11][:, c6]
            for hf in range(2):
                psr = B.nps()
                B.mm(psr[:, 0:384], ones, DG[:, 3 * hf:3 * hf + 3, :].re("p h f -> p (h f)"))
                R3 = psr[:, 0:384].re("p (h f) -> p h f", h=3)
                d1 = Dm[:, 3 * hf:3 * hf + 3, :]
                d2 = DmT[:, 3 * hf:3 * hf + 3, :]
                B.tt(d1, R3, NSTRIT.bc(1, [128, 3, 128]), ALU.subtract)
                B.tt(d2, R3, NTRI.bc(1, [128, 3, 128]), ALU.add)
                for hh in range(3):
                    h = 3 * hf + hh
                    B.act(Dm[:, h, :], Dm[:, h, :], AF.Exp, bias=gc[:, h:h + 1], scale=-1.0)
                    B.act(DmT[:, h, :], DmT[:, h, :], AF.Exp, bias=ngc[:, h:h + 1])
            bv = pgf(25, 1, 384).re("p (h v) -> p h v", h=6)
            kb = pgf(26, 1, 384).re("p (h v) -> p h v", h=6)
            kdec = pgf(27, 1, 384).re("p (h v) -> p h v", h=6)
            otm = pgf(28 if blk % 2 == 0 else 3, 1, 384).re("p (h v) -> p h v", h=6)
            v3 = vtm.re("p (h v) -> p h v", h=6)
            k3 = ktm.re("p (h v) -> p h v", h=6)
            pe_ = "pool" if PE_ else "dve"
            B.tt(bv, v3, beta.bc(2, [128, 6, 64]), ALU.mult, eng=pe_)
            B.tt(kb, k3, bexp.bc(2, [128, 6, 64]), ALU.mult, eng=pe_)
            B.tt(kdec, k3, kdf.bc(2, [128, 6, 64]), ALU.mult, eng=pe_)
            if NS == 1:
                glb = gatv[10][:, c6]
            else:
                SEL = pgf(31, 1, 96)
                glb = pgf(31, 1, 192)[:, 96:192].re("p (h s) -> p h s", h=6)
                B.tt(SEL.re("p (h s) -> p h s", h=6), egl.bc(2, [128, 6, 16]), LAST.bc(1, [128, 6, 16]), ALU.mult)
                psl = B.nps()
                B.mm(psl[:, 0:96], ones, SEL)
                B.cp(glb, psl[:, 0:96].re("p (h s) -> p h s", h=6), eng="act")
            SAs = sab

            def head_gen(h, slot):
                hp, po = h // 2, (h % 2) * 64
                kT = qk(3 + hp)[po:po + 64, tc0:tc0 + 128]
                qT = qk(hp)[po:po + 64, tc0:tc0 + 128]
                hb = [V(hbtR.ap[:, 5 * slot + i, :], ["hb%d_%d" % (slot, i)]) for i in range(4)]
                hw = [V(hbt2R.ap[:, 2 * slot + i, :], ["hc%d_%d" % (slot, i)]) for i in range(2)]
                Nm, NmT, Pa, Pb = hb
                Wa, Wb = hw
                NN = V(hbtR.ap[:, 5 * slot:5 * slot + 2, :].rearrange("p a b -> p (a b)"), Nm.k + NmT.k)
                PPa = V(hbtR.ap[:, 5 * slot + 2:5 * slot + 4, :].rearrange("p a b -> p (a b)"), Pa.k + Pb.k)
                PPb = V(hbtR.ap[:, 5 * slot + 3:5 * slot + 5, :].rearrange("p a b -> p (a b)"), Pb.k)
                ps = B.nps()
                B.mm(ps[:, 0:128], kT, kT)
                B.mm(ps[:, 128:256], kT, qT)
                yield
                B.stt(Nm, ps[:, 0:128], nbeta[:, h:h + 1], Dm[:, h, :], ALU.mult, ALU.mult)
                qkmT = V(otm_x.ap[:, slot, :], ["qkm%d" % slot])
                B.tt(qkmT, ps[:, 128:256], DmT[:, h, :], ALU.mult)
                ps = B.nps()
                B.tr(ps[:, 0:128], Nm.f(), ident)
                yield
                B.cp(NmT, ps[:, 0:128], eng="dve")
                B.tt(Wa[:, 128:256], ps[:, 0:128], ident, ALU.add)
                ps = B.nps()
                ps2 = B.nps()
                if SOLVE_R:
                    B.mm(ps[:, 0:256], NmT, NN)
                    B.mm(ps2[:, 0:256], Nm, NN)
                else:
                    B.mm(ps[:, 0:128], NmT, Nm)
                    B.mm(ps2[:, 128:256], Nm, NmT)
                yield
                B.cp(Pa, ps[:, 0:128], eng="act")
                B.cp(Wa[:, 0:128], ps2[:, 128:256], eng="dve")
                Pc, Pn, Wc, Wn, PPc, PPn = Pa, Pb, Wa, Wb, PPa, PPb
                for k in range(1, KL + 1):
                    lastk = k == KL
                    ps = B.nps()
                    if lastk and not SOLVE_R:
                        B.mm(ps[:, 128:256], Pc, Wc[:, 128:256])
                    else:
                        B.mm(ps[:, 0:256], Pc, Wc)
                    if not lastk:
                        ps2 = B.nps()
                        if SOLVE_R:
                            B.mm(ps2[:, 0:256], Wc[:, 0:128], PPc)
                        else:
                            B.mm(ps2[:, 0:128], Wc[:, 0:128], Pc)
                    yield
                    B.tt(Wn[:, 128:256], Wc[:, 128:256].f(), ps[:, 128:256], ALU.add)
                    if not lastk:
                        B.cp(Wn[:, 0:128], ps[:, 0:128], eng="dve")
                        B.cp(Pn, ps2[:, 0:128], eng="act")
                    Pc, Pn, Wc, Wn, PPc, PPn = Pn, Pc, Wn, Wc, PPn, PPc
                AT = Wc[:, 128:256].f()
                u_sb = V(uwb.ap[:, slot, 0:64], ["uw%d_0" % slot])
                w_sb = V(uwb.ap[:, slot, 64:128], ["uw%d_1" % slot])
                qSe = V(uwb.ap[:, slot, 128:192], ["uw%d_2" % slot])
                wkT = V(uwb.ap[:, slot, 192:320], ["uw%d_3" % slot])
                ps = B.nps()
                B.mm(ps[:, 0:64], AT, bv[:, h, :])
                B.mm(ps[po:po + 64, 128:256], kb[:, h, :], AT)
                yield
                B.cp(u_sb, ps[:, 0:64], eng="dve")
                B.cp(wkT[po:po + 64, :], ps[po:po + 64, 128:256], eng="dve")
                if NS == 1:
                    Sh = SAp[l][po:po + 64, hp, :]
                    ps = B.nps()
                    B.mm(ps[:, 0:64], wkT[po:po + 64, :], Sh)
                    B.mm(ps[:, 64:128], qT, Sh)
                    yield
                    B.tt(w_sb, u_sb, ps[:, 0:64], ALU.subtract)
                    B.ts(qSe, ps[:, 64:128], egc[:, h:h + 1], ALU.mult)
                else:
                    if h % 2 == 0:
                        for q in range(2):
                            B.dma(SAs[64 * q:64 * q + 64, :, :], sdl[l][:, h + q, :, :].re("s d v -> d s v"), group="sa")
                    ps = B.nps()
                    for s in range(16):
                        B.mm(ps[0:64, 8 * s:8 * s + 8], SAs[po:po + 64, s, :], wkT[po:po + 64, 8 * s:8 * s + 8])
                        B.mm(ps[0:64, 128 + 8 * s:128 + 8 * s + 8], SAs[po:po + 64, s, :], qT[:, 8 * s:8 * s + 8])
                    yield
                    cTa = hbuf()
                    cTb = hbuf()
                    B.cp(cTa[0:64, :], ps[0:64, 0:128], eng="act")
                    B.cp(cTb[0:64, :], ps[0:64, 128:256], eng="act")
                    ps = B.nps()
                    B.tr(ps[:, 0:64], cTa[0:64, :], ident[0:64, 0:64])
                    B.tr(ps[:, 64:128], cTb[0:64, :], ident[0:64, 0:64])
                    yield
                    B.tt(w_sb, u_sb, ps[:, 0:64], ALU.subtract)
                    B.act(qSe, ps[:, 64:128], AF.Copy, scale=egc[:, h:h + 1])
                ps = B.nps()
                B.mm(ps[:, 0:64], qkmT, w_sb)
                if NS == 1:
                    B.mm(ps[po:po + 64, 128:192], kdec[:, h, :], w_sb)
                    yield
                    B.tt(otm[:, h, :], qSe, ps[:, 0:64], ALU.add)
                    B.stt(Sh, Sh, glb[po:po + 64, h:h + 1], ps[po:po + 64, 128:192], ALU.mult, ALU.add)
                else:
                    yield
                    B.tt(otm[:, h, :], qSe, ps[:, 0:64], ALU.add)
                    Wexp = wxb
                    B.tt(Wexp, w_sb.bc(1, [128, 16, 64]), SEG.bc(2, [128, 16, 64]), ALU.mult)
                    for hf in range(2):
                        psU = B.nps()
                        B.mm(psU[po:po + 64, 0:512], kdec[:, h, :], Wexp[:, 8 * hf:8 * hf + 8, :].re("p s v -> p (s v)"))
                        Sv = SAs[po:po + 64, 8 * hf:8 * hf + 8, :]
                        B.tt(Sv, Sv, glb[po:po + 64, h, 8 * hf:8 * hf + 8].bc(2, [64, 8, 64]), ALU.mult)
                        B.tt(Sv, Sv, psU[po:po + 64, 0:512].re("p (s v) -> p s v", s=8), ALU.add)
                    if h % 2 == 1:
                        for q in range(2):
                            B.dma(o_sdl[l][:, h - 1 + q, :, :].re("s d v -> d s v"), SAs[64 * q:64 * q + 64, :, :], group="sa")

            G = cfg.get('g_prompt', 3) if NS == 1 else 2
            for h0 in range(0, 6, G):
                gens = [head_gen(h0 + i, i) for i in range(G) if h0 + i < 6]
                if h0 == 0 and pending_rms:
                    gens.append(pending_rms.pop())
                while gens:
                    for gen in list(gens):
                        try:
                            next(gen)
                        except StopIteration:
                            gens.remove(gen)
            if blk == 0:
                dump("oa_raw_%d" % l, otm.re("p h v -> p (h v)"))
            pending_rms.append(rms_gate_gen(tc, l, blk, otm, za, rb[l][:, 12:76], 0))
        for gen in pending_rms:
            for _ in gen:
                pass
        if tc.last:
            for h in range(6):
                hp, po = h // 2, (h % 2) * 64
                B.dma(o_pdl[l][h], SAp[l][po:po + 64, hp, :], group="pdl")

    def phase_B(tc, l):
        T, S, Tt = tc.T, tc.S, tc.Tt
        W = 3 + Tt

        def pre(cb):
            return pg(cb)[:, 0, 0:S * W].re("p (s w) -> p s w", s=S)

        if tc.samp:
            hist_from_state(slc[l].re("s r n -> (s r) n"), 3, 256, lambda b: pre(b)[:, :, 0:3])
            xi = xin[0]
            B.dma(xi[0:16, 0:256], slr[l])
            h0 = pgf(16, 1, 32).re("p (c s) -> p c s", c=2)
            for cb in range(2):
                ps = B.nps()
                B.tr(ps[:, 0:16], xi[0:16, 128 * cb:128 * cb + 128], ident[0:16, 0:16])
                evac(h0[:, cb, :], ps[:, 0:16])
            hl_s = pgf(17, 1, 32).re("p (c s) -> p c s", c=2)
        else:
            B.cp(pg(0, 2)[:, :, 0:3], histB[l], eng="dve")
        sv = fill(wi[l].cols(1152, 256), 256)
        if tc.state_out:
            pt = proj_tm(sv, 256, tc.NB - 1)
            state_rows_out(tc, pt, 256, 3, o_plc[l], o_slc[l], 0)
        for cb in range(2):
            ps = proj_fm(sv, cb, 128, T)
            evac(pre(cb)[:, :, 3:3 + Tt], ps[:, 0:T].re("p (s t) -> p s t", s=S))
        if not tc.samp:
            B.cp(histB[l], pg(0, 2)[:, :, 512:515], eng="dve")
        svg = fill(wi[l].cols(1408, 256), 256)
        for cb in range(2):
            xc = pg(2 + cb)[:, 0, 0:T]
            conv_fm(xc.re("p (s t) -> p s t", s=S), pre(cb), Tt,
                    [pf[l][:, PF_CB + 4 * cb + i:PF_CB + 4 * cb + i + 1] for i in range(4)], 4,
                    bias=pf[l][:, PF_CBB + cb:PF_CBB + cb + 1])
            rs = pg(4 + cb)[:, 0, 0:T]
            is_ = pg(6 + cb)[:, 0, 0:T]
            a = pg(8 + cb)[:, 0, 0:T]
            sq = pg(10 + cb)[:, 0, 0:T]
            hh = pg(12 + cb)[:, 0, 0:T]
            psr = B.nps()
            B.mm(psr[:, 0:T], lw[l][:, 128 * cb:128 * cb + 128], xc)
            B.act(rs, psr[:, 0:T], AF.Sigmoid, bias=pf[l][:, PF_LBR + cb:PF_LBR + cb + 1])
            psi = B.nps()
            B.mm(psi[:, 0:T], lw[l][:, 256 + 128 * cb:256 + 128 * cb + 128], xc)
            B.act(is_, psi[:, 0:T], AF.Sigmoid, bias=pf[l][:, PF_LBI + cb:PF_LBI + cb + 1])
            B.act(a, rs, AF.Exp, scale=lc12[l][:, cb:cb + 1])
            B.act(sq, rs, AF.Exp, scale=lc12[l][:, 2 + cb:3 + cb])
            B.act(sq, sq, AF.Sqrt, bias=1.0, scale=-1.0)
            B.tt(is_, is_, xc, ALU.mult)
            B.tt(is_, is_, sq, ALU.mult)
            if tc.samp:
                for s in range(16):
                    B.scan(hh[:, 8 * s:8 * s + 8], a[:, 8 * s:8 * s + 8], is_[:, 8 * s:8 * s + 8], h0[:, cb, s:s + 1])
                B.cp(hl_s[:, cb, :], hh.re("p (s t) -> p s t", t=8)[:, :, 7], eng="dve")
            else:
                B.scan(hh, a, is_, hlp[l][:, cb:cb + 1])
                B.cp(hlp[l][:, cb:cb + 1], hh[:, T - 1:T], eng="dve")
            if cb == 0:
                dump("h_lru_%d" % l, hh)
            psg = proj_fm(svg, cb, 128, T)
            gg = pg(14 + cb)[:, 0, 0:T]
            B.act(gg, psg[:, 0:T], AF.Gelu_apprx_tanh)
            B.tt(rpg(3 + cb)[:, 0, 0:T], hh, gg, ALU.mult)
        if tc.samp:
            for cb in range(2):
                ps = B.nps()
                B.tr(ps[0:16, 0:128], hl_s[:, cb, :], ident)
                evac(stg[0:16, 128 * cb:128 * cb + 128], ps[0:16, 0:128])
            B.dma(o_slr[l], stg[0:16, 0:256], group="so")
        elif tc.last:
            B.dma(o_plr[l].re("(c p) -> p c", p=128), hlp[l], group="plr", nc_ok=True)

    def phase_C(tc, l):
        T, NB, S, Tt, NS, mb = tc.T, tc.NB, tc.S, tc.Tt, tc.NS, tc.mb
        TRI = cst[:, mb + M_TRI:mb + M_TRI + 128]
        SEGM = cst[:, mb + M_SEGM:mb + M_SEGM + 128]
        SEG = cst[:, mb + M_SEG:mb + M_SEG + 16]
        qcT = [pg(0 + g)[:, 0, 0:T] for g in range(2)]
        kcT = [pg(2 + g)[:, 0, 0:T] for g in range(2)]
        lcT = pg(4)[0:32, 0, 0:T]
        vct = pgf(5, 3, NB * 384).re("p (b n) -> p b n", b=NB)
        zct = pgf(8, 3, NB * 384).re("p (b n) -> p b n", b=NB)
        kct = pgf(11, 2, NB * 192).re("p (b n) -> p b n", b=NB)
        sv = fill(wi[l].cols(1664, 256), 256)
        for g in range(2):
            ps = proj_fm(sv, g, 96, T)
            evac(qcT[g][0:96, :], ps[0:96, 0:T], scale=32.0 ** -0.5)
        sv = fill(wi[l].cols(1920, 256), 256)
        for g in range(2):
            ps = proj_fm(sv, g, 96, T)
            evac(kcT[g][0:96, :], ps[0:96, 0:T])
        sv = fill(wi[l].cols(2176, 16), 16)
        B.memset(lcT, 1.0)
        ps = proj_fm(sv, 0, 16, T)
        evac(lcT[0:16, :], ps[0:16, 0:T])
        for (c0, dsts) in ((2816, [(vct, 0, 0, 256)]), (4352, [(vct, 256, 0, 128), (zct, 256, 128, 128)]),
                           (3328, [(zct, 0, 0, 256)]), (3840, [(kct, 0, 0, 192)])):
            sv = fill(wi[l].cols(c0, 256), 256)
            for blk in range(NB):
                pt = proj_tm(sv, 256, blk)
                for (dst, d0, p0, n) in dsts:
                    evac(dst[:, blk, d0:d0 + n], pt[:, p0:p0 + n])
        SCs = sab
        pre_c = {}
        pending_rms_c = []

        def pre_gen_c(blk):
            tc0 = 128 * blk
            pA = pgf(20 + 3 * blk, 1, 384)
            pB = pgf(21 + 3 * blk, 1, 224)
            pC = pgf(22 + 3 * blk, 1, 512)
            logf = pA[:, 0:192]
            b_sb = pA[:, 192:384]
            kdc = pB[:, 0:192]
            ebl = pB[:, 192:224].re("p (g s) -> p g s", g=2)
            qt = pC[:, 0:256].re("p (g t) -> p g t", g=2)
            kt = pC[:, 256:512].re("p (g t) -> p g t", g=2)
            psl = B.nps()
            B.mm(psl[:, 0:192], lcT[:, tc0:tc0 + 128], w2[l])
            yield
            B.act(logf, psl[:, 0:192], AF.Exp, scale=-1.0)
            B.act(logf, logf, AF.Ln, bias=1.0)
            B.ts(logf, logf, -1.0 / 16, ALU.mult)
            if blk == 0:
                dump("logf_%d" % l, logf)
            psb = B.nps()
            B.mm(psb[:, 0:192], TRI, logf)
            B.mm(psb[:, 256:448], SEGM, logf)
            psT = B.nps()
            for g in range(2):
                B.mm(psT[0:96, 128 * g:128 * g + 128], logf[:, 96 * g:96 * g + 96], TRI)
                B.mm(psT[0:96, 256 + 16 * g:256 + 16 * g + 16], logf[:, 96 * g:96 * g + 96], SEG)
            yield
            B.cp(b_sb, psb[:, 0:192], eng="dve")
            B.tt(kdc, psb[:, 256:448], b_sb, ALU.subtract)
            B.act(kdc, kdc, AF.Exp)
            B.tt(kdc, kdc, kct[:, blk, :], ALU.mult)
            B.act(qt[0:96], psT[0:96, 0:256].re("p (g t) -> p g t", g=2), AF.Exp)
            B.act(kt[0:96], psT[0:96, 0:256].re("p (g t) -> p g t", g=2), AF.Exp, scale=-1.0)
            B.act(ebl[0:96], psT[0:96, 256:288].re("p (g s) -> p g s", g=2), AF.Exp)
            for g in range(2):
                B.tt(qt[0:96, g, :], qt[0:96, g, :], qcT[g][0:96, tc0:tc0 + 128], ALU.mult)
                B.tt(kt[0:96, g, :], kt[0:96, g, :], kcT[g][0:96, tc0:tc0 + 128], ALU.mult)
            pre_c[blk] = (kdc, ebl, qt, kt)

        for b0 in range(0, NB, 2):
            gens = [pre_gen_c(b0 + i) for i in range(2) if b0 + i < NB]
            while gens:
                for gen in list(gens):
                    try:
                        next(gen)
                    except StopIteration:
                        gens.remove(gen)
        for blk in range(NB):
            tc0 = 128 * blk
            kdc, ebl, qt, kt = pre_c[blk]
            otm = pgf(19 if blk % 2 == 0 else 15, 1, 384).re("p (h v) -> p h v", h=6)

            def head_gen_c(h):
                g, po = h // 3, (h % 3) * 32
                vh = vct[:, blk, 64 * h:64 * h + 64]
                ps = B.nps()
                B.mm(ps[:, 0:128], kt[po:po + 32, g, :], qt[po:po + 32, g, :])
                yield
                attm = hbuf()
                B.tt(attm, ps[:, 0:128], TRI, ALU.mult)
                if NS == 1:
                    Sh = SCp[l][po:po + 32, g, :]
                    ps = B.nps()
                    B.mm(ps[:, 0:64], qt[po:po + 32, g, :], Sh, start=True, stop=False)
                    B.mm(ps[:, 0:64], attm, vh, start=False, stop=True)
                    B.mm(ps[po:po + 32, 128:192], kdc[:, 32 * h:32 * h + 32], vh)
                    yield
                    B.cp(otm[:, h, :], ps[:, 0:64], eng="dve")
                    B.stt(Sh, Sh, ebl[po:po + 32, g, 0:1], ps[po:po + 32, 128:192], ALU.mult, ALU.add)
                else:
                    if h % 3 == 0:
                        for q in range(3):
                            B.dma(SCs[32 * q:32 * q + 32, :, :], sgl[l][:, h + q, :, :].re("s k v -> k s v"), group="sa")
                    ps = B.nps()
                    for s in range(16):
                        B.mm(ps[0:64, 8 * s:8 * s + 8], SCs[po:po + 32, s, :], qt[po:po + 32, g, 8 * s:8 * s + 8])
                    yield
                    cT = hbuf()
                    B.cp(cT[0:64, :], ps[0:64, 0:128], eng="act")
                    ps = B.nps()
                    B.tr(ps[:, 0:64], cT[0:64, :], ident[0:64, 0:64])
                    B.mm(ps[:, 64:128], attm, vh)
                    yield
                    qS = hbuf()[:, 0:64]
                    B.cp(qS, ps[:, 0:64], eng="act")
                    B.tt(otm[:, h, :], qS, ps[:, 64:128], ALU.add)
                    Vexp = wxb
                    B.tt(Vexp, vh.bc(1, [128, 16, 64]), SEG.bc(2, [128, 16, 64]), ALU.mult)
                    for hf in range(2):
                        psU = B.nps()
                        B.mm(psU[po:po + 32, 0:512], kdc[:, 32 * h:32 * h + 32], Vexp[:, 8 * hf:8 * hf + 8, :].re("p s v -> p (s v)"))
                        Sv = SCs[po:po + 32, 8 * hf:8 * hf + 8, :]
                        B.tt(Sv, Sv, ebl[po:po + 32, g, 8 * hf:8 * hf + 8].bc(2, [32, 8, 64]), ALU.mult)
                        B.tt(Sv, Sv, psU[po:po + 32, 0:512].re("p (s v) -> p s v", s=8), ALU.add)
                    if h % 3 == 2:
                        for q in range(3):
                            B.dma(o_sgl[l][:, h - 2 + q, :, :].re("s k v -> k s v"), SCs[32 * q:32 * q + 32, :, :], group="sa")

            GC = cfg.get("g_c", 3)
            for h0 in range(0, 6, GC):
                gens = [head_gen_c(h0 + i) for i in range(GC) if h0 + i < 6]
                if h0 == 0 and pending_rms_c:
                    gens.append(pending_rms_c.pop())
                while gens:
                    for gen in list(gens):
                        try:
                            next(gen)
                        except StopIteration:
                            gens.remove(gen)
            if blk == 0:
                dump("oc_raw_%d" % l, otm.re("p h v -> p (h v)"))
            pending_rms_c.append(rms_gate_gen(tc, l, blk, otm, zct[:, blk, :], rb[l][:, 76:140], 5, p_sq=13, p_sz=14))
        for gen in pending_rms_c:
            for _ in gen:
                pass
        if tc.last:
            for h in range(6):
                g, po = h // 3, (h % 3) * 32
                B.dma(o_pgl[l][h], SCp[l][po:po + 32, g, :], group="pgl")

    def layer_norm(tc, l, gcol, bcol):
        T = tc.T
        psS = B.nps()
        psQ = B.nps()
        for m in range(8):
            y = pg(m)[:, 0, 0:T]
            ysq = rpg(8 + m % 2)[:, 0, 0:T]
            yr = rpg(10 + m % 2)[:, 0, 0:T]
            B.act(ysq, y, AF.Square)
            B.cp(yr, y, eng="act")
            B.mm(psS[:, 0:T], onesR, yr, start=(m == 0), stop=(m == 7))
            B.mm(psQ[:, 0:T], onesR, ysq, start=(m == 0), stop=(m == 7))
        mean = pg(13)[:, 0, 0:T]
        rstd = pg(14)[:, 0, 0:T]
        msq = pg(15)[:, 0, 0:T]
        B.act(msq, psS[:, 0:T], AF.Square, scale=1.0 / D)
        B.stt(rstd, psQ[:, 0:T], 1.0 / D, msq, ALU.mult, ALU.subtract)
        B.act(rstd, rstd, AF.Sqrt, bias=EPS)
        B.recip(rstd, rstd)
        for m in range(8):
            y = pg(m)[:, 0, 0:T]
            le = "pool" if (m % 2 == 1 and cfg.get("ln_pool", 0)) else "dve"
            B.stt(y, psS[:, 0:T], -1.0 / D, y, ALU.mult, ALU.add)
            B.tt(y, y, rstd, ALU.mult, eng=le)
            B.act(xTr[:, m, 0:T], y, AF.Identity, bias=pf[l][:, bcol + m:bcol + m + 1], scale=pf[l][:, gcol + m:gcol + m + 1])

    def wout_ln1(tc, l):
        T = tc.T
        hd = rpg(0, 8)
        m = 0
        for (c0, ncol) in ((0, 256), (256, 256), (512, 256), (768, 256)):
            sv = fill(wo[l].cols(c0, ncol), ncol)
            for j in range(ncol // 128):
                ps = proj_fm(sv, j, 128, T, rhsT=hd)
                B.stt(pg(m)[:, 0, 0:T], xT[:, m, 0:T], float(ALPHA), ps[:, 0:T], ALU.mult, ALU.add)
                m += 1
        for gb in range(2):
            prefetch(("U", l, gb), wu[l].cols(256 * gb, 256), 256)
        layer_norm(tc, l, PF_L1G, PF_L1B)

    def ffn_ln2(tc, l):
        T, S, Tt = tc.T, tc.S, tc.Tt
        W = 2 + Tt
        hT = rpg(0, 22)
        if tc.samp:
            hs = pgf(16, 2, 22 * 32).re("p (b s r) -> p b s r", b=22, s=16)
            hist_from_state(sfc[l].re("s r n -> (s r) n"), 2, DFF, lambda b: hs[:, b, :, :])
        for gb in range(22):
            sv = fill(wu[l].cols(256 * gb, 256), 256, key=("U", l, gb))
            gpre = pg(8 + gb % 3)[:, 0, 0:S * W].re("p (s w) -> p s w", s=S)
            if tc.samp:
                B.cp(gpre[:, :, 0:2], hs[:, gb, :, :], eng="act")
            else:
                B.cp(gpre[:, 0, 0:2], histF[l][:, gb, :], eng="act")
            psg = proj_fm(sv, 0, 128, T)
            B.cp(gpre[:, :, 2:2 + Tt], psg[:, 0:T].re("p (s t) -> p s t", s=S), eng="act")
            psv = proj_fm(sv, 1, 128, T)
            if not tc.samp:
                B.cp(histF[l][:, gb, :], gpre[:, 0, Tt:Tt + 2], eng="act")
            gcv = pg(11 + gb % 2)[:, 0, 0:T]
            conv_fm(gcv.re("p (s t) -> p s t", s=S), gpre, Tt,
                    [pf[l][:, PF_FCW + 3 * gb + i:PF_FCW + 3 * gb + i + 1] for i in range(3)], 3)
            B.act(gcv, gcv, AF.Gelu_apprx_tanh, bias=pf[l][:, PF_FCB + gb:PF_FCB + gb + 1])
            B.tt(hT[:, gb, 0:T], gcv, psv[:, 0:T], ALU.mult)
            if tc.state_out:
                pt = proj_tm(sv, 256, tc.NB - 1)
                state_rows_out(tc, pt, 128, 2, o_pfc[l], o_sfc[l], 128 * gb)
        for m in range(8):
            ps = B.nps()
            for a in range(2):
                s_ = slots[slot_i[0] % NSLOT]
                slot_i[0] += 1
                B.dma(s_[:, 0:1408], wd[l][m][:, 1408 * a:1408 * a + 1408], eng="pool", group="w:" + s_.k[0])
                sv = s_[:, 0:1408].re("p (c n) -> p c n", c=11)
                TN = max(T, 256)
                for c in range(11):
                    B.mm(ps[:, 0:TN], sv[:, c, :], hT[:, 11 * a + c, 0:TN], start=(a == 0 and c == 0), stop=(a == 1 and c == 10))
            B.stt(pg(m)[:, 0, 0:T], xT[:, m, 0:T], float(ALPHA), ps[:, 0:T], ALU.mult, ALU.add)
        nxt = next_layer.get((tc.kind, tc.ti, l))
        if nxt is not None:
            for si in range(2):
                prefetch(("A", nxt, si), wi[nxt].cols(256 * si, 256), 256)
        layer_norm(tc, l, PF_L2G, PF_L2B)

    stages = cfg.get("stages", "ABCWF")
    next_layer = {}
    seq = [(k, t, l) for (k, t) in tiles for l in range(L)]
    if "A" in stages:
        for a, b in zip(seq[:-1], seq[1:]):
            next_layer[a] = b[2]
    for (kind, ti) in tiles:
        tc = TileCtx(kind, ti, NPT)
        load_x(tc)
        for l in range(L):
            if "A" in stages:
                phase_A(tc, l)
            if "B" in stages:
                phase_B(tc, l)
            if "C" in stages:
                phase_C(tc, l)
            if "W" in stages:
                dump("heads_%d" % l, rpg(0, 8).f()[:, :, 0:tc.T])
                wout_ln1(tc, l)
                dump("x1_%d" % l, xT[:, :, 0:tc.T])
            if "F" in stages:
                ffn_ln2(tc, l)
                dump("x2_%d" % l, xT[:, :, 0:tc.T])
        store_y(tc)
    with B.stack:
        stats = B.P.build()
    return nc, stats


_W_NAMES = ["w_in", "conv_a_w", "a_log", "dt_bias", "norm_a_w", "conv_b_w", "conv_b_b", "lru_w_r", "lru_b_r", "lru_w_i", "lru_b_i",
            "lru_lambda", "gla_w2", "gla_b2", "norm_c_w", "w_out", "ln1_g", "ln1_b", "ffn_w_up", "ffn_conv_w", "ffn_conv_b",
            "ffn_w_down", "ln2_g", "ln2_b"]


def make_in_maps(inputs, cores):
    w = {k: np.asarray(inputs[k], np.float32) for k in _W_NAMES}
    pk = pack_weights(w)
    maps = []
    for c in cores:
        m = dict(pk)
        m["xp"] = np.ascontiguousarray(inputs["x_prompt"][c])
        m["xs"] = np.ascontiguousarray(inputs["x_sample"][16 * c:16 * c + 16].reshape(128, D))
        for nm, key in (("sdc", "state_delta_conv"), ("sdl", "state_delta"), ("slc", "state_lru_conv"), ("slr", "state_lru"),
                        ("sgl", "state_gla"), ("sfc", "state_ffn_conv")):
            m[nm] = np.ascontiguousarray(inputs[key][:, 16 * c:16 * c + 16])
        maps.append(m)
    return maps


def kernel(**inputs):
    inputs = {k: np.asarray(v) for k, v in inputs.items()}
    nc, stats = build_program({})
    maps = make_in_maps(inputs, list(range(NCORES)))
    res = run_bass_kernel_spmd(nc, maps, core_ids=list(range(NCORES)))
    r = res.results
    y_prompt = np.stack([r[c]["yp"] for c in range(NCORES)]).reshape(8, 2048, D)
    y_sample = np.concatenate([r[c]["ys"].reshape(16, 8, D) for c in range(NCORES)], axis=0)
    outs = [y_prompt, y_sample]
    for nm in ["o_pdc", "o_pdl", "o_plc", "o_plr", "o_pgl", "o_pfc"]:
        outs.append(np.stack([r[c][nm] for c in range(NCORES)], axis=1))
    for nm in ["o_sdc", "o_sdl", "o_slc", "o_slr", "o_sgl", "o_sfc"]:
        outs.append(np.concatenate([r[c][nm] for c in range(NCORES)], axis=1))
    return tuple(np.ascontiguousarray(o, dtype=np.float32) for o in outs)
```

```python
import bisect
import contextlib
import numpy as np
import concourse.bass as bass
import concourse.mybir as mybir
from concourse.bass_utils import run_bass_kernel_spmd

F32 = mybir.dt.float32
F32R = mybir.dt.float32r
AF = mybir.ActivationFunctionType
ALU = mybir.AluOpType
AX = mybir.AxisListType

D = 1024
DFF = 2816
ALPHA = 4.0 ** 0.25
EPS = 1e-6
NCORES = 8
EPOCH = 8192
STRICT_SAME_ENGINE = True
ENGS = ("pe", "dve", "act", "pool", "sp")


class Prog:
    def __init__(self, nc):
        self.nc = nc
        self.ops = []

    def op(self, eng, fn, reads=(), writes=(), group=None):
        self.ops.append((eng, fn, tuple(reads), tuple(writes), group))

    def build(self):
        nc = self.nc
        ops = self.ops
        n = len(ops)
        last_w = {}
        readers = {}
        deps = [None] * n
        for i, (eng, fn, rd, wr, grp) in enumerate(ops):
            d = set()
            for k in rd:
                j = last_w.get(k)
                if j is not None:
                    d.add((j, True))
                if k.startswith("ps"):
                    for r in readers.get(k, ()):
                        if ops[r][0] != eng:
                            d.add((r, False))
            for k in wr:
                j = last_w.get(k)
                if j is not None:
                    d.add((j, False))
                for r in readers.get(k, ()):
                    d.add((r, False))
            deps[i] = d
            for k in rd:
                readers.setdefault(k, []).append(i)
            for k in wr:
                last_w[k] = i
                readers[k] = []
        has_consumer = [False] * n
        fdeps = [None] * n
        for i, (eng, fn, rd, wr, grp) in enumerate(ops):
            raw = {j for (j, r) in deps[i] if r}
            nd = set()
            for (j, r) in deps[i]:
                if j == i:
                    continue
                pe, _, _, _, pg = ops[j]
                if pe == eng and pg is None:
                    if eng == "pe":
                        continue
                    if j not in raw and not STRICT_SAME_ENGINE:
                        continue
                nd.add(j)
            fdeps[i] = nd
            for j in nd:
                has_consumer[j] = True
        eng_cnt = {e: 0 for e in ENGS}
        grp_ops = {}
        sig = [None] * n
        for i, (eng, fn, rd, wr, grp) in enumerate(ops):
            if grp is not None:
                grp_ops.setdefault(grp, []).append(i)
                sig[i] = ("g", grp, 0)
            elif has_consumer[i]:
                c = eng_cnt[eng]
                eng_cnt[eng] = c + 1
                sig[i] = ("e", (eng, c // EPOCH), (c % EPOCH) + 1)
        sem_keys = []
        seen_keys = set()
        for s in sig:
            if s is not None and (s[0], s[1]) not in seen_keys:
                seen_keys.add((s[0], s[1]))
                sem_keys.append((s[0], s[1]))
        stack = contextlib.ExitStack()
        sems = {}
        for num, k in enumerate(sem_keys):
            sems[k] = stack.enter_context(nc.semaphore("s%d" % num))
        per_eng = {e: [] for e in ENGS}
        for i, o in enumerate(ops):
            per_eng[o[0]].append(i)
        group_owner = {}
        for i, o in enumerate(ops):
            if o[4] is not None:
                group_owner.setdefault(o[4], o[0])
        self.stats = dict(n_ops=n, n_sems=len(sem_keys), per_eng={e: len(v) for e, v in per_eng.items()})

        def emit_engine(eng_name, eobj):
            seen = {}
            for i in per_eng[eng_name]:
                eng, fn, rd, wr, grp = ops[i]
                need = {}
                for j in fdeps[i]:
                    kind, key, val = sig[j]
                    if kind == "g":
                        val = 16 * bisect.bisect_left(grp_ops[key], i)
                    kk = (kind, key)
                    if val > need.get(kk, 0):
                        need[kk] = val
                for kk, val in need.items():
                    if seen.get(kk, 0) >= val:
                        continue
                    seen[kk] = val
                    eobj.wait_ge(sems[kk], val)
                inst = fn(eobj)
                s = sig[i]
                if s is not None:
                    inst.then_inc(sems[(s[0], s[1])], 16 if s[0] == "g" else 1)
            for g, lst in grp_ops.items():
                if group_owner[g] == eng_name:
                    eobj.wait_ge(sems[("g", g)], 16 * len(lst))

        with stack:
            with nc.Block() as block:
                @block.tensor
                def _(e):
                    emit_engine("pe", e)

                @block.vector
                def _(e):
                    emit_engine("dve", e)

                @block.scalar
                def _(e):
                    emit_engine("act", e)

                @block.gpsimd
                def _(e):
                    emit_engine("pool", e)

                @block.sync
                def _(e):
                    emit_engine("sp", e)
        return self.stats


class V:
    def __init__(self, ap, keys):
        self.ap = ap
        self.k = tuple(keys)

    def __getitem__(self, idx):
        return V(self.ap[idx], self.k)

    def re(self, pat, **kw):
        return V(self.ap.rearrange(pat, **kw), self.k)

    def bc(self, axis, shape):
        return V(self.ap.unsqueeze(axis).to_broadcast(list(shape)), self.k)

    def kk(self, *keys):
        return V(self.ap, keys)

    def r(self):
        return V(self.ap.bitcast(F32R), self.k)

    def f(self):
        return V(self.ap.bitcast(F32), self.k)


def _ks(*vs):
    out = []
    for v in vs:
        if isinstance(v, V):
            out.extend(v.k)
    return out


def _a(v):
    return v.ap if isinstance(v, V) else v


class Builder:
    def __init__(self, nc):
        self.nc = nc
        self.P = Prog(nc)
        self.stack = contextlib.ExitStack()
        self.psn = 0
        self.ps = []
        self.uid = 0

    def sb(self, name, shape, dt=F32, key=None):
        t = self.stack.enter_context(self.nc.sbuf_tensor("sb_" + name, list(shape), dt))
        return V(t[:], [key or name])

    def init_psum(self):
        for i in range(8):
            t = self.stack.enter_context(self.nc.psum_tensor("ps%d" % i, [128, 512], F32))
            self.ps.append(V(t[:], ["ps%d" % i]))

    def nps(self):
        v = self.ps[self.psn % 8]
        self.psn += 1
        return v

    def mm(self, out, lhsT, rhs, start=True, stop=True):
        rd = _ks(lhsT, rhs) + ([] if start else _ks(out))
        self.P.op("pe", lambda e: e.matmul(out.ap, lhsT=lhsT.ap, rhs=rhs.ap, start=start, stop=stop), rd, _ks(out))

    def tr(self, out, in_, ident):
        self.P.op("pe", lambda e: e.transpose(out.ap, in_.ap, ident.ap), _ks(in_, ident), _ks(out))

    def act(self, out, in_, func, bias=None, scale=None, eng="act"):
        kw = {}
        if bias is not None:
            kw["bias"] = _a(bias)
        if scale is not None:
            kw["scale"] = _a(scale)
        self.P.op("act", lambda e: e.activation(out=out.ap, in_=in_.ap, func=func, **kw), _ks(in_, bias, scale), _ks(out))

    def tt(self, out, a, b, op, eng="dve"):
        self.P.op(eng, lambda e: e.tensor_tensor(out=out.ap, in0=a.ap, in1=b.ap, op=op), _ks(a, b), _ks(out))

    def ts(self, out, a, s1, op0, s2=None, op1=None, eng="dve"):
        if op1 is None:
            self.P.op(eng, lambda e: e.tensor_scalar(out=out.ap, in0=a.ap, scalar1=_a(s1), scalar2=None, op0=op0), _ks(a, s1), _ks(out))
        else:
            self.P.op(eng, lambda e: e.tensor_scalar(out=out.ap, in0=a.ap, scalar1=_a(s1), scalar2=_a(s2), op0=op0, op1=op1),
                      _ks(a, s1, s2), _ks(out))

    def stt(self, out, a, s, b, op0, op1):
        self.P.op("dve", lambda e: e.scalar_tensor_tensor(out=out.ap, in0=a.ap, scalar=_a(s), in1=b.ap, op0=op0, op1=op1),
                  _ks(a, s, b), _ks(out))

    def cp(self, out, in_, eng="dve"):
        if eng == "act":
            self.P.op("act", lambda e: e.activation(out=out.ap, in_=in_.ap, func=AF.Copy), _ks(in_), _ks(out))
        else:
            self.P.op(eng, lambda e: e.tensor_copy(out=out.ap, in_=in_.ap), _ks(in_), _ks(out))

    def red(self, out, in_, op=ALU.add):
        self.P.op("dve", lambda e: e.tensor_reduce(out=out.ap, in_=in_.ap, axis=AX.X, op=op), _ks(in_), _ks(out))

    def recip(self, out, in_):
        self.P.op("dve", lambda e: e.reciprocal(out=out.ap, in_=in_.ap), _ks(in_), _ks(out))

    def scan(self, out, d0, d1, init):
        self.P.op("dve", lambda e: e.tensor_tensor_scan(out=out.ap, data0=d0.ap, data1=d1.ap, initial=_a(init), op0=ALU.mult, op1=ALU.add),
                  _ks(d0, d1, init), _ks(out))

    def memset(self, out, val, eng="dve"):
        self.P.op(eng, lambda e: e.memset(out.ap, val), [], _ks(out))

    def dma(self, out, in_, eng="sp", group=None, nc_ok=False):
        self.uid += 1
        g = group or ("d%d" % self.uid)
        if nc_ok:
            self.P.op(eng, lambda e: e.dma_start(out=out.ap, in_=in_.ap, allow_slow_non_contiguous=True), _ks(in_), _ks(out), group=g)
        else:
            self.P.op(eng, lambda e: e.dma_start(out=out.ap, in_=in_.ap), _ks(in_), _ks(out), group=g)


NWI = 18 * 128 + 4 * 512 + 256
FM_SRC = [(128 * b, 128) for b in range(9)] + [(1548, 128), (1676, 128), (1804, 128), (1932, 128),
                                                (2060, 96), (2156, 96), (2252, 96), (2348, 96), (3212, 16)]
TM_SRC = [(1152, 396), (2444, 384), (2828, 384), (2252, 192)]
NPF = 172
PF_CA, PF_CB, PF_CBB, PF_LBR, PF_LBI, PF_LAM, PF_L1G, PF_L1B, PF_L2G, PF_L2B, PF_FCW, PF_FCB = 0, 36, 44, 46, 48, 50, 52, 60, 68, 76, 84, 150
NRB = 140
NEG = -30000.0


def _mask_set(c):
    idx = np.arange(128)
    seg = idx // c
    same = seg[:, None] == seg[None, :]
    ns = 128 // c
    tri = (same & (idx[:, None] <= idx[None, :])).astype(np.float32)
    segm = same.astype(np.float32)
    neg_strit = np.where(same & (idx[None, :] < idx[:, None]), 0.0, NEG).astype(np.float32)
    neg_tri = np.where(same & (idx[:, None] <= idx[None, :]), 0.0, NEG).astype(np.float32)
    seg01 = np.zeros((128, 16), np.float32)
    seg01[idx, seg] = 1.0
    last01 = np.zeros((128, 16), np.float32)
    li = (idx % c) == (c - 1)
    last01[idx[li], seg[li]] = 1.0
    return np.concatenate([tri, segm, neg_strit, neg_tri, seg01, last01], axis=1)


C_ID, C_ONE, C_B64, C_MP, C_MS = 0, 128, 256, 384, 384 + 544
NCONST = 384 + 2 * 544
M_TRI, M_SEGM, M_NSTRIT, M_NTRI, M_SEG, M_LAST = 0, 128, 256, 384, 512, 528


WI_PANELS = [(0, 256), (256, 256), (512, 256), (768, 256), (1024, 128), (2304, 256), (2560, 256),
             (1152, 256), (1408, 256), (1664, 256), (1920, 256), (2176, 16),
             (2816, 256), (4352, 256), (3328, 256), (3840, 256)]
WO_PANELS = [(0, 256), (256, 256), (512, 256), (768, 256)]
WU_PANELS = [(256 * gb, 256) for gb in range(22)]


def _panel_offsets(panels):
    offs, o = {}, 0
    for (c0, n) in panels:
        offs[(c0, n)] = o
        o += 8 * n
    return offs, o


WI_OFF, WI_TOT = _panel_offsets(WI_PANELS)
WO_OFF, WO_TOT = _panel_offsets(WO_PANELS)
WU_OFF, WU_TOT = _panel_offsets(WU_PANELS)


def _pack_panels(w, panels, tot):
    L = w.shape[0]
    out = np.zeros((L, 128, tot), np.float32)
    o = 0
    for (c0, n) in panels:
        blk = w[:, :, c0:c0 + n].reshape(L, 8, 128, n).transpose(0, 2, 1, 3).reshape(L, 128, 8 * n)
        out[:, :, o:o + 8 * n] = blk
        o += 8 * n
    return out


def make_consts():
    ident = np.eye(128, dtype=np.float32)
    ones = np.ones((128, 128), np.float32)
    b64 = np.zeros((128, 128), np.float32)
    b64[:64, :64] = 1.0
    b64[64:, 64:] = 1.0
    return np.ascontiguousarray(np.concatenate([ident, ones, b64, _mask_set(128), _mask_set(8)], axis=1))


def pack_weights(w):
    L = 2
    wi = np.zeros((L, D, NWI), np.float32)
    for b, (s, n) in enumerate(FM_SRC):
        wi[:, :, 128 * b:128 * b + n] = w["w_in"][:, :, s:s + n]
    for g, (s, n) in enumerate(TM_SRC):
        wi[:, :, 2304 + 512 * g:2304 + 512 * g + n] = w["w_in"][:, :, s:s + n]
    wi[:, :, 4352:4480] = w["w_in"][:, :, 2444 + 256:2444 + 384]
    wi[:, :, 4480:4608] = w["w_in"][:, :, 2828 + 256:2828 + 384]
    wu = np.zeros((L, D, 2 * DFF), np.float32)
    for gb in range(22):
        wu[:, :, 256 * gb:256 * gb + 128] = w["ffn_w_up"][:, :, 128 * gb:128 * gb + 128]
        wu[:, :, 256 * gb + 128:256 * gb + 256] = w["ffn_w_up"][:, :, DFF + 128 * gb:DFF + 128 * gb + 128]
    wd = np.ascontiguousarray(w["ffn_w_down"].reshape(L, 22, 128, 8, 128).transpose(0, 3, 2, 1, 4)).reshape(L, 8, 128, 22 * 128)
    pf = np.zeros((L, 128, NPF), np.float32)

    def pc(a):
        return a.reshape(L, -1, 128).transpose(0, 2, 1)
    for i in range(4):
        pf[:, :, PF_CA + i:PF_CA + 36:4] = pc(w["conv_a_w"][:, i])
        pf[:, :, PF_CB + i:PF_CB + 8:4] = pc(w["conv_b_w"][:, i])
    pf[:, :, PF_CBB:PF_CBB + 2] = pc(w["conv_b_b"])
    pf[:, :, PF_LBR:PF_LBR + 2] = pc(w["lru_b_r"])
    pf[:, :, PF_LBI:PF_LBI + 2] = pc(w["lru_b_i"])
    pf[:, :, PF_LAM:PF_LAM + 2] = pc(w["lru_lambda"])
    pf[:, :, PF_L1G:PF_L1G + 8] = pc(w["ln1_g"])
    pf[:, :, PF_L1B:PF_L1B + 8] = pc(w["ln1_b"])
    pf[:, :, PF_L2G:PF_L2G + 8] = pc(w["ln2_g"])
    pf[:, :, PF_L2B:PF_L2B + 8] = pc(w["ln2_b"])
    for i in range(3):
        pf[:, :, PF_FCW + i:PF_FCW + 66:3] = pc(w["ffn_conv_w"][:, i])
    pf[:, :, PF_FCB:PF_FCB + 22] = pc(w["ffn_conv_b"])
    rb = np.zeros((L, 128, NRB), np.float32)
    rb[:, :, 0:6] = w["a_log"][:, None, :]
    rb[:, :, 6:12] = w["dt_bias"][:, None, :]
    rb[:, :, 12:76] = w["norm_a_w"][:, None, :]
    rb[:, :, 76:140] = w["norm_c_w"][:, None, :]
    lw = np.zeros((L, 128, 4, 128), np.float32)
    for gi, nm in enumerate(["lru_w_r", "lru_w_i"]):
        for blk in range(2):
            for q in range(2):
                lw[:, 64 * q:64 * q + 64, 2 * gi + blk, 64 * q:64 * q + 64] = w[nm][:, 2 * blk + q]
    w2 = np.zeros((L, 32, 192), np.float32)
    w2[:, 0:16] = w["gla_w2"]
    w2[:, 16] = w["gla_b2"]
    return dict(wi=_pack_panels(wi, WI_PANELS, WI_TOT), wo=_pack_panels(np.ascontiguousarray(w["w_out"]), WO_PANELS, WO_TOT),
                wu=_pack_panels(wu, WU_PANELS, WU_TOT), wd=wd, pf=pf, rb=rb,
                lw=np.ascontiguousarray(lw.reshape(L, 128, 512)), w2=w2, cst=make_consts())


PW = 516
NPAGES = 32
NSLOT = 3
SLOTW = 2048


class TileCtx:
    def __init__(self, kind, ti=0, nprompt=4):
        self.kind = kind
        self.ti = ti
        self.samp = kind == "s"
        self.T = 128 if self.samp else 512
        self.NB = self.T // 128
        self.S = 16 if self.samp else 1
        self.Tt = 8 if self.samp else 512
        self.NS = 16 if self.samp else 1
        self.K = 2 if self.samp else 6
        self.mb = C_MS if self.samp else C_MP
        self.first = (not self.samp) and ti == 0
        self.last = (not self.samp) and ti == nprompt - 1
        self.state_out = self.samp or self.last


def build_program(cfg):
    nc = bass.Bass("TRN2", target_bir_lowering=False)
    B = Builder(nc)
    L = cfg.get("depth", 2)
    NPT = cfg.get("nprompt", 4)
    tiles = cfg.get("tiles", [("p", i) for i in range(NPT)] + [("s", 0)])
    dbg = cfg.get("dbg", {})

    def din(name, shape):
        return V(nc.dram_tensor(name, list(shape), F32, kind="ExternalInput").ap(), ())

    def dout(name, shape):
        return V(nc.dram_tensor(name, list(shape), F32, kind="ExternalOutput").ap(), ())

    xp = din("xp", [2048, D]); xs = din("xs", [128, D])
    sdc = din("sdc", [2, 16, 3, 1152]); sdl = din("sdl", [2, 16, 6, 64, 64]); slc = din("slc", [2, 16, 3, 256])
    slr = din("slr", [2, 16, 256]); sgl = din("sgl", [2, 16, 6, 32, 64]); sfc = din("sfc", [2, 16, 2, DFF])
    wi_p = din("wi", [2, 128, WI_TOT]); wo_p = din("wo", [2, 128, WO_TOT]); wu_p = din("wu", [2, 128, WU_TOT]); wd = din("wd", [2, 8, 128, DFF])

    class _Panels:
        def __init__(self, packed, offs):
            self.packed, self.offs, self.l = packed, offs, None

        def __getitem__(self, l):
            p = _Panels(self.packed, self.offs)
            p.l = l
            return p

        def cols(self, c0, n):
            o = self.offs[(c0, n)]
            return self.packed[self.l][:, o:o + 8 * n]
    wi = _Panels(wi_p, WI_OFF); wo = _Panels(wo_p, WO_OFF); wu = _Panels(wu_p, WU_OFF)
    pfd = din("pf", [2, 128, NPF]); rbd = din("rb", [2, 128, NRB]); lwd = din("lw", [2, 128, 512]); w2d = din("w2", [2, 32, 192])
    cstd = din("cst", [128, NCONST])
    yp = dout("yp", [2048, D]); ys = dout("ys", [128, D])
    o_pdc = dout("o_pdc", [2, 3, 1152]); o_pdl = dout("o_pdl", [2, 6, 64, 64]); o_plc = dout("o_plc", [2, 3, 256])
    o_plr = dout("o_plr", [2, 256]); o_pgl = dout("o_pgl", [2, 6, 32, 64]); o_pfc = dout("o_pfc", [2, 2, DFF])
    o_sdc = dout("o_sdc", [2, 16, 3, 1152]); o_sdl = dout("o_sdl", [2, 16, 6, 64, 64]); o_slc = dout("o_slc", [2, 16, 3, 256])
    o_slr = dout("o_slr", [2, 16, 256]); o_sgl = dout("o_sgl", [2, 16, 6, 32, 64]); o_sfc = dout("o_sfc", [2, 16, 2, DFF])
    dbg_out = {k: dout("dbg_" + k, shp) for k, shp in dbg.items()}

    B.init_psum()
    xT = B.sb("xT", [128, 8, 512])
    slots = [B.sb("slot%d" % i, [128, SLOTW], F32R) for i in range(NSLOT)]
    xin = [B.sb("xin%d" % i, [128, 1024]) for i in range(2)]
    stg = B.sb("stg", [128, 256])
    cst = B.sb("cst", [128, NCONST])
    cstR = B.sb("cstR", [128, 256], F32R)
    pf = [B.sb("pf%d" % l, [128, NPF]) for l in range(2)]
    rb = [B.sb("rb%d" % l, [128, NRB]) for l in range(2)]
    lw = [B.sb("lw%d" % l, [128, 512]) for l in range(2)]
    w2 = [B.sb("w2_%d" % l, [32, 192]) for l in range(2)]
    nea = [B.sb("nea%d" % l, [128, 6]) for l in range(2)]
    lc12 = [B.sb("lc12_%d" % l, [128, 4]) for l in range(2)]
    histA = [B.sb("histA%d" % l, [128, 9, 3]) for l in range(2)]
    histB = [B.sb("histB%d" % l, [128, 2, 3]) for l in range(2)]
    histF = [B.sb("histF%d" % l, [128, 22, 2]) for l in range(2)]
    SAp = [B.sb("SAp%d" % l, [128, 3, 64]) for l in range(2)]
    SCp = [B.sb("SCp%d" % l, [128, 2, 64]) for l in range(2)]
    hlp = [B.sb("hlp%d" % l, [128, 2]) for l in range(2)]
    small = B.sb("small", [128, 256])
    HG = 3
    SOLVE_R = cfg.get("solve_r", False)
    SDT = F32R if SOLVE_R else F32
    hbtR = B.sb("hbtR", [128, 5 * HG, 128], SDT)
    hbt2R = B.sb("hbt2R", [128, 2 * HG, 256], SDT)
    hbt = B.sb("hbt", [128, 4, 128])
    uwb = B.sb("uwb", [128, HG, 320])
    otm_x = B.sb("otm_x", [128, 3, 128])
    gat = B.sb("gat", [128, 12, 24])
    sab = B.sb("sab", [128, 16, 64])
    wxb = B.sb("wxb", [128, 16, 64])
    B.memset(hbt[:, 0, :].kk("hbs0"), 0.0)
    for sl_ in range(HG):
        B.cp(V(hbtR.ap[:, 5 * sl_ + 4, :], ["hbpad%d" % sl_]), hbt[:, 0, :].kk("hbs0"), eng="dve")
    arena_t = B.stack.enter_context(nc.sbuf_tensor("arena", [128, NPAGES, PW], F32))
    arenaR_t = B.stack.enter_context(nc.sbuf_tensor("arenaR", [128, 22, 512], F32R))

    def pg(p0, n=1):
        return V(arena_t[:, p0:p0 + n, :], ["ar%d" % p for p in range(p0, p0 + n)])

    def pgf(p0, n, width):
        assert width <= n * PW
        return V(arena_t[:, p0:p0 + n, :].rearrange("p a b -> p (a b)")[:, 0:width], ["ar%d" % p for p in range(p0, p0 + n)])

    def rpg(p0, n=1):
        return V(arenaR_t[:, p0:p0 + n, :], ["rp%d" % p for p in range(p0, p0 + n)])

    ident = cst[:, C_ID:C_ID + 128]
    ones = cst[:, C_ONE:C_ONE + 128]
    onesR = cstR[:, 0:128]
    b64R = cstR[:, 128:256]

    scnt = [0]

    def sm(n, tag=""):
        o = scnt[0] % 16
        scnt[0] += 1
        return V(small.ap[:, 16 * o:16 * o + n], ["sm%d" % o])

    hcnt = {"a": 0, "b": 0}

    def hbuf(tag=""):
        i = hcnt["a"] % 4
        hcnt["a"] += 1
        return V(hbt.ap[:, i, :], ["hbs%d" % i])

    def hbuf2(tag=""):
        raise RuntimeError("unused")

    B.dma(cst, cstd, group="const")
    for l in range(L):
        B.dma(pf[l], pfd[l], group="const")
        B.dma(rb[l], rbd[l], group="const")
        B.dma(lw[l], lwd[l], group="const")
        B.dma(w2[l], w2d[l], group="const")
    B.cp(cstR, cst[:, C_ONE:C_ONE + 256], eng="dve")
    for l in range(L):
        t = sm(6)
        B.act(t, rb[l][:, 0:6], AF.Exp)
        B.ts(nea[l], t, -1.0, ALU.mult)
        t2 = sm(2)
        B.act(t2, pf[l][:, PF_LAM:PF_LAM + 2], AF.Exp, scale=-1.0)
        t3 = sm(2)
        B.act(t3, t2, AF.Ln, bias=1.0)
        B.ts(lc12[l][:, 0:2], t3, -8.0, ALU.mult)
        B.ts(lc12[l][:, 2:4], t3, -16.0, ALU.mult)
        for tl in (SAp[l], SCp[l], hlp[l], histA[l], histB[l], histF[l]):
            B.memset(tl, 0.0)

    slot_i = [0]

    prefetched = {}

    def prefetch(key, src2d, ncols):
        if key not in prefetched:
            prefetched[key] = fill(src2d, ncols)

    def fill(src2d, ncols, nk=8, key=None):
        if key is not None and key in prefetched:
            return prefetched.pop(key)
        assert nk * ncols <= SLOTW
        s = slots[slot_i[0] % NSLOT]
        slot_i[0] += 1
        sv = s[:, 0:nk * ncols].re("p (c n) -> p c n", c=nk)
        B.dma(s[:, 0:nk * ncols], src2d, eng="pool", group="w:" + s.k[0])
        return sv

    evn = [0]

    def evac(out, in_, scale=None):
        evn[0] += 1
        if scale is not None:
            B.act(out, in_, AF.Copy, scale=scale)
        elif evn[0] % 2:
            B.cp(out, in_, eng="act")
        else:
            B.cp(out, in_, eng="dve")

    def dump(name, view):
        if name in dbg_out:
            B.dma(dbg_out[name], view, group="dbg_" + name)

    xTr = xT.r()

    def proj_fm(sv, j, n, T, rhsT=None):
        ps = B.nps()
        rr = xTr if rhsT is None else rhsT
        TN = max(T, 256)
        for c in range(8):
            B.mm(ps[0:n, 0:TN], sv[:, c, 128 * j:128 * j + n], rr[:, c, 0:TN], start=(c == 0), stop=(c == 7))
        return ps

    def proj_tm(sv, n, blk, c0=0):
        ps = B.nps()
        for c in range(8):
            B.mm(ps[:, 0:n], xTr[:, c, 128 * blk:128 * blk + 128], sv[:, c, c0:c0 + n], start=(c == 0), stop=(c == 7))
        return ps

    def load_x(tc):
        src = xs if tc.samp else xp
        r0 = 0 if tc.samp else tc.ti * 512
        for blk in range(tc.NB):
            xi = xin[blk % 2]
            B.dma(xi, src[r0 + 128 * blk:r0 + 128 * blk + 128, :])
            for half in range(2):
                ps = B.nps()
                for q in range(4):
                    c = 4 * half + q
                    B.tr(ps[:, 128 * q:128 * q + 128], xi[:, 128 * c:128 * c + 128], ident)
                evac(xTr[:, 4 * half:4 * half + 4, 128 * blk:128 * blk + 128], ps.re("p (a b) -> p a b", a=4))

    def store_y(tc):
        dst = ys if tc.samp else yp
        r0 = 0 if tc.samp else tc.ti * 512
        for blk in range(tc.NB):
            xi = xin[blk % 2]
            for half in range(2):
                ps = B.nps()
                for q in range(4):
                    c = 4 * half + q
                    B.tr(ps[:, 128 * q:128 * q + 128], xT[:, c, 128 * blk:128 * blk + 128], ident)
                evac(xi[:, 512 * half:512 * half + 512], ps)
            B.dma(dst[r0 + 128 * blk:r0 + 128 * blk + 128, :], xi)

    def conv_fm(out3, pre3, Tt, wcols, ntap, bias=None):
        if bias is not None:
            B.ts(out3, pre3[:, :, 0:Tt], wcols[0], ALU.mult, bias, ALU.add)
        else:
            B.ts(out3, pre3[:, :, 0:Tt], wcols[0], ALU.mult)
        for i in range(1, ntap):
            B.stt(out3, pre3[:, :, i:i + Tt], wcols[i], out3, ALU.mult, ALU.add)

    def hist_from_state(state2d, nrows, nch, dst_fn):
        R = 16 * nrows
        nb = nch // 128
        for b0 in range(0, nb, 8):
            nbb = min(8, nb - b0)
            xi = xin[(b0 // 8) % 2]
            B.dma(xi[0:R, 0:128 * nbb], state2d[:, 128 * b0:128 * (b0 + nbb)])
            for b in range(nbb):
                ps = B.nps()
                B.tr(ps[:, 0:R], xi[0:R, 128 * b:128 * b + 128], ident[0:R, 0:R])
                evac(dst_fn(b0 + b), ps[:, 0:R].re("p (s r) -> p s r", r=nrows))

    def state_rows_out(tc, ps_tm, ncols, nrows, dst_p, dst_s, col0):
        evac(stg[:, 0:ncols], ps_tm[:, 0:ncols])
        if tc.samp:
            for r in range(nrows):
                base = stg.ap[:, 0:ncols]
                pstep = base.ap[0][0]
                srcv = V(bass.AP(base.tensor, base.offset + (8 - nrows + r) * pstep, [[8 * pstep, 16], [1, ncols]]), stg.k)
                B.dma(dst_s[:, r, col0:col0 + ncols], srcv, group="so")
        else:
            B.dma(dst_p[:, col0:col0 + ncols], stg[128 - nrows:128, 0:ncols], group="so")

    def rms_gate_gen(tc, l, blk, o_tm, z_view, nw, hd0, p_sq=29, p_sz=30):
        o2 = o_tm.re("p h v -> p (h v)")
        sq = pgf(p_sq, 1, 384)
        B.act(sq, o2, AF.Square)
        sz = pgf(p_sz, 1, 384)
        B.act(sz, z_view, AF.Silu)
        yield
        ss = sm(6)
        B.red(ss, sq.re("p (h v) -> p h v", h=6))
        B.ts(ss, ss, 1.0 / 64, ALU.mult, EPS, ALU.add)
        yield
        B.act(ss, ss, AF.Sqrt)
        yield
        rs = sm(6)
        B.recip(rs, ss)
        B.tt(o_tm, o_tm, rs.bc(2, [128, 6, 64]), ALU.mult)
        yield
        B.tt(o_tm, o_tm, nw.bc(1, [128, 6, 64]), ALU.mult)
        yield
        B.tt(o2, o2, sz, ALU.mult)
        ps = B.nps()
        for j in range(3):
            B.tr(ps[:, 128 * j:128 * j + 128], o2[:, 128 * j:128 * j + 128], ident)
        yield
        evac(rpg(hd0, 3)[:, :, 128 * blk:128 * blk + 128], ps[:, 0:384].re("p (a b) -> p a b", a=3))

    def rms_gate_heads(tc, l, blk, o_tm, z_view, nw, hd0, p_sq=29, p_sz=30):
        for _ in rms_gate_gen(tc, l, blk, o_tm, z_view, nw, hd0, p_sq, p_sz):
            pass

    def phase_A(tc, l):
        T, NB, S, Tt, NS, mb = tc.T, tc.NB, tc.S, tc.Tt, tc.NS, tc.mb
        TRI = cst[:, mb + M_TRI:mb + M_TRI + 128]
        SEGM = cst[:, mb + M_SEGM:mb + M_SEGM + 128]
        NSTRIT = cst[:, mb + M_NSTRIT:mb + M_NSTRIT + 128]
        NTRI = cst[:, mb + M_NTRI:mb + M_NTRI + 128]
        SEG = cst[:, mb + M_SEG:mb + M_SEG + 16]
        LAST = cst[:, mb + M_LAST:mb + M_LAST + 16]
        W = 3 + Tt

        def pre(b):
            return pg(b % 3)[:, 0, 0:S * W].re("p (s w) -> p s w", s=S)

        def qk(b):
            return pg(4 + b)[:, 0, 0:T]

        hsA = pgf(3, 1, 9 * 48).re("p (b s r) -> p b s r", b=9, s=16)
        if tc.samp:
            hist_from_state(sdc[l].re("s r n -> (s r) n"), 3, 1152, lambda b: hsA[:, b, :, :])
        for si in range(5):
            c0 = 256 * si
            ncq = 256 if si < 4 else 128
            sv = fill(wi[l].cols(c0, ncq), ncq, key=("A", l, si))
            for j in range(ncq // 128):
                b = 2 * si + j
                if tc.samp:
                    B.cp(pre(b)[:, :, 0:3], hsA[:, b, :, :], eng="dve")
                else:
                    B.cp(pre(b)[:, 0, 0:3], histA[l][:, b, :], eng="dve")
                ps = proj_fm(sv, j, 128, T)
                evac(pre(b)[:, :, 3:3 + Tt], ps[:, 0:T].re("p (s t) -> p s t", s=S))
                if not tc.samp:
                    B.cp(histA[l][:, b, :], pre(b)[:, 0, Tt:Tt + 3], eng="dve")
                o3 = qk(b).re("p (s t) -> p s t", s=S)
                conv_fm(o3, pre(b), Tt, [pf[l][:, PF_CA + 4 * b + i:PF_CA + 4 * b + i + 1] for i in range(4)], 4)
                B.act(qk(b), qk(b), AF.Silu)
            if tc.state_out:
                pt = proj_tm(sv, ncq, tc.NB - 1)
                state_rows_out(tc, pt, ncq, 3, o_pdc[l], o_sdc[l], c0)
        tmA = pgf(13, 4, NB * 396).re("p (b n) -> p b n", b=NB)
        for (zc0, zn) in ((0, 256), (256, 140)):
            sv = fill(wi[l].cols(2304 + zc0, 256), 256)
            for blk in range(NB):
                pt = proj_tm(sv, 256, blk)
                evac(tmA[:, blk, zc0:zc0 + zn], pt[:, 0:zn])
        dump("qkv_silu_%d" % l, pg(4, 9)[:, :, 0:T])
        if cfg.get("a_stop", 99) <= 1:
            return
        sqs = [rpg(8 + b)[:, 0, 0:T] for b in range(6)]
        rns = [pg(19 + b)[:, 0, 0:T] for b in range(6)]
        pss = []
        for b in range(6):
            B.act(sqs[b], qk(b), AF.Square)
        for b in range(6):
            ps = B.nps()
            pss.append(ps)
            B.mm(ps[:, 0:T], b64R, sqs[b])
        for b in range(6):
            B.act(rns[b], pss[b][:, 0:T], AF.Sqrt, bias=EPS)
        for b in range(6):
            B.recip(rns[b], rns[b])
            if b < 3:
                B.stt(qk(b), rns[b], 0.125, qk(b), ALU.mult, ALU.mult)
            else:
                B.tt(qk(b), qk(b), rns[b], ALU.mult)
        dump("qkn_%d" % l, pg(4, 6)[:, :, 0:T])
        if cfg.get("a_stop", 99) <= 2:
            return

        KL = tc.K
        pending_rms = []
        NG = 6 * NB
        gatv = [V(gat.ap[:, i, 0:NG], ["gat%d" % i]) for i in range(12)]

        def g3(v):
            return v.re("p (b h) -> p b h", h=6)
        B.act(g3(gatv[0]), tmA[:, :, 384:390], AF.Sigmoid)
        B.ts(gatv[1], gatv[0], -1.0, ALU.mult)
        B.tt(g3(gatv[8]), tmA[:, :, 390:396], rb[l][:, 6:12].bc(1, [128, NB, 6]), ALU.add)
        B.act(gatv[8], gatv[8], AF.Exp)
        B.act(gatv[8], gatv[8], AF.Ln, bias=1.0)
        B.tt(g3(gatv[2]), g3(gatv[8]), nea[l].bc(1, [128, NB, 6]), ALU.mult)
        psg = B.nps()
        B.mm(psg[:, 0:NG], TRI, gatv[2])
        B.mm(psg[:, 32:32 + NG], SEGM, gatv[2])
        B.cp(gatv[3], psg[:, 0:NG], eng="dve")
        B.act(gatv[4], psg[:, 0:NG], AF.Exp)
        B.act(gatv[5], psg[:, 32:32 + NG], AF.Exp)
        B.tt(gatv[6], psg[:, 32:32 + NG], gatv[3], ALU.subtract)
        B.act(gatv[6], gatv[6], AF.Exp)
        B.tt(gatv[7], gatv[0], gatv[4], ALU.mult)
        B.ts(gatv[11], gatv[3], -1.0, ALU.mult)
        if NS == 1:
            B.ts(gatv[9], gatv[5], LAST[:, 0:1], ALU.mult)
            psl = B.nps()
            B.mm(psl[:, 0:NG], ones, gatv[9])
            B.cp(gatv[10], psl[:, 0:NG], eng="act")
        for blk in range(NB):
            tc0 = 128 * blk
            za = tmA[:, blk, 0:384]
            ba = tmA[:, blk, 384:390]
            aa = tmA[:, blk, 390:396]
            ktm = pgf(17, 1, 384)
            vtm = pgf(18, 1, 384)
            for (dst, b0) in ((ktm, 3), (vtm, 6)):
                ps = B.nps()
                for j in range(3):
                    B.tr(ps[:, 128 * j:128 * j + 128], qk(b0 + j)[:, tc0:tc0 + 128], ident)
                evac(dst, ps[:, 0:384])
            c6 = slice(6 * blk, 6 * blk + 6)
            beta = gatv[0][:, c6]; nbeta = gatv[1][:, c6]; g = gatv[2][:, c6]; gc = gatv[3][:, c6]; egc = gatv[4][:, c6]
            egl = gatv[5][:, c6]; kdf = gatv[6][:, c6]; bexp = gatv[7][:, c6]
            if blk == 0:
                dump("g_%d" % l, g)
                dump("beta_%d" % l, beta)
            DG = pgf(19, 2, 768).re("p (h f) -> p h f", h=6)
            Dm = pgf(21, 2, 768).re("p (h f) -> p h f", h=6)
            DmT = pgf(23, 2, 768).re("p (h f) -> p h f", h=6)
            PE_ = cfg.get("pool_pre", 1)
            B.tt(DG, ident.bc(1, [128, 6, 128]), gc.bc(2, [128, 6, 128]), ALU.mult, eng=("pool" if PE_ else "dve"))
            ngc = gatv[11][:, c6]
            for hf in range(2):
                psr = B.nps()
                B.mm(psr[:, 0:384], ones, DG[:, 3 * hf:3 * hf + 3, :].re("p h f -> p (h f)"))
                R3 = psr[:, 0:384].re("p (h f) -> p h f", h=3)
                d1 = Dm[:, 3 * hf:3 * hf + 3, :]
                d2 = DmT[:, 3 * hf:3 * hf + 3, :]
                B.tt(d1, R3, NSTRIT.bc(1, [128, 3, 128]), ALU.subtract)
                B.tt(d2, R3, NTRI.bc(1, [128, 3, 128]), ALU.add)
                for hh in range(3):
                    h = 3 * hf + hh
                    B.act(Dm[:, h, :], Dm[:, h, :], AF.Exp, bias=gc[:, h:h + 1], scale=-1.0)
                    B.act(DmT[:, h, :], DmT[:, h, :], AF.Exp, bias=ngc[:, h:h + 1])
            bv = pgf(25, 1, 384).re("p (h v) -> p h v", h=6)
            kb = pgf(26, 1, 384).re("p (h v) -> p h v", h=6)
            kdec = pgf(27, 1, 384).re("p (h v) -> p h v", h=6)
            otm = pgf(28 if blk % 2 == 0 else 3, 1, 384).re("p (h v) -> p h v", h=6)
            v3 = vtm.re("p (h v) -> p h v", h=6)
            k3 = ktm.re("p (h v) -> p h v", h=6)
            pe_ = "pool" if PE_ else "dve"
            B.tt(bv, v3, beta.bc(2, [128, 6, 64]), ALU.mult, eng=pe_)
            B.tt(kb, k3, bexp.bc(2, [128, 6, 64]), ALU.mult, eng=pe_)
            B.tt(kdec, k3, kdf.bc(2, [128, 6, 64]), ALU.mult, eng=pe_)
            if NS == 1:
                glb = gatv[10][:, c6]
            else:
                SEL = pgf(31, 1, 96)
                glb = pgf(31, 1, 192)[:, 96:192].re("p (h s) -> p h s", h=6)
                B.tt(SEL.re("p (h s) -> p h s", h=6), egl.bc(2, [128, 6, 16]), LAST.bc(1, [128, 6, 16]), ALU.mult)
                psl = B.nps()
                B.mm(psl[:, 0:96], ones, SEL)
                B.cp(glb, psl[:, 0:96].re("p (h s) -> p h s", h=6), eng="act")
            SAs = sab

            def head_gen(h, slot):
                hp, po = h // 2, (h % 2) * 64
                kT = qk(3 + hp)[po:po + 64, tc0:tc0 + 128]
                qT = qk(hp)[po:po + 64, tc0:tc0 + 128]
                hb = [V(hbtR.ap[:, 5 * slot + i, :], ["hb%d_%d" % (slot, i)]) for i in range(4)]
                hw = [V(hbt2R.ap[:, 2 * slot + i, :], ["hc%d_%d" % (slot, i)]) for i in range(2)]
                Nm, NmT, Pa, Pb = hb
                Wa, Wb = hw
                NN = V(hbtR.ap[:, 5 * slot:5 * slot + 2, :].rearrange("p a b -> p (a b)"), Nm.k + NmT.k)
                PPa = V(hbtR.ap[:, 5 * slot + 2:5 * slot + 4, :].rearrange("p a b -> p (a b)"), Pa.k + Pb.k)
                PPb = V(hbtR.ap[:, 5 * slot + 3:5 * slot + 5, :].rearrange("p a b -> p (a b)"), Pb.k)
                ps = B.nps()
                B.mm(ps[:, 0:128], kT, kT)
                B.mm(ps[:, 128:256], kT, qT)
                yield
                B.stt(Nm, ps[:, 0:128], nbeta[:, h:h + 1], Dm[:, h, :], ALU.mult, ALU.mult)
                qkmT = V(otm_x.ap[:, slot, :], ["qkm%d" % slot])
                B.tt(qkmT, ps[:, 128:256], DmT[:, h, :], ALU.mult)
                ps = B.nps()
                B.tr(ps[:, 0:128], Nm.f(), ident)
                yield
                B.cp(NmT, ps[:, 0:128], eng="dve")
                B.tt(Wa[:, 128:256], ps[:, 0:128], ident, ALU.add)
                ps = B.nps()
                ps2 = B.nps()
                if SOLVE_R:
                    B.mm(ps[:, 0:256], NmT, NN)
                    B.mm(ps2[:, 0:256], Nm, NN)
                else:
                    B.mm(ps[:, 0:128], NmT, Nm)
                    B.mm(ps2[:, 128:256], Nm, NmT)
                yield
                B.cp(Pa, ps[:, 0:128], eng="act")
                B.cp(Wa[:, 0:128], ps2[:, 128:256], eng="dve")
                Pc, Pn, Wc, Wn, PPc, PPn = Pa, Pb, Wa, Wb, PPa, PPb
                for k in range(1, KL + 1):
                    lastk = k == KL
                    ps = B.nps()
                    if lastk and not SOLVE_R:
                        B.mm(ps[:, 128:256], Pc, Wc[:, 128:256])
                    else:
                        B.mm(ps[:, 0:256], Pc, Wc)
                    if not lastk:
                        ps2 = B.nps()
                        if SOLVE_R:
                            B.mm(ps2[:, 0:256], Wc[:, 0:128], PPc)
                        else:
                            B.mm(ps2[:, 0:128], Wc[:, 0:128], Pc)
                    yield
                    B.tt(Wn[:, 128:256], Wc[:, 128:256].f(), ps[:, 128:256], ALU.add)
                    if not lastk:
                        B.cp(Wn[:, 0:128], ps[:, 0:128], eng="dve")
                        B.cp(Pn, ps2[:, 0:128], eng="act")
                    Pc, Pn, Wc, Wn, PPc, PPn = Pn, Pc, Wn, Wc, PPn, PPc
                AT = Wc[:, 128:256].f()
                u_sb = V(uwb.ap[:, slot, 0:64], ["uw%d_0" % slot])
                w_sb = V(uwb.ap[:, slot, 64:128], ["uw%d_1" % slot])
                qSe = V(uwb.ap[:, slot, 128:192], ["uw%d_2" % slot])
                wkT = V(uwb.ap[:, slot, 192:320], ["uw%d_3" % slot])
                ps = B.nps()
                B.mm(ps[:, 0:64], AT, bv[:, h, :])
                B.mm(ps[po:po + 64, 128:256], kb[:, h, :], AT)
                yield
                B.cp(u_sb, ps[:, 0:64], eng="dve")
                B.cp(wkT[po:po + 64, :], ps[po:po + 64, 128:256], eng="dve")
                if NS == 1:
                    Sh = SAp[l][po:po + 64, hp, :]
                    ps = B.nps()
                    B.mm(ps[:, 0:64], wkT[po:po + 64, :], Sh)
                    B.mm(ps[:, 64:128], qT, Sh)
                    yield
                    B.tt(w_sb, u_sb, ps[:, 0:64], ALU.subtract)
                    B.ts(qSe, ps[:, 64:128], egc[:, h:h + 1], ALU.mult)
                else:
                    if h % 2 == 0:
                        for q in range(2):
                            B.dma(SAs[64 * q:64 * q + 64, :, :], sdl[l][:, h + q, :, :].re("s d v -> d s v"), group="sa")
                    ps = B.nps()
                    for s in range(16):
                        B.mm(ps[0:64, 8 * s:8 * s + 8], SAs[po:po + 64, s, :], wkT[po:po + 64, 8 * s:8 * s + 8])
                        B.mm(ps[0:64, 128 + 8 * s:128 + 8 * s + 8], SAs[po:po + 64, s, :], qT[:, 8 * s:8 * s + 8])
                    yield
                    cTa = hbuf()
                    cTb = hbuf()
                    B.cp(cTa[0:64, :], ps[0:64, 0:128], eng="act")
                    B.cp(cTb[0:64, :], ps[0:64, 128:256], eng="act")
                    ps = B.nps()
                    B.tr(ps[:, 0:64], cTa[0:64, :], ident[0:64, 0:64])
                    B.tr(ps[:, 64:128], cTb[0:64, :], ident[0:64, 0:64])
                    yield
                    B.tt(w_sb, u_sb, ps[:, 0:64], ALU.subtract)
                    B.act(qSe, ps[:, 64:128], AF.Copy, scale=egc[:, h:h + 1])
                ps = B.nps()
                B.mm(ps[:, 0:64], qkmT, w_sb)
                if NS == 1:
                    B.mm(ps[po:po + 64, 128:192], kdec[:, h, :], w_sb)
                    yield
                    B.tt(otm[:, h, :], qSe, ps[:, 0:64], ALU.add)
                    B.stt(Sh, Sh, glb[po:po + 64, h:h + 1], ps[po:po + 64, 128:192], ALU.mult, ALU.add)
                else:
                    yield
                    B.tt(otm[:, h, :], qSe, ps[:, 0:64], ALU.add)
                    Wexp = wxb
                    B.tt(Wexp, w_sb.bc(1, [128, 16, 64]), SEG.bc(2, [128, 16, 64]), ALU.mult)
                    for hf in range(2):
                        psU = B.nps()
                        B.mm(psU[po:po + 64, 0:512], kdec[:, h, :], Wexp[:, 8 * hf:8 * hf + 8, :].re("p s v -> p (s v)"))
                        Sv = SAs[po:po + 64, 8 * hf:8 * hf + 8, :]
                        B.tt(Sv, Sv, glb[po:po + 64, h, 8 * hf:8 * hf + 8].bc(2, [64, 8, 64]), ALU.mult)
                        B.tt(Sv, Sv, psU[po:po + 64, 0:512].re("p (s v) -> p s v", s=8), ALU.add)
                    if h % 2 == 1:
                        for q in range(2):
                            B.dma(o_sdl[l][:, h - 1 + q, :, :].re("s d v -> d s v"), SAs[64 * q:64 * q + 64, :, :], group="sa")

            G = cfg.get('g_prompt', 3) if NS == 1 else 2
            for h0 in range(0, 6, G):
                gens = [head_gen(h0 + i, i) for i in range(G) if h0 + i < 6]
                if h0 == 0 and pending_rms:
                    gens.append(pending_rms.pop())
                while gens:
                    for gen in list(gens):
                        try:
                            next(gen)
                        except StopIteration:
                            gens.remove(gen)
            if blk == 0:
                dump("oa_raw_%d" % l, otm.re("p h v -> p (h v)"))
            pending_rms.append(rms_gate_gen(tc, l, blk, otm, za, rb[l][:, 12:76], 0))
        for gen in pending_rms:
            for _ in gen:
                pass
        if tc.last:
            for h in range(6):
                hp, po = h // 2, (h % 2) * 64
                B.dma(o_pdl[l][h], SAp[l][po:po + 64, hp, :], group="pdl")

    def phase_B(tc, l):
        T, S, Tt = tc.T, tc.S, tc.Tt
        W = 3 + Tt

        def pre(cb):
            return pg(cb)[:, 0, 0:S * W].re("p (s w) -> p s w", s=S)

        if tc.samp:
            hist_from_state(slc[l].re("s r n -> (s r) n"), 3, 256, lambda b: pre(b)[:, :, 0:3])
            xi = xin[0]
            B.dma(xi[0:16, 0:256], slr[l])
            h0 = pgf(16, 1, 32).re("p (c s) -> p c s", c=2)
            for cb in range(2):
                ps = B.nps()
                B.tr(ps[:, 0:16], xi[0:16, 128 * cb:128 * cb + 128], ident[0:16, 0:16])
                evac(h0[:, cb, :], ps[:, 0:16])
            hl_s = pgf(17, 1, 32).re("p (c s) -> p c s", c=2)
        else:
            B.cp(pg(0, 2)[:, :, 0:3], histB[l], eng="dve")
        sv = fill(wi[l].cols(1152, 256), 256)
        if tc.state_out:
            pt = proj_tm(sv, 256, tc.NB - 1)
            state_rows_out(tc, pt, 256, 3, o_plc[l], o_slc[l], 0)
        for cb in range(2):
            ps = proj_fm(sv, cb, 128, T)
            evac(pre(cb)[:, :, 3:3 + Tt], ps[:, 0:T].re("p (s t) -> p s t", s=S))
        if not tc.samp:
            B.cp(histB[l], pg(0, 2)[:, :, 512:515], eng="dve")
        svg = fill(wi[l].cols(1408, 256), 256)
        def gen_b(cb):
            xc = pg(2 + cb)[:, 0, 0:T]
            conv_fm(xc.re("p (s t) -> p s t", s=S), pre(cb), Tt,
                    [pf[l][:, PF_CB + 4 * cb + i:PF_CB + 4 * cb + i + 1] for i in range(4)], 4,
                    bias=pf[l][:, PF_CBB + cb:PF_CBB + cb + 1])
            yield
            rs = pg(4 + cb)[:, 0, 0:T]
            is_ = pg(6 + cb)[:, 0, 0:T]
            a = pg(8 + cb)[:, 0, 0:T]
            sq = pg(10 + cb)[:, 0, 0:T]
            hh = pg(12 + cb)[:, 0, 0:T]
            psr = B.nps()
            B.mm(psr[:, 0:T], lw[l][:, 128 * cb:128 * cb + 128], xc)
            B.act(rs, psr[:, 0:T], AF.Sigmoid, bias=pf[l][:, PF_LBR + cb:PF_LBR + cb + 1])
            psi = B.nps()
            B.mm(psi[:, 0:T], lw[l][:, 256 + 128 * cb:256 + 128 * cb + 128], xc)
            B.act(is_, psi[:, 0:T], AF.Sigmoid, bias=pf[l][:, PF_LBI + cb:PF_LBI + cb + 1])
            yield
            B.act(a, rs, AF.Exp, scale=lc12[l][:, cb:cb + 1])
            B.act(sq, rs, AF.Exp, scale=lc12[l][:, 2 + cb:3 + cb])
            B.act(sq, sq, AF.Sqrt, bias=1.0, scale=-1.0)
            yield
            B.tt(is_, is_, xc, ALU.mult)
            B.tt(is_, is_, sq, ALU.mult)
            yield
            if tc.samp:
                for s in range(16):
                    B.scan(hh[:, 8 * s:8 * s + 8], a[:, 8 * s:8 * s + 8], is_[:, 8 * s:8 * s + 8], h0[:, cb, s:s + 1])
                B.cp(hl_s[:, cb, :], hh.re("p (s t) -> p s t", t=8)[:, :, 7], eng="dve")
            else:
                B.scan(hh, a, is_, hlp[l][:, cb:cb + 1])
                B.cp(hlp[l][:, cb:cb + 1], hh[:, T - 1:T], eng="dve")
            if cb == 0:
                dump("h_lru_%d" % l, hh)
            psg = proj_fm(svg, cb, 128, T)
            gg = pg(14 + cb)[:, 0, 0:T]
            B.act(gg, psg[:, 0:T], AF.Gelu_apprx_tanh)
            B.tt(rpg(3 + cb)[:, 0, 0:T], hh, gg, ALU.mult)
        gens = [gen_b(0), gen_b(1)]
        while gens:
            for gen in list(gens):
                try:
                    next(gen)
                except StopIteration:
                    gens.remove(gen)
        if tc.samp:
            for cb in range(2):
                ps = B.nps()
                B.tr(ps[0:16, 0:128], hl_s[:, cb, :], ident)
                evac(stg[0:16, 128 * cb:128 * cb + 128], ps[0:16, 0:128])
            B.dma(o_slr[l], stg[0:16, 0:256], group="so")
        elif tc.last:
            B.dma(o_plr[l].re("(c p) -> p c", p=128), hlp[l], group="plr", nc_ok=True)

    def phase_C(tc, l):
        T, NB, S, Tt, NS, mb = tc.T, tc.NB, tc.S, tc.Tt, tc.NS, tc.mb
        TRI = cst[:, mb + M_TRI:mb + M_TRI + 128]
        SEGM = cst[:, mb + M_SEGM:mb + M_SEGM + 128]
        SEG = cst[:, mb + M_SEG:mb + M_SEG + 16]
        qcT = [pg(0 + g)[:, 0, 0:T] for g in range(2)]
        kcT = [pg(2 + g)[:, 0, 0:T] for g in range(2)]
        lcT = pg(4)[0:32, 0, 0:T]
        vct = pgf(5, 3, NB * 384).re("p (b n) -> p b n", b=NB)
        zct = pgf(8, 3, NB * 384).re("p (b n) -> p b n", b=NB)
        kct = pgf(11, 2, NB * 192).re("p (b n) -> p b n", b=NB)
        sv = fill(wi[l].cols(1664, 256), 256)
        for g in range(2):
            ps = proj_fm(sv, g, 96, T)
            evac(qcT[g][0:96, :], ps[0:96, 0:T], scale=32.0 ** -0.5)
        sv = fill(wi[l].cols(1920, 256), 256)
        for g in range(2):
            ps = proj_fm(sv, g, 96, T)
            evac(kcT[g][0:96, :], ps[0:96, 0:T])
        sv = fill(wi[l].cols(2176, 16), 16)
        B.memset(lcT, 1.0)
        ps = proj_fm(sv, 0, 16, T)
        evac(lcT[0:16, :], ps[0:16, 0:T])
        for (c0, dsts) in ((2816, [(vct, 0, 0, 256)]), (4352, [(vct, 256, 0, 128), (zct, 256, 128, 128)]),
                           (3328, [(zct, 0, 0, 256)]), (3840, [(kct, 0, 0, 192)])):
            sv = fill(wi[l].cols(c0, 256), 256)
            for blk in range(NB):
                pt = proj_tm(sv, 256, blk)
                for (dst, d0, p0, n) in dsts:
                    evac(dst[:, blk, d0:d0 + n], pt[:, p0:p0 + n])
        SCs = sab
        pre_c = {}
        pending_rms_c = []

        def pre_gen_c(blk):
            tc0 = 128 * blk
            pA = pgf(20 + 3 * blk, 1, 384)
            pB = pgf(21 + 3 * blk, 1, 224)
            pC = pgf(22 + 3 * blk, 1, 512)
            logf = pA[:, 0:192]
            b_sb = pA[:, 192:384]
            kdc = pB[:, 0:192]
            ebl = pB[:, 192:224].re("p (g s) -> p g s", g=2)
            qt = pC[:, 0:256].re("p (g t) -> p g t", g=2)
            kt = pC[:, 256:512].re("p (g t) -> p g t", g=2)
            psl = B.nps()
            B.mm(psl[:, 0:192], lcT[:, tc0:tc0 + 128], w2[l])
            yield
            B.act(logf, psl[:, 0:192], AF.Exp, scale=-1.0)
            B.act(logf, logf, AF.Ln, bias=1.0)
            B.ts(logf, logf, -1.0 / 16, ALU.mult)
            if blk == 0:
                dump("logf_%d" % l, logf)
            psb = B.nps()
            B.mm(psb[:, 0:192], TRI, logf)
            B.mm(psb[:, 256:448], SEGM, logf)
            psT = B.nps()
            for g in range(2):
                B.mm(psT[0:96, 128 * g:128 * g + 128], logf[:, 96 * g:96 * g + 96], TRI)
                B.mm(psT[0:96, 256 + 16 * g:256 + 16 * g + 16], logf[:, 96 * g:96 * g + 96], SEG)
            yield
            B.cp(b_sb, psb[:, 0:192], eng="dve")
            B.tt(kdc, psb[:, 256:448], b_sb, ALU.subtract)
            B.act(kdc, kdc, AF.Exp)
            B.tt(kdc, kdc, kct[:, blk, :], ALU.mult)
            B.act(qt[0:96], psT[0:96, 0:256].re("p (g t) -> p g t", g=2), AF.Exp)
            B.act(kt[0:96], psT[0:96, 0:256].re("p (g t) -> p g t", g=2), AF.Exp, scale=-1.0)
            B.act(ebl[0:96], psT[0:96, 256:288].re("p (g s) -> p g s", g=2), AF.Exp)
            for g in range(2):
                B.tt(qt[0:96, g, :], qt[0:96, g, :], qcT[g][0:96, tc0:tc0 + 128], ALU.mult)
                B.tt(kt[0:96, g, :], kt[0:96, g, :], kcT[g][0:96, tc0:tc0 + 128], ALU.mult)
            pre_c[blk] = (kdc, ebl, qt, kt)

        for b0 in range(0, NB, 2):
            gens = [pre_gen_c(b0 + i) for i in range(2) if b0 + i < NB]
            while gens:
                for gen in list(gens):
                    try:
                        next(gen)
                    except StopIteration:
                        gens.remove(gen)
        for blk in range(NB):
            tc0 = 128 * blk
            kdc, ebl, qt, kt = pre_c[blk]
            otm = pgf(19 if blk % 2 == 0 else 15, 1, 384).re("p (h v) -> p h v", h=6)

            def head_gen_c(h):
                g, po = h // 3, (h % 3) * 32
                vh = vct[:, blk, 64 * h:64 * h + 64]
                ps = B.nps()
                B.mm(ps[:, 0:128], kt[po:po + 32, g, :], qt[po:po + 32, g, :])
                yield
                attm = hbuf()
                B.tt(attm, ps[:, 0:128], TRI, ALU.mult)
                if NS == 1:
                    Sh = SCp[l][po:po + 32, g, :]
                    ps = B.nps()
                    B.mm(ps[:, 0:64], qt[po:po + 32, g, :], Sh, start=True, stop=False)
                    B.mm(ps[:, 0:64], attm, vh, start=False, stop=True)
                    B.mm(ps[po:po + 32, 128:192], kdc[:, 32 * h:32 * h + 32], vh)
                    yield
                    B.cp(otm[:, h, :], ps[:, 0:64], eng="dve")
                    B.stt(Sh, Sh, ebl[po:po + 32, g, 0:1], ps[po:po + 32, 128:192], ALU.mult, ALU.add)
                else:
                    if h % 3 == 0:
                        for q in range(3):
                            B.dma(SCs[32 * q:32 * q + 32, :, :], sgl[l][:, h + q, :, :].re("s k v -> k s v"), group="sa")
                    ps = B.nps()
                    for s in range(16):
                        B.mm(ps[0:64, 8 * s:8 * s + 8], SCs[po:po + 32, s, :], qt[po:po + 32, g, 8 * s:8 * s + 8])
                    yield
                    cT = hbuf()
                    B.cp(cT[0:64, :], ps[0:64, 0:128], eng="act")
                    ps = B.nps()
                    B.tr(ps[:, 0:64], cT[0:64, :], ident[0:64, 0:64])
                    B.mm(ps[:, 64:128], attm, vh)
                    yield
                    qS = hbuf()[:, 0:64]
                    B.cp(qS, ps[:, 0:64], eng="act")
                    B.tt(otm[:, h, :], qS, ps[:, 64:128], ALU.add)
                    Vexp = wxb
                    B.tt(Vexp, vh.bc(1, [128, 16, 64]), SEG.bc(2, [128, 16, 64]), ALU.mult)
                    for hf in range(2):
                        psU = B.nps()
                        B.mm(psU[po:po + 32, 0:512], kdc[:, 32 * h:32 * h + 32], Vexp[:, 8 * hf:8 * hf + 8, :].re("p s v -> p (s v)"))
                        Sv = SCs[po:po + 32, 8 * hf:8 * hf + 8, :]
                        B.tt(Sv, Sv, ebl[po:po + 32, g, 8 * hf:8 * hf + 8].bc(2, [32, 8, 64]), ALU.mult)
                        B.tt(Sv, Sv, psU[po:po + 32, 0:512].re("p (s v) -> p s v", s=8), ALU.add)
                    if h % 3 == 2:
                        for q in range(3):
                            B.dma(o_sgl[l][:, h - 2 + q, :, :].re("s k v -> k s v"), SCs[32 * q:32 * q + 32, :, :], group="sa")

            GC = cfg.get("g_c", 3)
            for h0 in range(0, 6, GC):
                gens = [head_gen_c(h0 + i) for i in range(GC) if h0 + i < 6]
                if h0 == 0 and pending_rms_c:
                    gens.append(pending_rms_c.pop())
                while gens:
                    for gen in list(gens):
                        try:
                            next(gen)
                        except StopIteration:
                            gens.remove(gen)
            if blk == 0:
                dump("oc_raw_%d" % l, otm.re("p h v -> p (h v)"))
            pending_rms_c.append(rms_gate_gen(tc, l, blk, otm, zct[:, blk, :], rb[l][:, 76:140], 5, p_sq=13, p_sz=14))
        for gen in pending_rms_c:
            for _ in gen:
                pass
        if tc.last:
            for h in range(6):
                g, po = h // 3, (h % 3) * 32
                B.dma(o_pgl[l][h], SCp[l][po:po + 32, g, :], group="pgl")

    def layer_norm(tc, l, gcol, bcol):
        T = tc.T
        psS = B.nps()
        psQ = B.nps()
        for m in range(8):
            y = pg(m)[:, 0, 0:T]
            ysq = rpg(8 + m % 2)[:, 0, 0:T]
            yr = rpg(10 + m % 2)[:, 0, 0:T]
            B.act(ysq, y, AF.Square)
            B.cp(yr, y, eng="act")
            B.mm(psS[:, 0:T], onesR, yr, start=(m == 0), stop=(m == 7))
            B.mm(psQ[:, 0:T], onesR, ysq, start=(m == 0), stop=(m == 7))
        mean = pg(13)[:, 0, 0:T]
        rstd = pg(14)[:, 0, 0:T]
        msq = pg(15)[:, 0, 0:T]
        B.act(msq, psS[:, 0:T], AF.Square, scale=1.0 / D)
        B.stt(rstd, psQ[:, 0:T], 1.0 / D, msq, ALU.mult, ALU.subtract)
        B.act(rstd, rstd, AF.Sqrt, bias=EPS)
        B.recip(rstd, rstd)
        for m in range(8):
            y = pg(m)[:, 0, 0:T]
            le = "pool" if (m % 2 == 1 and cfg.get("ln_pool", 0)) else "dve"
            B.stt(y, psS[:, 0:T], -1.0 / D, y, ALU.mult, ALU.add)
            B.tt(y, y, rstd, ALU.mult, eng=le)
            B.act(xTr[:, m, 0:T], y, AF.Identity, bias=pf[l][:, bcol + m:bcol + m + 1], scale=pf[l][:, gcol + m:gcol + m + 1])

    def wout_ln1(tc, l):
        T = tc.T
        hd = rpg(0, 8)
        m = 0
        for (c0, ncol) in ((0, 256), (256, 256), (512, 256), (768, 256)):
            sv = fill(wo[l].cols(c0, ncol), ncol)
            for j in range(ncol // 128):
                ps = proj_fm(sv, j, 128, T, rhsT=hd)
                B.stt(pg(m)[:, 0, 0:T], xT[:, m, 0:T], float(ALPHA), ps[:, 0:T], ALU.mult, ALU.add)
                m += 1
        for gb in range(2):
            prefetch(("U", l, gb), wu[l].cols(256 * gb, 256), 256)
        layer_norm(tc, l, PF_L1G, PF_L1B)

    def ffn_ln2(tc, l):
        T, S, Tt = tc.T, tc.S, tc.Tt
        W = 2 + Tt
        hT = rpg(0, 22)
        if tc.samp:
            hs = pgf(16, 2, 22 * 32).re("p (b s r) -> p b s r", b=22, s=16)
            hist_from_state(sfc[l].re("s r n -> (s r) n"), 2, DFF, lambda b: hs[:, b, :, :])
        for gb in range(22):
            sv = fill(wu[l].cols(256 * gb, 256), 256, key=("U", l, gb))
            gpre = pg(8 + gb % 3)[:, 0, 0:S * W].re("p (s w) -> p s w", s=S)
            if tc.samp:
                B.cp(gpre[:, :, 0:2], hs[:, gb, :, :], eng="act")
            else:
                B.cp(gpre[:, 0, 0:2], histF[l][:, gb, :], eng="act")
            psg = proj_fm(sv, 0, 128, T)
            B.cp(gpre[:, :, 2:2 + Tt], psg[:, 0:T].re("p (s t) -> p s t", s=S), eng="act")
            psv = proj_fm(sv, 1, 128, T)
            if not tc.samp:
                B.cp(histF[l][:, gb, :], gpre[:, 0, Tt:Tt + 2], eng="act")
            gcv = pg(11 + gb % 2)[:, 0, 0:T]
            conv_fm(gcv.re("p (s t) -> p s t", s=S), gpre, Tt,
                    [pf[l][:, PF_FCW + 3 * gb + i:PF_FCW + 3 * gb + i + 1] for i in range(3)], 3)
            B.act(gcv, gcv, AF.Gelu_apprx_tanh, bias=pf[l][:, PF_FCB + gb:PF_FCB + gb + 1])
            B.tt(hT[:, gb, 0:T], gcv, psv[:, 0:T], ALU.mult)
            if tc.state_out:
                pt = proj_tm(sv, 256, tc.NB - 1)
                state_rows_out(tc, pt, 128, 2, o_pfc[l], o_sfc[l], 128 * gb)
        for m in range(8):
            ps = B.nps()
            for a in range(2):
                s_ = slots[slot_i[0] % NSLOT]
                slot_i[0] += 1
                B.dma(s_[:, 0:1408], wd[l][m][:, 1408 * a:1408 * a + 1408], eng="pool", group="w:" + s_.k[0])
                sv = s_[:, 0:1408].re("p (c n) -> p c n", c=11)
                TN = max(T, 256)
                for c in range(11):
                    B.mm(ps[:, 0:TN], sv[:, c, :], hT[:, 11 * a + c, 0:TN], start=(a == 0 and c == 0), stop=(a == 1 and c == 10))
            B.stt(pg(m)[:, 0, 0:T], xT[:, m, 0:T], float(ALPHA), ps[:, 0:T], ALU.mult, ALU.add)
        nxt = next_layer.get((tc.kind, tc.ti, l))
        if nxt is not None:
            for si in range(2):
                prefetch(("A", nxt, si), wi[nxt].cols(256 * si, 256), 256)
        layer_norm(tc, l, PF_L2G, PF_L2B)

    stages = cfg.get("stages", "ABCWF")
    next_layer = {}
    seq = [(k, t, l) for (k, t) in tiles for l in range(L)]
    if "A" in stages:
        for a, b in zip(seq[:-1], seq[1:]):
            next_layer[a] = b[2]
    for (kind, ti) in tiles:
        tc = TileCtx(kind, ti, NPT)
        load_x(tc)
        for l in range(L):
            if "A" in stages:
                phase_A(tc, l)
            if "B" in stages:
                phase_B(tc, l)
            if "C" in stages:
                phase_C(tc, l)
            if "W" in stages:
                dump("heads_%d" % l, rpg(0, 8).f()[:, :, 0:tc.T])
                wout_ln1(tc, l)
                dump("x1_%d" % l, xT[:, :, 0:tc.T])
            if "F" in stages:
                ffn_ln2(tc, l)
                dump("x2_%d" % l, xT[:, :, 0:tc.T])
        store_y(tc)
    with B.stack:
        stats = B.P.build()
    return nc, stats


_W_NAMES = ["w_in", "conv_a_w", "a_log", "dt_bias", "norm_a_w", "conv_b_w", "conv_b_b", "lru_w_r", "lru_b_r", "lru_w_i", "lru_b_i",
            "lru_lambda", "gla_w2", "gla_b2", "norm_c_w", "w_out", "ln1_g", "ln1_b", "ffn_w_up", "ffn_conv_w", "ffn_conv_b",
            "ffn_w_down", "ln2_g", "ln2_b"]


def make_in_maps(inputs, cores):
    w = {k: np.asarray(inputs[k], np.float32) for k in _W_NAMES}
    pk = pack_weights(w)
    maps = []
    for c in cores:
        m = dict(pk)
        m["xp"] = np.ascontiguousarray(inputs["x_prompt"][c])
        m["xs"] = np.ascontiguousarray(inputs["x_sample"][16 * c:16 * c + 16].reshape(128, D))
        for nm, key in (("sdc", "state_delta_conv"), ("sdl", "state_delta"), ("slc", "state_lru_conv"), ("slr", "state_lru"),
                        ("sgl", "state_gla"), ("sfc", "state_ffn_conv")):
            m[nm] = np.ascontiguousarray(inputs[key][:, 16 * c:16 * c + 16])
        maps.append(m)
    return maps


def kernel(**inputs):
    inputs = {k: np.asarray(v) for k, v in inputs.items()}
    nc, stats = build_program({})
    maps = make_in_maps(inputs, list(range(NCORES)))
    res = run_bass_kernel_spmd(nc, maps, core_ids=list(range(NCORES)))
    r = res.results
    y_prompt = np.stack([r[c]["yp"] for c in range(NCORES)]).reshape(8, 2048, D)
    y_sample = np.concatenate([r[c]["ys"].reshape(16, 8, D) for c in range(NCORES)], axis=0)
    outs = [y_prompt, y_sample]
    for nm in ["o_pdc", "o_pdl", "o_plc", "o_plr", "o_pgl", "o_pfc"]:
        outs.append(np.stack([r[c][nm] for c in range(NCORES)], axis=1))
    for nm in ["o_sdc", "o_sdl", "o_slc", "o_slr", "o_sgl", "o_sfc"]:
        outs.append(np.concatenate([r[c][nm] for c in range(NCORES)], axis=1))
    return tuple(np.ascontiguousarray(o, dtype=np.float32) for o in outs)
```

```python
import bisect
import contextlib
import numpy as np
import concourse.bass as bass
import concourse.mybir as mybir
from concourse.bass_utils import run_bass_kernel_spmd

F32 = mybir.dt.float32
F32R = mybir.dt.float32r
AF = mybir.ActivationFunctionType
ALU = mybir.AluOpType
AX = mybir.AxisListType

D = 1024
DFF = 2816
ALPHA = 4.0 ** 0.25
EPS = 1e-6
NCORES = 8
EPOCH = 8192
STRICT_SAME_ENGINE = False
ENGS = ("pe", "dve", "act", "pool", "sp")


class Prog:
    def __init__(self, nc):
        self.nc = nc
        self.ops = []

    def op(self, eng, fn, reads=(), writes=(), group=None):
        self.ops.append((eng, fn, tuple(reads), tuple(writes), group))

    def build(self):
        nc = self.nc
        ops = self.ops
        n = len(ops)
        last_w = {}
        readers = {}
        deps = [None] * n
        for i, (eng, fn, rd, wr, grp) in enumerate(ops):
            d = set()
            for k in rd:
                j = last_w.get(k)
                if j is not None:
                    d.add((j, True))
                if k.startswith("ps"):
                    for r in readers.get(k, ()):
                        if ops[r][0] != eng:
                            d.add((r, False))
            for k in wr:
                j = last_w.get(k)
                if j is not None:
                    d.add((j, False))
                for r in readers.get(k, ()):
                    d.add((r, False))
            deps[i] = d
            for k in rd:
                readers.setdefault(k, []).append(i)
            for k in wr:
                last_w[k] = i
                readers[k] = []
        has_consumer = [False] * n
        fdeps = [None] * n
        for i, (eng, fn, rd, wr, grp) in enumerate(ops):
            raw = {j for (j, r) in deps[i] if r}
            nd = set()
            for (j, r) in deps[i]:
                if j == i:
                    continue
                pe, _, _, _, pg = ops[j]
                if pe == eng and pg is None:
                    if eng == "pe":
                        continue
                    if j not in raw and not STRICT_SAME_ENGINE:
                        continue
                nd.add(j)
            fdeps[i] = nd
            for j in nd:
                has_consumer[j] = True
        eng_cnt = {e: 0 for e in ENGS}
        grp_ops = {}
        sig = [None] * n
        for i, (eng, fn, rd, wr, grp) in enumerate(ops):
            if grp is not None:
                grp_ops.setdefault(grp, []).append(i)
                sig[i] = ("g", grp, 0)
            elif has_consumer[i]:
                c = eng_cnt[eng]
                eng_cnt[eng] = c + 1
                sig[i] = ("e", (eng, c // EPOCH), (c % EPOCH) + 1)
        sem_keys = []
        seen_keys = set()
        for s in sig:
            if s is not None and (s[0], s[1]) not in seen_keys:
                seen_keys.add((s[0], s[1]))
                sem_keys.append((s[0], s[1]))
        stack = contextlib.ExitStack()
        sems = {}
        for num, k in enumerate(sem_keys):
            sems[k] = stack.enter_context(nc.semaphore("s%d" % num))
        per_eng = {e: [] for e in ENGS}
        for i, o in enumerate(ops):
            per_eng[o[0]].append(i)
        group_owner = {}
        for i, o in enumerate(ops):
            if o[4] is not None:
                group_owner.setdefault(o[4], o[0])
        self.stats = dict(n_ops=n, n_sems=len(sem_keys), per_eng={e: len(v) for e, v in per_eng.items()})

        def emit_engine(eng_name, eobj):
            seen = {}
            for i in per_eng[eng_name]:
                eng, fn, rd, wr, grp = ops[i]
                need = {}
                for j in fdeps[i]:
                    kind, key, val = sig[j]
                    if kind == "g":
                        val = 16 * bisect.bisect_left(grp_ops[key], i)
                    kk = (kind, key)
                    if val > need.get(kk, 0):
                        need[kk] = val
                for kk, val in need.items():
                    if seen.get(kk, 0) >= val:
                        continue
                    seen[kk] = val
                    eobj.wait_ge(sems[kk], val)
                inst = fn(eobj)
                s = sig[i]
                if s is not None:
                    inst.then_inc(sems[(s[0], s[1])], 16 if s[0] == "g" else 1)
            for g, lst in grp_ops.items():
                if group_owner[g] == eng_name:
                    eobj.wait_ge(sems[("g", g)], 16 * len(lst))

        with stack:
            with nc.Block() as block:
                @block.tensor
                def _(e):
                    emit_engine("pe", e)

                @block.vector
                def _(e):
                    emit_engine("dve", e)

                @block.scalar
                def _(e):
                    emit_engine("act", e)

                @block.gpsimd
                def _(e):
                    emit_engine("pool", e)

                @block.sync
                def _(e):
                    emit_engine("sp", e)
        return self.stats


class V:
    def __init__(self, ap, keys):
        self.ap = ap
        self.k = tuple(keys)

    def __getitem__(self, idx):
        return V(self.ap[idx], self.k)

    def re(self, pat, **kw):
        return V(self.ap.rearrange(pat, **kw), self.k)

    def bc(self, axis, shape):
        return V(self.ap.unsqueeze(axis).to_broadcast(list(shape)), self.k)

    def kk(self, *keys):
        return V(self.ap, keys)

    def r(self):
        return V(self.ap.bitcast(F32R), self.k)

    def f(self):
        return V(self.ap.bitcast(F32), self.k)


def _ks(*vs):
    out = []
    for v in vs:
        if isinstance(v, V):
            out.extend(v.k)
    return out


def _a(v):
    return v.ap if isinstance(v, V) else v


class Builder:
    def __init__(self, nc):
        self.nc = nc
        self.P = Prog(nc)
        self.stack = contextlib.ExitStack()
        self.psn = 0
        self.ps = []
        self.uid = 0

    def sb(self, name, shape, dt=F32, key=None):
        t = self.stack.enter_context(self.nc.sbuf_tensor("sb_" + name, list(shape), dt))
        return V(t[:], [key or name])

    def init_psum(self):
        for i in range(8):
            t = self.stack.enter_context(self.nc.psum_tensor("ps%d" % i, [128, 512], F32))
            self.ps.append(V(t[:], ["ps%d" % i]))

    def nps(self):
        v = self.ps[self.psn % 8]
        self.psn += 1
        return v

    def mm(self, out, lhsT, rhs, start=True, stop=True):
        rd = _ks(lhsT, rhs) + ([] if start else _ks(out))
        self.P.op("pe", lambda e: e.matmul(out.ap, lhsT=lhsT.ap, rhs=rhs.ap, start=start, stop=stop), rd, _ks(out))

    def tr(self, out, in_, ident):
        self.P.op("pe", lambda e: e.transpose(out.ap, in_.ap, ident.ap), _ks(in_, ident), _ks(out))

    def act(self, out, in_, func, bias=None, scale=None, eng="act"):
        kw = {}
        if bias is not None:
            kw["bias"] = _a(bias)
        if scale is not None:
            kw["scale"] = _a(scale)
        self.P.op("act", lambda e: e.activation(out=out.ap, in_=in_.ap, func=func, **kw), _ks(in_, bias, scale), _ks(out))

    def tt(self, out, a, b, op, eng="dve"):
        self.P.op(eng, lambda e: e.tensor_tensor(out=out.ap, in0=a.ap, in1=b.ap, op=op), _ks(a, b), _ks(out))

    def ts(self, out, a, s1, op0, s2=None, op1=None, eng="dve"):
        if op1 is None:
            self.P.op(eng, lambda e: e.tensor_scalar(out=out.ap, in0=a.ap, scalar1=_a(s1), scalar2=None, op0=op0), _ks(a, s1), _ks(out))
        else:
            self.P.op(eng, lambda e: e.tensor_scalar(out=out.ap, in0=a.ap, scalar1=_a(s1), scalar2=_a(s2), op0=op0, op1=op1),
                      _ks(a, s1, s2), _ks(out))

    def stt(self, out, a, s, b, op0, op1):
        self.P.op("dve", lambda e: e.scalar_tensor_tensor(out=out.ap, in0=a.ap, scalar=_a(s), in1=b.ap, op0=op0, op1=op1),
                  _ks(a, s, b), _ks(out))

    def cp(self, out, in_, eng="dve"):
        if eng == "act":
            self.P.op("act", lambda e: e.activation(out=out.ap, in_=in_.ap, func=AF.Copy), _ks(in_), _ks(out))
        else:
            self.P.op(eng, lambda e: e.tensor_copy(out=out.ap, in_=in_.ap), _ks(in_), _ks(out))

    def red(self, out, in_, op=ALU.add):
        self.P.op("dve", lambda e: e.tensor_reduce(out=out.ap, in_=in_.ap, axis=AX.X, op=op), _ks(in_), _ks(out))

    def recip(self, out, in_):
        self.P.op("dve", lambda e: e.reciprocal(out=out.ap, in_=in_.ap), _ks(in_), _ks(out))

    def scan(self, out, d0, d1, init):
        self.P.op("dve", lambda e: e.tensor_tensor_scan(out=out.ap, data0=d0.ap, data1=d1.ap, initial=_a(init), op0=ALU.mult, op1=ALU.add),
                  _ks(d0, d1, init), _ks(out))

    def memset(self, out, val, eng="dve"):
        self.P.op(eng, lambda e: e.memset(out.ap, val), [], _ks(out))

    def dma(self, out, in_, eng="sp", group=None, nc_ok=False):
        self.uid += 1
        g = group or ("d%d" % self.uid)
        if nc_ok:
            self.P.op(eng, lambda e: e.dma_start(out=out.ap, in_=in_.ap, allow_slow_non_contiguous=True), _ks(in_), _ks(out), group=g)
        else:
            self.P.op(eng, lambda e: e.dma_start(out=out.ap, in_=in_.ap), _ks(in_), _ks(out), group=g)


NWI = 18 * 128 + 4 * 512 + 256
FM_SRC = [(128 * b, 128) for b in range(9)] + [(1548, 128), (1676, 128), (1804, 128), (1932, 128),
                                                (2060, 96), (2156, 96), (2252, 96), (2348, 96), (3212, 16)]
TM_SRC = [(1152, 396), (2444, 384), (2828, 384), (2252, 192)]
NPF = 172
PF_CA, PF_CB, PF_CBB, PF_LBR, PF_LBI, PF_LAM, PF_L1G, PF_L1B, PF_L2G, PF_L2B, PF_FCW, PF_FCB = 0, 36, 44, 46, 48, 50, 52, 60, 68, 76, 84, 150
NRB = 140
NEG = -30000.0


def _mask_set(c):
    idx = np.arange(128)
    seg = idx // c
    same = seg[:, None] == seg[None, :]
    ns = 128 // c
    tri = (same & (idx[:, None] <= idx[None, :])).astype(np.float32)
    segm = same.astype(np.float32)
    neg_strit = np.where(same & (idx[None, :] < idx[:, None]), 0.0, NEG).astype(np.float32)
    neg_tri = np.where(same & (idx[:, None] <= idx[None, :]), 0.0, NEG).astype(np.float32)
    seg01 = np.zeros((128, 16), np.float32)
    seg01[idx, seg] = 1.0
    last01 = np.zeros((128, 16), np.float32)
    li = (idx % c) == (c - 1)
    last01[idx[li], seg[li]] = 1.0
    return np.concatenate([tri, segm, neg_strit, neg_tri, seg01, last01], axis=1)


C_ID, C_ONE, C_B64, C_MP, C_MS = 0, 128, 256, 384, 384 + 544
NCONST = 384 + 2 * 544
M_TRI, M_SEGM, M_NSTRIT, M_NTRI, M_SEG, M_LAST = 0, 128, 256, 384, 512, 528


WI_PANELS = [(0, 256), (256, 256), (512, 256), (768, 256), (1024, 128), (2304, 256), (2560, 256),
             (1152, 256), (1408, 256), (1664, 256), (1920, 256), (2176, 16),
             (2816, 256), (4352, 256), (3328, 256), (3840, 256)]
WO_PANELS = [(0, 256), (256, 256), (512, 256), (768, 256)]
WU_PANELS = [(256 * gb, 256) for gb in range(22)]


def _panel_offsets(panels):
    offs, o = {}, 0
    for (c0, n) in panels:
        offs[(c0, n)] = o
        o += 8 * n
    return offs, o


WI_OFF, WI_TOT = _panel_offsets(WI_PANELS)
WO_OFF, WO_TOT = _panel_offsets(WO_PANELS)
WU_OFF, WU_TOT = _panel_offsets(WU_PANELS)


def _pack_panels(w, panels, tot):
    L = w.shape[0]
    out = np.zeros((L, 128, tot), np.float32)
    o = 0
    for (c0, n) in panels:
        blk = w[:, :, c0:c0 + n].reshape(L, 8, 128, n).transpose(0, 2, 1, 3).reshape(L, 128, 8 * n)
        out[:, :, o:o + 8 * n] = blk
        o += 8 * n
    return out


def make_consts():
    ident = np.eye(128, dtype=np.float32)
    ones = np.ones((128, 128), np.float32)
    b64 = np.zeros((128, 128), np.float32)
    b64[:64, :64] = 1.0
    b64[64:, 64:] = 1.0
    return np.ascontiguousarray(np.concatenate([ident, ones, b64, _mask_set(128), _mask_set(8)], axis=1))


def pack_weights(w):
    L = 2
    wi = np.zeros((L, D, NWI), np.float32)
    for b, (s, n) in enumerate(FM_SRC):
        wi[:, :, 128 * b:128 * b + n] = w["w_in"][:, :, s:s + n]
    for g, (s, n) in enumerate(TM_SRC):
        wi[:, :, 2304 + 512 * g:2304 + 512 * g + n] = w["w_in"][:, :, s:s + n]
    wi[:, :, 4352:4480] = w["w_in"][:, :, 2444 + 256:2444 + 384]
    wi[:, :, 4480:4608] = w["w_in"][:, :, 2828 + 256:2828 + 384]
    wu = np.zeros((L, D, 2 * DFF), np.float32)
    for gb in range(22):
        wu[:, :, 256 * gb:256 * gb + 128] = w["ffn_w_up"][:, :, 128 * gb:128 * gb + 128]
        wu[:, :, 256 * gb + 128:256 * gb + 256] = w["ffn_w_up"][:, :, DFF + 128 * gb:DFF + 128 * gb + 128]
    wd = np.ascontiguousarray(w["ffn_w_down"].reshape(L, 22, 128, 8, 128).transpose(0, 3, 2, 1, 4)).reshape(L, 8, 128, 22 * 128)
    pf = np.zeros((L, 128, NPF), np.float32)

    def pc(a):
        return a.reshape(L, -1, 128).transpose(0, 2, 1)
    for i in range(4):
        pf[:, :, PF_CA + i:PF_CA + 36:4] = pc(w["conv_a_w"][:, i])
        pf[:, :, PF_CB + i:PF_CB + 8:4] = pc(w["conv_b_w"][:, i])
    pf[:, :, PF_CBB:PF_CBB + 2] = pc(w["conv_b_b"])
    pf[:, :, PF_LBR:PF_LBR + 2] = pc(w["lru_b_r"])
    pf[:, :, PF_LBI:PF_LBI + 2] = pc(w["lru_b_i"])
    pf[:, :, PF_LAM:PF_LAM + 2] = pc(w["lru_lambda"])
    pf[:, :, PF_L1G:PF_L1G + 8] = pc(w["ln1_g"])
    pf[:, :, PF_L1B:PF_L1B + 8] = pc(w["ln1_b"])
    pf[:, :, PF_L2G:PF_L2G + 8] = pc(w["ln2_g"])
    pf[:, :, PF_L2B:PF_L2B + 8] = pc(w["ln2_b"])
    for i in range(3):
        pf[:, :, PF_FCW + i:PF_FCW + 66:3] = pc(w["ffn_conv_w"][:, i])
    pf[:, :, PF_FCB:PF_FCB + 22] = pc(w["ffn_conv_b"])
    rb = np.zeros((L, 128, NRB), np.float32)
    rb[:, :, 0:6] = w["a_log"][:, None, :]
    rb[:, :, 6:12] = w["dt_bias"][:, None, :]
    rb[:, :, 12:76] = w["norm_a_w"][:, None, :]
    rb[:, :, 76:140] = w["norm_c_w"][:, None, :]
    lw = np.zeros((L, 128, 4, 128), np.float32)
    for gi, nm in enumerate(["lru_w_r", "lru_w_i"]):
        for blk in range(2):
            for q in range(2):
                lw[:, 64 * q:64 * q + 64, 2 * gi + blk, 64 * q:64 * q + 64] = w[nm][:, 2 * blk + q]
    w2 = np.zeros((L, 32, 192), np.float32)
    w2[:, 0:16] = w["gla_w2"]
    w2[:, 16] = w["gla_b2"]
    return dict(wi=_pack_panels(wi, WI_PANELS, WI_TOT), wo=_pack_panels(np.ascontiguousarray(w["w_out"]), WO_PANELS, WO_TOT),
                wu=_pack_panels(wu, WU_PANELS, WU_TOT), wd=wd, pf=pf, rb=rb,
                lw=np.ascontiguousarray(lw.reshape(L, 128, 512)), w2=w2, cst=make_consts())


PW = 516
NPAGES = 32
NSLOT = 3
SLOTW = 2048


class TileCtx:
    def __init__(self, kind, ti=0, nprompt=4):
        self.kind = kind
        self.ti = ti
        self.samp = kind == "s"
        self.T = 128 if self.samp else 512
        self.NB = self.T // 128
        self.S = 16 if self.samp else 1
        self.Tt = 8 if self.samp else 512
        self.NS = 16 if self.samp else 1
        self.K = 2 if self.samp else 6
        self.mb = C_MS if self.samp else C_MP
        self.first = (not self.samp) and ti == 0
        self.last = (not self.samp) and ti == nprompt - 1
        self.state_out = self.samp or self.last


def build_program(cfg):
    nc = bass.Bass("TRN2", target_bir_lowering=False)
    B = Builder(nc)
    L = cfg.get("depth", 2)
    NPT = cfg.get("nprompt", 4)
    tiles = cfg.get("tiles", [("p", i) for i in range(NPT)] + [("s", 0)])
    dbg = cfg.get("dbg", {})

    def din(name, shape):
        return V(nc.dram_tensor(name, list(shape), F32, kind="ExternalInput").ap(), ())

    def dout(name, shape):
        return V(nc.dram_tensor(name, list(shape), F32, kind="ExternalOutput").ap(), ())

    xp = din("xp", [2048, D]); xs = din("xs", [128, D])
    sdc = din("sdc", [2, 16, 3, 1152]); sdl = din("sdl", [2, 16, 6, 64, 64]); slc = din("slc", [2, 16, 3, 256])
    slr = din("slr", [2, 16, 256]); sgl = din("sgl", [2, 16, 6, 32, 64]); sfc = din("sfc", [2, 16, 2, DFF])
    wi_p = din("wi", [2, 128, WI_TOT]); wo_p = din("wo", [2, 128, WO_TOT]); wu_p = din("wu", [2, 128, WU_TOT]); wd = din("wd", [2, 8, 128, DFF])

    class _Panels:
        def __init__(self, packed, offs):
            self.packed, self.offs, self.l = packed, offs, None

        def __getitem__(self, l):
            p = _Panels(self.packed, self.offs)
            p.l = l
            return p

        def cols(self, c0, n):
            o = self.offs[(c0, n)]
            return self.packed[self.l][:, o:o + 8 * n]
    wi = _Panels(wi_p, WI_OFF); wo = _Panels(wo_p, WO_OFF); wu = _Panels(wu_p, WU_OFF)
    pfd = din("pf", [2, 128, NPF]); rbd = din("rb", [2, 128, NRB]); lwd = din("lw", [2, 128, 512]); w2d = din("w2", [2, 32, 192])
    cstd = din("cst", [128, NCONST])
    yp = dout("yp", [2048, D]); ys = dout("ys", [128, D])
    o_pdc = dout("o_pdc", [2, 3, 1152]); o_pdl = dout("o_pdl", [2, 6, 64, 64]); o_plc = dout("o_plc", [2, 3, 256])
    o_plr = dout("o_plr", [2, 256]); o_pgl = dout("o_pgl", [2, 6, 32, 64]); o_pfc = dout("o_pfc", [2, 2, DFF])
    o_sdc = dout("o_sdc", [2, 16, 3, 1152]); o_sdl = dout("o_sdl", [2, 16, 6, 64, 64]); o_slc = dout("o_slc", [2, 16, 3, 256])
    o_slr = dout("o_slr", [2, 16, 256]); o_sgl = dout("o_sgl", [2, 16, 6, 32, 64]); o_sfc = dout("o_sfc", [2, 16, 2, DFF])
    dbg_out = {k: dout("dbg_" + k, shp) for k, shp in dbg.items()}

    B.init_psum()
    xT = B.sb("xT", [128, 8, 512])
    slots = [B.sb("slot%d" % i, [128, SLOTW], F32R) for i in range(NSLOT)]
    xin = [B.sb("xin%d" % i, [128, 1024]) for i in range(2)]
    stg = B.sb("stg", [128, 256])
    cst = B.sb("cst", [128, NCONST])
    cstR = B.sb("cstR", [128, 256], F32R)
    pf = [B.sb("pf%d" % l, [128, NPF]) for l in range(2)]
    rb = [B.sb("rb%d" % l, [128, NRB]) for l in range(2)]
    lw = [B.sb("lw%d" % l, [128, 512]) for l in range(2)]
    w2 = [B.sb("w2_%d" % l, [32, 192]) for l in range(2)]
    nea = [B.sb("nea%d" % l, [128, 6]) for l in range(2)]
    lc12 = [B.sb("lc12_%d" % l, [128, 4]) for l in range(2)]
    histA = [B.sb("histA%d" % l, [128, 9, 3]) for l in range(2)]
    histB = [B.sb("histB%d" % l, [128, 2, 3]) for l in range(2)]
    histF = [B.sb("histF%d" % l, [128, 22, 2]) for l in range(2)]
    SAp = [B.sb("SAp%d" % l, [128, 3, 64]) for l in range(2)]
    SCp = [B.sb("SCp%d" % l, [128, 2, 64]) for l in range(2)]
    hlp = [B.sb("hlp%d" % l, [128, 2]) for l in range(2)]
    small = B.sb("small", [128, 256])
    HG = 3
    SOLVE_R = cfg.get("solve_r", False)
    SDT = F32R if SOLVE_R else F32
    hbtR = B.sb("hbtR", [128, 5 * HG, 128], SDT)
    hbt2R = B.sb("hbt2R", [128, 2 * HG, 256], SDT)
    hbt = B.sb("hbt", [128, 4, 128])
    uwb = B.sb("uwb", [128, HG, 320])
    otm_x = B.sb("otm_x", [128, 3, 128])
    gat = B.sb("gat", [128, 12, 24])
    sab = B.sb("sab", [128, 16, 64])
    wxb = B.sb("wxb", [128, 16, 64])
    B.memset(hbt[:, 0, :].kk("hbs0"), 0.0)
    for sl_ in range(HG):
        B.cp(V(hbtR.ap[:, 5 * sl_ + 4, :], ["hbpad%d" % sl_]), hbt[:, 0, :].kk("hbs0"), eng="dve")
    arena_t = B.stack.enter_context(nc.sbuf_tensor("arena", [128, NPAGES, PW], F32))
    arenaR_t = B.stack.enter_context(nc.sbuf_tensor("arenaR", [128, 22, 512], F32R))

    def pg(p0, n=1):
        return V(arena_t[:, p0:p0 + n, :], ["ar%d" % p for p in range(p0, p0 + n)])

    def pgf(p0, n, width):
        assert width <= n * PW
        return V(arena_t[:, p0:p0 + n, :].rearrange("p a b -> p (a b)")[:, 0:width], ["ar%d" % p for p in range(p0, p0 + n)])

    def rpg(p0, n=1):
        return V(arenaR_t[:, p0:p0 + n, :], ["rp%d" % p for p in range(p0, p0 + n)])

    ident = cst[:, C_ID:C_ID + 128]
    ones = cst[:, C_ONE:C_ONE + 128]
    onesR = cstR[:, 0:128]
    b64R = cstR[:, 128:256]

    scnt = [0]

    def sm(n, tag=""):
        o = scnt[0] % 16
        scnt[0] += 1
        return V(small.ap[:, 16 * o:16 * o + n], ["sm%d" % o])

    hcnt = {"a": 0, "b": 0}

    def hbuf(tag=""):
        i = hcnt["a"] % 4
        hcnt["a"] += 1
        return V(hbt.ap[:, i, :], ["hbs%d" % i])

    def hbuf2(tag=""):
        raise RuntimeError("unused")

    B.dma(cst, cstd, group="const")
    for l in range(L):
        B.dma(pf[l], pfd[l], group="const")
        B.dma(rb[l], rbd[l], group="const")
        B.dma(lw[l], lwd[l], group="const")
        B.dma(w2[l], w2d[l], group="const")
    B.cp(cstR, cst[:, C_ONE:C_ONE + 256], eng="dve")
    for l in range(L):
        t = sm(6)
        B.act(t, rb[l][:, 0:6], AF.Exp)
        B.ts(nea[l], t, -1.0, ALU.mult)
        t2 = sm(2)
        B.act(t2, pf[l][:, PF_LAM:PF_LAM + 2], AF.Exp, scale=-1.0)
        t3 = sm(2)
        B.act(t3, t2, AF.Ln, bias=1.0)
        B.ts(lc12[l][:, 0:2], t3, -8.0, ALU.mult)
        B.ts(lc12[l][:, 2:4], t3, -16.0, ALU.mult)
        for tl in (SAp[l], SCp[l], hlp[l], histA[l], histB[l], histF[l]):
            B.memset(tl, 0.0)

    slot_i = [0]

    prefetched = {}

    def prefetch(key, src2d, ncols):
        if key not in prefetched:
            prefetched[key] = fill(src2d, ncols)

    def fill(src2d, ncols, nk=8, key=None):
        if key is not None and key in prefetched:
            return prefetched.pop(key)
        assert nk * ncols <= SLOTW
        s = slots[slot_i[0] % NSLOT]
        slot_i[0] += 1
        sv = s[:, 0:nk * ncols].re("p (c n) -> p c n", c=nk)
        B.dma(s[:, 0:nk * ncols], src2d, eng="pool", group="w:" + s.k[0])
        return sv

    evn = [0]

    def evac(out, in_, scale=None):
        evn[0] += 1
        if scale is not None:
            B.act(out, in_, AF.Copy, scale=scale)
        elif evn[0] % 2:
            B.cp(out, in_, eng="act")
        else:
            B.cp(out, in_, eng="dve")

    def dump(name, view):
        if name in dbg_out:
            B.dma(dbg_out[name], view, group="dbg_" + name)

    xTr = xT.r()

    def proj_fm(sv, j, n, T, rhsT=None):
        ps = B.nps()
        rr = xTr if rhsT is None else rhsT
        TN = max(T, 256)
        for c in range(8):
            B.mm(ps[0:n, 0:TN], sv[:, c, 128 * j:128 * j + n], rr[:, c, 0:TN], start=(c == 0), stop=(c == 7))
        return ps

    def proj_tm(sv, n, blk, c0=0):
        ps = B.nps()
        for c in range(8):
            B.mm(ps[:, 0:n], xTr[:, c, 128 * blk:128 * blk + 128], sv[:, c, c0:c0 + n], start=(c == 0), stop=(c == 7))
        return ps

    def load_x(tc):
        src = xs if tc.samp else xp
        r0 = 0 if tc.samp else tc.ti * 512
        for blk in range(tc.NB):
            xi = xin[blk % 2]
            B.dma(xi, src[r0 + 128 * blk:r0 + 128 * blk + 128, :])
            for half in range(2):
                ps = B.nps()
                for q in range(4):
                    c = 4 * half + q
                    B.tr(ps[:, 128 * q:128 * q + 128], xi[:, 128 * c:128 * c + 128], ident)
                evac(xTr[:, 4 * half:4 * half + 4, 128 * blk:128 * blk + 128], ps.re("p (a b) -> p a b", a=4))

    def store_y(tc):
        dst = ys if tc.samp else yp
        r0 = 0 if tc.samp else tc.ti * 512
        for blk in range(tc.NB):
            xi = xin[blk % 2]
            for half in range(2):
                ps = B.nps()
                for q in range(4):
                    c = 4 * half + q
                    B.tr(ps[:, 128 * q:128 * q + 128], xT[:, c, 128 * blk:128 * blk + 128], ident)
                evac(xi[:, 512 * half:512 * half + 512], ps)
            B.dma(dst[r0 + 128 * blk:r0 + 128 * blk + 128, :], xi)

    def conv_fm(out3, pre3, Tt, wcols, ntap, bias=None):
        if bias is not None:
            B.ts(out3, pre3[:, :, 0:Tt], wcols[0], ALU.mult, bias, ALU.add)
        else:
            B.ts(out3, pre3[:, :, 0:Tt], wcols[0], ALU.mult)
        for i in range(1, ntap):
            B.stt(out3, pre3[:, :, i:i + Tt], wcols[i], out3, ALU.mult, ALU.add)

    def hist_from_state(state2d, nrows, nch, dst_fn):
        R = 16 * nrows
        nb = nch // 128
        for b0 in range(0, nb, 8):
            nbb = min(8, nb - b0)
            xi = xin[(b0 // 8) % 2]
            B.dma(xi[0:R, 0:128 * nbb], state2d[:, 128 * b0:128 * (b0 + nbb)])
            for b in range(nbb):
                ps = B.nps()
                B.tr(ps[:, 0:R], xi[0:R, 128 * b:128 * b + 128], ident[0:R, 0:R])
                evac(dst_fn(b0 + b), ps[:, 0:R].re("p (s r) -> p s r", r=nrows))

    def state_rows_out(tc, ps_tm, ncols, nrows, dst_p, dst_s, col0):
        evac(stg[:, 0:ncols], ps_tm[:, 0:ncols])
        if tc.samp:
            for r in range(nrows):
                base = stg.ap[:, 0:ncols]
                pstep = base.ap[0][0]
                srcv = V(bass.AP(base.tensor, base.offset + (8 - nrows + r) * pstep, [[8 * pstep, 16], [1, ncols]]), stg.k)
                B.dma(dst_s[:, r, col0:col0 + ncols], srcv, group="so")
        else:
            B.dma(dst_p[:, col0:col0 + ncols], stg[128 - nrows:128, 0:ncols], group="so")

    def rms_gate_gen(tc, l, blk, o_tm, z_view, nw, hd0, p_sq=29, p_sz=30):
        o2 = o_tm.re("p h v -> p (h v)")
        sq = pgf(p_sq, 1, 384)
        B.act(sq, o2, AF.Square)
        sz = pgf(p_sz, 1, 384)
        B.act(sz, z_view, AF.Silu)
        yield
        ss = sm(6)
        B.red(ss, sq.re("p (h v) -> p h v", h=6))
        B.ts(ss, ss, 1.0 / 64, ALU.mult, EPS, ALU.add)
        yield
        B.act(ss, ss, AF.Sqrt)
        yield
        rs = sm(6)
        B.recip(rs, ss)
        B.tt(o_tm, o_tm, rs.bc(2, [128, 6, 64]), ALU.mult)
        yield
        B.tt(o_tm, o_tm, nw.bc(1, [128, 6, 64]), ALU.mult)
        yield
        B.tt(o2, o2, sz, ALU.mult)
        ps = B.nps()
        for j in range(3):
            B.tr(ps[:, 128 * j:128 * j + 128], o2[:, 128 * j:128 * j + 128], ident)
        yield
        evac(rpg(hd0, 3)[:, :, 128 * blk:128 * blk + 128], ps[:, 0:384].re("p (a b) -> p a b", a=3))

    def rms_gate_heads(tc, l, blk, o_tm, z_view, nw, hd0, p_sq=29, p_sz=30):
        for _ in rms_gate_gen(tc, l, blk, o_tm, z_view, nw, hd0, p_sq, p_sz):
            pass

    def phase_A(tc, l):
        T, NB, S, Tt, NS, mb = tc.T, tc.NB, tc.S, tc.Tt, tc.NS, tc.mb
        TRI = cst[:, mb + M_TRI:mb + M_TRI + 128]
        SEGM = cst[:, mb + M_SEGM:mb + M_SEGM + 128]
        NSTRIT = cst[:, mb + M_NSTRIT:mb + M_NSTRIT + 128]
        NTRI = cst[:, mb + M_NTRI:mb + M_NTRI + 128]
        SEG = cst[:, mb + M_SEG:mb + M_SEG + 16]
        LAST = cst[:, mb + M_LAST:mb + M_LAST + 16]
        W = 3 + Tt

        def pre(b):
            return pg(b % 3)[:, 0, 0:S * W].re("p (s w) -> p s w", s=S)

        def qk(b):
            return pg(4 + b)[:, 0, 0:T]

        hsA = pgf(3, 1, 9 * 48).re("p (b s r) -> p b s r", b=9, s=16)
        if tc.samp:
            hist_from_state(sdc[l].re("s r n -> (s r) n"), 3, 1152, lambda b: hsA[:, b, :, :])
        for si in range(5):
            c0 = 256 * si
            ncq = 256 if si < 4 else 128
            sv = fill(wi[l].cols(c0, ncq), ncq, key=("A", l, si))
            for j in range(ncq // 128):
                b = 2 * si + j
                if tc.samp:
                    B.cp(pre(b)[:, :, 0:3], hsA[:, b, :, :], eng="dve")
                else:
                    B.cp(pre(b)[:, 0, 0:3], histA[l][:, b, :], eng="dve")
                ps = proj_fm(sv, j, 128, T)
                evac(pre(b)[:, :, 3:3 + Tt], ps[:, 0:T].re("p (s t) -> p s t", s=S))
                if not tc.samp:
                    B.cp(histA[l][:, b, :], pre(b)[:, 0, Tt:Tt + 3], eng="dve")
                o3 = qk(b).re("p (s t) -> p s t", s=S)
                conv_fm(o3, pre(b), Tt, [pf[l][:, PF_CA + 4 * b + i:PF_CA + 4 * b + i + 1] for i in range(4)], 4)
                B.act(qk(b), qk(b), AF.Silu)
            if tc.state_out:
                pt = proj_tm(sv, ncq, tc.NB - 1)
                state_rows_out(tc, pt, ncq, 3, o_pdc[l], o_sdc[l], c0)
        tmA = pgf(13, 4, NB * 396).re("p (b n) -> p b n", b=NB)
        for (zc0, zn) in ((0, 256), (256, 140)):
            sv = fill(wi[l].cols(2304 + zc0, 256), 256)
            for blk in range(NB):
                pt = proj_tm(sv, 256, blk)
                evac(tmA[:, blk, zc0:zc0 + zn], pt[:, 0:zn])
        dump("qkv_silu_%d" % l, pg(4, 9)[:, :, 0:T])
        if cfg.get("a_stop", 99) <= 1:
            return
        sqs = [rpg(8 + b)[:, 0, 0:T] for b in range(6)]
        rns = [pg(19 + b)[:, 0, 0:T] for b in range(6)]
        pss = []
        for b in range(6):
            B.act(sqs[b], qk(b), AF.Square)
        for b in range(6):
            ps = B.nps()
            pss.append(ps)
            B.mm(ps[:, 0:T], b64R, sqs[b])
        for b in range(6):
            B.act(rns[b], pss[b][:, 0:T], AF.Sqrt, bias=EPS)
        for b in range(6):
            B.recip(rns[b], rns[b])
            if b < 3:
                B.stt(qk(b), rns[b], 0.125, qk(b), ALU.mult, ALU.mult)
            else:
                B.tt(qk(b), qk(b), rns[b], ALU.mult)
        dump("qkn_%d" % l, pg(4, 6)[:, :, 0:T])
        if cfg.get("a_stop", 99) <= 2:
            return

        KL = tc.K
        pending_rms = []
        NG = 6 * NB
        gatv = [V(gat.ap[:, i, 0:NG], ["gat%d" % i]) for i in range(12)]

        def g3(v):
            return v.re("p (b h) -> p b h", h=6)
        B.act(g3(gatv[0]), tmA[:, :, 384:390], AF.Sigmoid)
        B.ts(gatv[1], gatv[0], -1.0, ALU.mult)
        B.tt(g3(gatv[8]), tmA[:, :, 390:396], rb[l][:, 6:12].bc(1, [128, NB, 6]), ALU.add)
        B.act(gatv[8], gatv[8], AF.Exp)
        B.act(gatv[8], gatv[8], AF.Ln, bias=1.0)
        B.tt(g3(gatv[2]), g3(gatv[8]), nea[l].bc(1, [128, NB, 6]), ALU.mult)
        psg = B.nps()
        B.mm(psg[:, 0:NG], TRI, gatv[2])
        B.mm(psg[:, 32:32 + NG], SEGM, gatv[2])
        B.cp(gatv[3], psg[:, 0:NG], eng="dve")
        B.act(gatv[4], psg[:, 0:NG], AF.Exp)
        B.act(gatv[5], psg[:, 32:32 + NG], AF.Exp)
        B.tt(gatv[6], psg[:, 32:32 + NG], gatv[3], ALU.subtract)
        B.act(gatv[6], gatv[6], AF.Exp)
        B.tt(gatv[7], gatv[0], gatv[4], ALU.mult)
        B.ts(gatv[11], gatv[3], -1.0, ALU.mult)
        if NS == 1:
            B.ts(gatv[9], gatv[5], LAST[:, 0:1], ALU.mult)
            psl = B.nps()
            B.mm(psl[:, 0:NG], ones, gatv[9])
            B.cp(gatv[10], psl[:, 0:NG], eng="act")
        for blk in range(NB):
            tc0 = 128 * blk
            za = tmA[:, blk, 0:384]
            ba = tmA[:, blk, 384:390]
            aa = tmA[:, blk, 390:396]
            ktm = pgf(17, 1, 384)
            vtm = pgf(18, 1, 384)
            for (dst, b0) in ((ktm, 3), (vtm, 6)):
                ps = B.nps()
                for j in range(3):
                    B.tr(ps[:, 128 * j:128 * j + 128], qk(b0 + j)[:, tc0:tc0 + 128], ident)
                evac(dst, ps[:, 0:384])
            c6 = slice(6 * blk, 6 * blk + 6)
            beta = gatv[0][:, c6]; nbeta = gatv[1][:, c6]; g = gatv[2][:, c6]; gc = gatv[3][:, c6]; egc = gatv[4][:, c6]
            egl = gatv[5][:, c6]; kdf = gatv[6][:, c6]; bexp = gatv[7][:, c6]
            if blk == 0:
                dump("g_%d" % l, g)
                dump("beta_%d" % l, beta)
            DG = pgf(19, 2, 768).re("p (h f) -> p h f", h=6)
            Dm = pgf(21, 2, 768).re("p (h f) -> p h f", h=6)
            DmT = pgf(23, 2, 768).re("p (h f) -> p h f", h=6)
            PE_ = cfg.get("pool_pre", 1)
            B.tt(DG, ident.bc(1, [128, 6, 128]), gc.bc(2, [128, 6, 128]), ALU.mult, eng=("pool" if PE_ else "dve"))
            ngc = gatv[11][:, c6]
            for hf in range(2):
                psr = B.nps()
                B.mm(psr[:, 0:384], ones, DG[:, 3 * hf:3 * hf + 3, :].re("p h f -> p (h f)"))
                R3 = psr[:, 0:384].re("p (h f) -> p h f", h=3)
                d1 = Dm[:, 3 * hf:3 * hf + 3, :]
                d2 = DmT[:, 3 * hf:3 * hf + 3, :]
                B.tt(d1, R3, NSTRIT.bc(1, [128, 3, 128]), ALU.subtract)
                B.tt(d2, R3, NTRI.bc(1, [128, 3, 128]), ALU.add)
                for hh in range(3):
                    h = 3 * hf + hh
                    B.act(Dm[:, h, :], Dm[:, h, :], AF.Exp, bias=gc[:, h:h + 1], scale=-1.0)
                    B.act(DmT[:, h, :], DmT[:, h, :], AF.Exp, bias=ngc[:, h:h + 1])
            bv = pgf(25, 1, 384).re("p (h v) -> p h v", h=6)
            kb = pgf(26, 1, 384).re("p (h v) -> p h v", h=6)
            kdec = pgf(27, 1, 384).re("p (h v) -> p h v", h=6)
            otm = pgf(28 if blk % 2 == 0 else 3, 1, 384).re("p (h v) -> p h v", h=6)
            v3 = vtm.re("p (h v) -> p h v", h=6)
            k3 = ktm.re("p (h v) -> p h v", h=6)
            pe_ = "pool" if PE_ else "dve"
            B.tt(bv, v3, beta.bc(2, [128, 6, 64]), ALU.mult, eng=pe_)
            B.tt(kb, k3, bexp.bc(2, [128, 6, 64]), ALU.mult, eng=pe_)
            B.tt(kdec, k3, kdf.bc(2, [128, 6, 64]), ALU.mult, eng=pe_)
            if NS == 1:
                glb = gatv[10][:, c6]
            else:
                SEL = pgf(31, 1, 96)
                glb = pgf(31, 1, 192)[:, 96:192].re("p (h s) -> p h s", h=6)
                B.tt(SEL.re("p (h s) -> p h s", h=6), egl.bc(2, [128, 6, 16]), LAST.bc(1, [128, 6, 16]), ALU.mult)
                psl = B.nps()
                B.mm(psl[:, 0:96], ones, SEL)
                B.cp(glb, psl[:, 0:96].re("p (h s) -> p h s", h=6), eng="act")
            SAs = sab

            def head_gen(h, slot):
                hp, po = h // 2, (h % 2) * 64
                kT = qk(3 + hp)[po:po + 64, tc0:tc0 + 128]
                qT = qk(hp)[po:po + 64, tc0:tc0 + 128]
                hb = [V(hbtR.ap[:, 5 * slot + i, :], ["hb%d_%d" % (slot, i)]) for i in range(4)]
                hw = [V(hbt2R.ap[:, 2 * slot + i, :], ["hc%d_%d" % (slot, i)]) for i in range(2)]
                Nm, NmT, Pa, Pb = hb
                Wa, Wb = hw
                NN = V(hbtR.ap[:, 5 * slot:5 * slot + 2, :].rearrange("p a b -> p (a b)"), Nm.k + NmT.k)
                PPa = V(hbtR.ap[:, 5 * slot + 2:5 * slot + 4, :].rearrange("p a b -> p (a b)"), Pa.k + Pb.k)
                PPb = V(hbtR.ap[:, 5 * slot + 3:5 * slot + 5, :].rearrange("p a b -> p (a b)"), Pb.k)
                ps = B.nps()
                B.mm(ps[:, 0:128], kT, kT)
                B.mm(ps[:, 128:256], kT, qT)
                yield
                B.stt(Nm, ps[:, 0:128], nbeta[:, h:h + 1], Dm[:, h, :], ALU.mult, ALU.mult)
                qkmT = V(otm_x.ap[:, slot, :], ["qkm%d" % slot])
                B.tt(qkmT, ps[:, 128:256], DmT[:, h, :], ALU.mult)
                ps = B.nps()
                B.tr(ps[:, 0:128], Nm.f(), ident)
                yield
                B.cp(NmT, ps[:, 0:128], eng="dve")
                B.tt(Wa[:, 128:256], ps[:, 0:128], ident, ALU.add)
                ps = B.nps()
                ps2 = B.nps()
                if SOLVE_R:
                    B.mm(ps[:, 0:256], NmT, NN)
                    B.mm(ps2[:, 0:256], Nm, NN)
                else:
                    B.mm(ps[:, 0:128], NmT, Nm)
                    B.mm(ps2[:, 128:256], Nm, NmT)
                yield
                B.cp(Pa, ps[:, 0:128], eng="act")
                B.cp(Wa[:, 0:128], ps2[:, 128:256], eng="dve")
                Pc, Pn, Wc, Wn, PPc, PPn = Pa, Pb, Wa, Wb, PPa, PPb
                for k in range(1, KL + 1):
                    lastk = k == KL
                    ps = B.nps()
                    if lastk and not SOLVE_R:
                        B.mm(ps[:, 128:256], Pc, Wc[:, 128:256])
                    else:
                        B.mm(ps[:, 0:256], Pc, Wc)
                    if not lastk:
                        ps2 = B.nps()
                        if SOLVE_R:
                            B.mm(ps2[:, 0:256], Wc[:, 0:128], PPc)
                        else:
                            B.mm(ps2[:, 0:128], Wc[:, 0:128], Pc)
                    yield
                    B.tt(Wn[:, 128:256], Wc[:, 128:256].f(), ps[:, 128:256], ALU.add)
                    if not lastk:
                        B.cp(Wn[:, 0:128], ps[:, 0:128], eng="dve")
                        B.cp(Pn, ps2[:, 0:128], eng="act")
                    Pc, Pn, Wc, Wn, PPc, PPn = Pn, Pc, Wn, Wc, PPn, PPc
                AT = Wc[:, 128:256].f()
                u_sb = V(uwb.ap[:, slot, 0:64], ["uw%d_0" % slot])
                w_sb = V(uwb.ap[:, slot, 64:128], ["uw%d_1" % slot])
                qSe = V(uwb.ap[:, slot, 128:192], ["uw%d_2" % slot])
                wkT = V(uwb.ap[:, slot, 192:320], ["uw%d_3" % slot])
                ps = B.nps()
                B.mm(ps[:, 0:64], AT, bv[:, h, :])
                B.mm(ps[po:po + 64, 128:256], kb[:, h, :], AT)
                yield
                B.cp(u_sb, ps[:, 0:64], eng="dve")
                B.cp(wkT[po:po + 64, :], ps[po:po + 64, 128:256], eng="dve")
                if NS == 1:
                    Sh = SAp[l][po:po + 64, hp, :]
                    ps = B.nps()
                    B.mm(ps[:, 0:64], wkT[po:po + 64, :], Sh)
                    B.mm(ps[:, 64:128], qT, Sh)
                    yield
                    B.tt(w_sb, u_sb, ps[:, 0:64], ALU.subtract)
                    B.ts(qSe, ps[:, 64:128], egc[:, h:h + 1], ALU.mult)
                else:
                    if h % 2 == 0:
                        for q in range(2):
                            B.dma(SAs[64 * q:64 * q + 64, :, :], sdl[l][:, h + q, :, :].re("s d v -> d s v"), group="sa")
                    ps = B.nps()
                    for s in range(16):
                        B.mm(ps[0:64, 8 * s:8 * s + 8], SAs[po:po + 64, s, :], wkT[po:po + 64, 8 * s:8 * s + 8])
                        B.mm(ps[0:64, 128 + 8 * s:128 + 8 * s + 8], SAs[po:po + 64, s, :], qT[:, 8 * s:8 * s + 8])
                    yield
                    cTa = hbuf()
                    cTb = hbuf()
                    B.cp(cTa[0:64, :], ps[0:64, 0:128], eng="act")
                    B.cp(cTb[0:64, :], ps[0:64, 128:256], eng="act")
                    ps = B.nps()
                    B.tr(ps[:, 0:64], cTa[0:64, :], ident[0:64, 0:64])
                    B.tr(ps[:, 64:128], cTb[0:64, :], ident[0:64, 0:64])
                    yield
                    B.tt(w_sb, u_sb, ps[:, 0:64], ALU.subtract)
                    B.act(qSe, ps[:, 64:128], AF.Copy, scale=egc[:, h:h + 1])
                ps = B.nps()
                B.mm(ps[:, 0:64], qkmT, w_sb)
                if NS == 1:
                    B.mm(ps[po:po + 64, 128:192], kdec[:, h, :], w_sb)
                    yield
                    B.tt(otm[:, h, :], qSe, ps[:, 0:64], ALU.add)
                    B.stt(Sh, Sh, glb[po:po + 64, h:h + 1], ps[po:po + 64, 128:192], ALU.mult, ALU.add)
                else:
                    yield
                    B.tt(otm[:, h, :], qSe, ps[:, 0:64], ALU.add)
                    Wexp = wxb
                    B.tt(Wexp, w_sb.bc(1, [128, 16, 64]), SEG.bc(2, [128, 16, 64]), ALU.mult)
                    for hf in range(2):
                        psU = B.nps()
                        B.mm(psU[po:po + 64, 0:512], kdec[:, h, :], Wexp[:, 8 * hf:8 * hf + 8, :].re("p s v -> p (s v)"))
                        Sv = SAs[po:po + 64, 8 * hf:8 * hf + 8, :]
                        B.tt(Sv, Sv, glb[po:po + 64, h, 8 * hf:8 * hf + 8].bc(2, [64, 8, 64]), ALU.mult)
                        B.tt(Sv, Sv, psU[po:po + 64, 0:512].re("p (s v) -> p s v", s=8), ALU.add)
                    if h % 2 == 1:
                        for q in range(2):
                            B.dma(o_sdl[l][:, h - 1 + q, :, :].re("s d v -> d s v"), SAs[64 * q:64 * q + 64, :, :], group="sa")

            G = cfg.get('g_prompt', 3) if NS == 1 else 2
            for h0 in range(0, 6, G):
                gens = [head_gen(h0 + i, i) for i in range(G) if h0 + i < 6]
                if h0 == 0 and pending_rms:
                    gens.append(pending_rms.pop())
                while gens:
                    for gen in list(gens):
                        try:
                            next(gen)
                        except StopIteration:
                            gens.remove(gen)
            if blk == 0:
                dump("oa_raw_%d" % l, otm.re("p h v -> p (h v)"))
            pending_rms.append(rms_gate_gen(tc, l, blk, otm, za, rb[l][:, 12:76], 0))
        for gen in pending_rms:
            for _ in gen:
                pass
        if tc.last:
            for h in range(6):
                hp, po = h // 2, (h % 2) * 64
                B.dma(o_pdl[l][h], SAp[l][po:po + 64, hp, :], group="pdl")

    def phase_B(tc, l):
        T, S, Tt = tc.T, tc.S, tc.Tt
        W = 3 + Tt

        def pre(cb):
            return pg(cb)[:, 0, 0:S * W].re("p (s w) -> p s w", s=S)

        if tc.samp:
            hist_from_state(slc[l].re("s r n -> (s r) n"), 3, 256, lambda b: pre(b)[:, :, 0:3])
            xi = xin[0]
            B.dma(xi[0:16, 0:256], slr[l])
            h0 = pgf(16, 1, 32).re("p (c s) -> p c s", c=2)
            for cb in range(2):
                ps = B.nps()
                B.tr(ps[:, 0:16], xi[0:16, 128 * cb:128 * cb + 128], ident[0:16, 0:16])
                evac(h0[:, cb, :], ps[:, 0:16])
            hl_s = pgf(17, 1, 32).re("p (c s) -> p c s", c=2)
        else:
            B.cp(pg(0, 2)[:, :, 0:3], histB[l], eng="dve")
        sv = fill(wi[l].cols(1152, 256), 256)
        if tc.state_out:
            pt = proj_tm(sv, 256, tc.NB - 1)
            state_rows_out(tc, pt, 256, 3, o_plc[l], o_slc[l], 0)
        for cb in range(2):
            ps = proj_fm(sv, cb, 128, T)
            evac(pre(cb)[:, :, 3:3 + Tt], ps[:, 0:T].re("p (s t) -> p s t", s=S))
        if not tc.samp:
            B.cp(histB[l], pg(0, 2)[:, :, 512:515], eng="dve")
        svg = fill(wi[l].cols(1408, 256), 256)
        def gen_b(cb):
            xc = pg(2 + cb)[:, 0, 0:T]
            conv_fm(xc.re("p (s t) -> p s t", s=S), pre(cb), Tt,
                    [pf[l][:, PF_CB + 4 * cb + i:PF_CB + 4 * cb + i + 1] for i in range(4)], 4,
                    bias=pf[l][:, PF_CBB + cb:PF_CBB + cb + 1])
            yield
            rs = pg(4 + cb)[:, 0, 0:T]
            is_ = pg(6 + cb)[:, 0, 0:T]
            a = pg(8 + cb)[:, 0, 0:T]
            sq = pg(10 + cb)[:, 0, 0:T]
            hh = pg(12 + cb)[:, 0, 0:T]
            psr = B.nps()
            B.mm(psr[:, 0:T], lw[l][:, 128 * cb:128 * cb + 128], xc)
            B.act(rs, psr[:, 0:T], AF.Sigmoid, bias=pf[l][:, PF_LBR + cb:PF_LBR + cb + 1])
            psi = B.nps()
            B.mm(psi[:, 0:T], lw[l][:, 256 + 128 * cb:256 + 128 * cb + 128], xc)
            B.act(is_, psi[:, 0:T], AF.Sigmoid, bias=pf[l][:, PF_LBI + cb:PF_LBI + cb + 1])
            yield
            B.act(a, rs, AF.Exp, scale=lc12[l][:, cb:cb + 1])
            B.act(sq, rs, AF.Exp, scale=lc12[l][:, 2 + cb:3 + cb])
            B.act(sq, sq, AF.Sqrt, bias=1.0, scale=-1.0)
            yield
            B.tt(is_, is_, xc, ALU.mult)
            B.tt(is_, is_, sq, ALU.mult)
            yield
            if tc.samp:
                for s in range(16):
                    B.scan(hh[:, 8 * s:8 * s + 8], a[:, 8 * s:8 * s + 8], is_[:, 8 * s:8 * s + 8], h0[:, cb, s:s + 1])
                B.cp(hl_s[:, cb, :], hh.re("p (s t) -> p s t", t=8)[:, :, 7], eng="dve")
            else:
                B.scan(hh, a, is_, hlp[l][:, cb:cb + 1])
                B.cp(hlp[l][:, cb:cb + 1], hh[:, T - 1:T], eng="dve")
            if cb == 0:
                dump("h_lru_%d" % l, hh)
            psg = proj_fm(svg, cb, 128, T)
            gg = pg(14 + cb)[:, 0, 0:T]
            B.act(gg, psg[:, 0:T], AF.Gelu_apprx_tanh)
            B.tt(rpg(3 + cb)[:, 0, 0:T], hh, gg, ALU.mult)
        gens = [gen_b(0), gen_b(1)]
        while gens:
            for gen in list(gens):
                try:
                    next(gen)
                except StopIteration:
                    gens.remove(gen)
        if tc.samp:
            for cb in range(2):
                ps = B.nps()
                B.tr(ps[0:16, 0:128], hl_s[:, cb, :], ident)
                evac(stg[0:16, 128 * cb:128 * cb + 128], ps[0:16, 0:128])
            B.dma(o_slr[l], stg[0:16, 0:256], group="so")
        elif tc.last:
            B.dma(o_plr[l].re("(c p) -> p c", p=128), hlp[l], group="plr", nc_ok=True)

    def phase_C(tc, l):
        T, NB, S, Tt, NS, mb = tc.T, tc.NB, tc.S, tc.Tt, tc.NS, tc.mb
        TRI = cst[:, mb + M_TRI:mb + M_TRI + 128]
        SEGM = cst[:, mb + M_SEGM:mb + M_SEGM + 128]
        SEG = cst[:, mb + M_SEG:mb + M_SEG + 16]
        qcT = [pg(0 + g)[:, 0, 0:T] for g in range(2)]
        kcT = [pg(2 + g)[:, 0, 0:T] for g in range(2)]
        lcT = pg(4)[0:32, 0, 0:T]
        vct = pgf(5, 3, NB * 384).re("p (b n) -> p b n", b=NB)
        zct = pgf(8, 3, NB * 384).re("p (b n) -> p b n", b=NB)
        kct = pgf(11, 2, NB * 192).re("p (b n) -> p b n", b=NB)
        sv = fill(wi[l].cols(1664, 256), 256)
        for g in range(2):
            ps = proj_fm(sv, g, 96, T)
            evac(qcT[g][0:96, :], ps[0:96, 0:T], scale=32.0 ** -0.5)
        sv = fill(wi[l].cols(1920, 256), 256)
        for g in range(2):
            ps = proj_fm(sv, g, 96, T)
            evac(kcT[g][0:96, :], ps[0:96, 0:T])
        sv = fill(wi[l].cols(2176, 16), 16)
        B.memset(lcT, 1.0)
        ps = proj_fm(sv, 0, 16, T)
        evac(lcT[0:16, :], ps[0:16, 0:T])
        for (c0, dsts) in ((2816, [(vct, 0, 0, 256)]), (4352, [(vct, 256, 0, 128), (zct, 256, 128, 128)]),
                           (3328, [(zct, 0, 0, 256)]), (3840, [(kct, 0, 0, 192)])):
            sv = fill(wi[l].cols(c0, 256), 256)
            for blk in range(NB):
                pt = proj_tm(sv, 256, blk)
                for (dst, d0, p0, n) in dsts:
                    evac(dst[:, blk, d0:d0 + n], pt[:, p0:p0 + n])
        SCs = sab
        pre_c = {}
        pending_rms_c = []

        def pre_gen_c(blk):
            tc0 = 128 * blk
            pA = pgf(20 + 3 * blk, 1, 384)
            pB = pgf(21 + 3 * blk, 1, 224)
            pC = pgf(22 + 3 * blk, 1, 512)
            logf = pA[:, 0:192]
            b_sb = pA[:, 192:384]
            kdc = pB[:, 0:192]
            ebl = pB[:, 192:224].re("p (g s) -> p g s", g=2)
            qt = pC[:, 0:256].re("p (g t) -> p g t", g=2)
            kt = pC[:, 256:512].re("p (g t) -> p g t", g=2)
            psl = B.nps()
            B.mm(psl[:, 0:192], lcT[:, tc0:tc0 + 128], w2[l])
            yield
            B.act(logf, psl[:, 0:192], AF.Exp, scale=-1.0)
            B.act(logf, logf, AF.Ln, bias=1.0)
            B.ts(logf, logf, -1.0 / 16, ALU.mult)
            if blk == 0:
                dump("logf_%d" % l, logf)
            psb = B.nps()
            B.mm(psb[:, 0:192], TRI, logf)
            B.mm(psb[:, 256:448], SEGM, logf)
            psT = B.nps()
            for g in range(2):
                B.mm(psT[0:96, 128 * g:128 * g + 128], logf[:, 96 * g:96 * g + 96], TRI)
                B.mm(psT[0:96, 256 + 16 * g:256 + 16 * g + 16], logf[:, 96 * g:96 * g + 96], SEG)
            yield
            B.cp(b_sb, psb[:, 0:192], eng="dve")
            B.tt(kdc, psb[:, 256:448], b_sb, ALU.subtract)
            B.act(kdc, kdc, AF.Exp)
            B.tt(kdc, kdc, kct[:, blk, :], ALU.mult)
            B.act(qt[0:96], psT[0:96, 0:256].re("p (g t) -> p g t", g=2), AF.Exp)
            B.act(kt[0:96], psT[0:96, 0:256].re("p (g t) -> p g t", g=2), AF.Exp, scale=-1.0)
            B.act(ebl[0:96], psT[0:96, 256:288].re("p (g s) -> p g s", g=2), AF.Exp)
            for g in range(2):
                B.tt(qt[0:96, g, :], qt[0:96, g, :], qcT[g][0:96, tc0:tc0 + 128], ALU.mult)
                B.tt(kt[0:96, g, :], kt[0:96, g, :], kcT[g][0:96, tc0:tc0 + 128], ALU.mult)
            pre_c[blk] = (kdc, ebl, qt, kt)

        for b0 in range(0, NB, 2):
            gens = [pre_gen_c(b0 + i) for i in range(2) if b0 + i < NB]
            while gens:
                for gen in list(gens):
                    try:
                        next(gen)
                    except StopIteration:
                        gens.remove(gen)
        for blk in range(NB):
            tc0 = 128 * blk
            kdc, ebl, qt, kt = pre_c[blk]
            otm = pgf(19 if blk % 2 == 0 else 15, 1, 384).re("p (h v) -> p h v", h=6)

            def head_gen_c(h):
                g, po = h // 3, (h % 3) * 32
                vh = vct[:, blk, 64 * h:64 * h + 64]
                ps = B.nps()
                B.mm(ps[:, 0:128], kt[po:po + 32, g, :], qt[po:po + 32, g, :])
                yield
                attm = hbuf()
                B.tt(attm, ps[:, 0:128], TRI, ALU.mult)
                if NS == 1:
                    Sh = SCp[l][po:po + 32, g, :]
                    ps = B.nps()
                    B.mm(ps[:, 0:64], qt[po:po + 32, g, :], Sh, start=True, stop=False)
                    B.mm(ps[:, 0:64], attm, vh, start=False, stop=True)
                    B.mm(ps[po:po + 32, 128:192], kdc[:, 32 * h:32 * h + 32], vh)
                    yield
                    B.cp(otm[:, h, :], ps[:, 0:64], eng="dve")
                    B.stt(Sh, Sh, ebl[po:po + 32, g, 0:1], ps[po:po + 32, 128:192], ALU.mult, ALU.add)
                else:
                    if h % 3 == 0:
                        for q in range(3):
                            B.dma(SCs[32 * q:32 * q + 32, :, :], sgl[l][:, h + q, :, :].re("s k v -> k s v"), group="sa")
                    ps = B.nps()
                    for s in range(16):
                        B.mm(ps[0:64, 8 * s:8 * s + 8], SCs[po:po + 32, s, :], qt[po:po + 32, g, 8 * s:8 * s + 8])
                    yield
                    cT = hbuf()
                    B.cp(cT[0:64, :], ps[0:64, 0:128], eng="act")
                    ps = B.nps()
                    B.tr(ps[:, 0:64], cT[0:64, :], ident[0:64, 0:64])
                    B.mm(ps[:, 64:128], attm, vh)
                    yield
                    qS = hbuf()[:, 0:64]
                    B.cp(qS, ps[:, 0:64], eng="act")
                    B.tt(otm[:, h, :], qS, ps[:, 64:128], ALU.add)
                    Vexp = wxb
                    B.tt(Vexp, vh.bc(1, [128, 16, 64]), SEG.bc(2, [128, 16, 64]), ALU.mult)
                    for hf in range(2):
                        psU = B.nps()
                        B.mm(psU[po:po + 32, 0:512], kdc[:, 32 * h:32 * h + 32], Vexp[:, 8 * hf:8 * hf + 8, :].re("p s v -> p (s v)"))
                        Sv = SCs[po:po + 32, 8 * hf:8 * hf + 8, :]
                        B.tt(Sv, Sv, ebl[po:po + 32, g, 8 * hf:8 * hf + 8].bc(2, [32, 8, 64]), ALU.mult)
                        B.tt(Sv, Sv, psU[po:po + 32, 0:512].re("p (s v) -> p s v", s=8), ALU.add)
                    if h % 3 == 2:
                        for q in range(3):
                            B.dma(o_sgl[l][:, h - 2 + q, :, :].re("s k v -> k s v"), SCs[32 * q:32 * q + 32, :, :], group="sa")

            GC = cfg.get("g_c", 3)
            for h0 in range(0, 6, GC):
                gens = [head_gen_c(h0 + i) for i in range(GC) if h0 + i < 6]
                if h0 == 0 and pending_rms_c:
                    gens.append(pending_rms_c.pop())
                while gens:
                    for gen in list(gens):
                        try:
                            next(gen)
                        except StopIteration:
                            gens.remove(gen)
            if blk == 0:
                dump("oc_raw_%d" % l, otm.re("p h v -> p (h v)"))
            pending_rms_c.append(rms_gate_gen(tc, l, blk, otm, zct[:, blk, :], rb[l][:, 76:140], 5, p_sq=13, p_sz=14))
        for gen in pending_rms_c:
            for _ in gen:
                pass
        if tc.last:
            for h in range(6):
                g, po = h // 3, (h % 3) * 32
                B.dma(o_pgl[l][h], SCp[l][po:po + 32, g, :], group="pgl")

    def layer_norm(tc, l, gcol, bcol):
        T = tc.T
        psS = B.nps()
        psQ = B.nps()
        for m in range(8):
            y = pg(m)[:, 0, 0:T]
            ysq = rpg(8 + m % 2)[:, 0, 0:T]
            yr = rpg(10 + m % 2)[:, 0, 0:T]
            B.act(ysq, y, AF.Square)
            B.cp(yr, y, eng="act")
            B.mm(psS[:, 0:T], onesR, yr, start=(m == 0), stop=(m == 7))
            B.mm(psQ[:, 0:T], onesR, ysq, start=(m == 0), stop=(m == 7))
        mean = pg(13)[:, 0, 0:T]
        rstd = pg(14)[:, 0, 0:T]
        msq = pg(15)[:, 0, 0:T]
        B.act(msq, psS[:, 0:T], AF.Square, scale=1.0 / D)
        B.stt(rstd, psQ[:, 0:T], 1.0 / D, msq, ALU.mult, ALU.subtract)
        B.act(rstd, rstd, AF.Sqrt, bias=EPS)
        B.recip(rstd, rstd)
        for m in range(8):
            y = pg(m)[:, 0, 0:T]
            le = "pool" if (m % 2 == 1 and cfg.get("ln_pool", 0)) else "dve"
            B.stt(y, psS[:, 0:T], -1.0 / D, y, ALU.mult, ALU.add)
            B.tt(y, y, rstd, ALU.mult, eng=le)
            B.act(xTr[:, m, 0:T], y, AF.Identity, bias=pf[l][:, bcol + m:bcol + m + 1], scale=pf[l][:, gcol + m:gcol + m + 1])

    def wout_ln1(tc, l):
        T = tc.T
        hd = rpg(0, 8)
        m = 0
        for (c0, ncol) in ((0, 256), (256, 256), (512, 256), (768, 256)):
            sv = fill(wo[l].cols(c0, ncol), ncol)
            for j in range(ncol // 128):
                ps = proj_fm(sv, j, 128, T, rhsT=hd)
                B.stt(pg(m)[:, 0, 0:T], xT[:, m, 0:T], float(ALPHA), ps[:, 0:T], ALU.mult, ALU.add)
                m += 1
        for gb in range(2):
            prefetch(("U", l, gb), wu[l].cols(256 * gb, 256), 256)
        layer_norm(tc, l, PF_L1G, PF_L1B)

    def ffn_ln2(tc, l):
        T, S, Tt = tc.T, tc.S, tc.Tt
        W = 2 + Tt
        hT = rpg(0, 22)
        if tc.samp:
            hs = pgf(16, 2, 22 * 32).re("p (b s r) -> p b s r", b=22, s=16)
            hist_from_state(sfc[l].re("s r n -> (s r) n"), 2, DFF, lambda b: hs[:, b, :, :])
        for gb in range(22):
            sv = fill(wu[l].cols(256 * gb, 256), 256, key=("U", l, gb))
            gpre = pg(8 + gb % 3)[:, 0, 0:S * W].re("p (s w) -> p s w", s=S)
            if tc.samp:
                B.cp(gpre[:, :, 0:2], hs[:, gb, :, :], eng="act")
            else:
                B.cp(gpre[:, 0, 0:2], histF[l][:, gb, :], eng="act")
            psg = proj_fm(sv, 0, 128, T)
            B.cp(gpre[:, :, 2:2 + Tt], psg[:, 0:T].re("p (s t) -> p s t", s=S), eng="act")
            psv = proj_fm(sv, 1, 128, T)
            if not tc.samp:
                B.cp(histF[l][:, gb, :], gpre[:, 0, Tt:Tt + 2], eng="act")
            gcv = pg(11 + gb % 2)[:, 0, 0:T]
            conv_fm(gcv.re("p (s t) -> p s t", s=S), gpre, Tt,
                    [pf[l][:, PF_FCW + 3 * gb + i:PF_FCW + 3 * gb + i + 1] for i in range(3)], 3)
            B.act(gcv, gcv, AF.Gelu_apprx_tanh, bias=pf[l][:, PF_FCB + gb:PF_FCB + gb + 1])
            B.tt(hT[:, gb, 0:T], gcv, psv[:, 0:T], ALU.mult)
            if tc.state_out:
                pt = proj_tm(sv, 256, tc.NB - 1)
                state_rows_out(tc, pt, 128, 2, o_pfc[l], o_sfc[l], 128 * gb)
        for m in range(8):
            ps = B.nps()
            for a in range(2):
                s_ = slots[slot_i[0] % NSLOT]
                slot_i[0] += 1
                B.dma(s_[:, 0:1408], wd[l][m][:, 1408 * a:1408 * a + 1408], eng="pool", group="w:" + s_.k[0])
                sv = s_[:, 0:1408].re("p (c n) -> p c n", c=11)
                TN = max(T, 256)
                for c in range(11):
                    B.mm(ps[:, 0:TN], sv[:, c, :], hT[:, 11 * a + c, 0:TN], start=(a == 0 and c == 0), stop=(a == 1 and c == 10))
            B.stt(pg(m)[:, 0, 0:T], xT[:, m, 0:T], float(ALPHA), ps[:, 0:T], ALU.mult, ALU.add)
        nxt = next_layer.get((tc.kind, tc.ti, l))
        if nxt is not None:
            for si in range(2):
                prefetch(("A", nxt, si), wi[nxt].cols(256 * si, 256), 256)
        layer_norm(tc, l, PF_L2G, PF_L2B)

    stages = cfg.get("stages", "ABCWF")
    next_layer = {}
    seq = [(k, t, l) for (k, t) in tiles for l in range(L)]
    if "A" in stages:
        for a, b in zip(seq[:-1], seq[1:]):
            next_layer[a] = b[2]
    for (kind, ti) in tiles:
        tc = TileCtx(kind, ti, NPT)
        load_x(tc)
        for l in range(L):
            if "A" in stages:
                phase_A(tc, l)
            if "B" in stages:
                phase_B(tc, l)
            if "C" in stages:
                phase_C(tc, l)
            if "W" in stages:
                dump("heads_%d" % l, rpg(0, 8).f()[:, :, 0:tc.T])
                wout_ln1(tc, l)
                dump("x1_%d" % l, xT[:, :, 0:tc.T])
            if "F" in stages:
                ffn_ln2(tc, l)
                dump("x2_%d" % l, xT[:, :, 0:tc.T])
        store_y(tc)
    with B.stack:
        stats = B.P.build()
    return nc, stats


_W_NAMES = ["w_in", "conv_a_w", "a_log", "dt_bias", "norm_a_w", "conv_b_w", "conv_b_b", "lru_w_r", "lru_b_r", "lru_w_i", "lru_b_i",
            "lru_lambda", "gla_w2", "gla_b2", "norm_c_w", "w_out", "ln1_g", "ln1_b", "ffn_w_up", "ffn_conv_w", "ffn_conv_b",
            "ffn_w_down", "ln2_g", "ln2_b"]


def make_in_maps(inputs, cores):
    w = {k: np.asarray(inputs[k], np.float32) for k in _W_NAMES}
    pk = pack_weights(w)
    maps = []
    for c in cores:
        m = dict(pk)
        m["xp"] = np.ascontiguousarray(inputs["x_prompt"][c])
        m["xs"] = np.ascontiguousarray(inputs["x_sample"][16 * c:16 * c + 16].reshape(128, D))
        for nm, key in (("sdc", "state_delta_conv"), ("sdl", "state_delta"), ("slc", "state_lru_conv"), ("slr", "state_lru"),
                        ("sgl", "state_gla"), ("sfc", "state_ffn_conv")):
            m[nm] = np.ascontiguousarray(inputs[key][:, 16 * c:16 * c + 16])
        maps.append(m)
    return maps


def kernel(**inputs):
    inputs = {k: np.asarray(v) for k, v in inputs.items()}
    nc, stats = build_program({})
    maps = make_in_maps(inputs, list(range(NCORES)))
    res = run_bass_kernel_spmd(nc, maps, core_ids=list(range(NCORES)))
    r = res.results
    y_prompt = np.stack([r[c]["yp"] for c in range(NCORES)]).reshape(8, 2048, D)
    y_sample = np.concatenate([r[c]["ys"].reshape(16, 8, D) for c in range(NCORES)], axis=0)
    outs = [y_prompt, y_sample]
    for nm in ["o_pdc", "o_pdl", "o_plc", "o_plr", "o_pgl", "o_pfc"]:
        outs.append(np.stack([r[c][nm] for c in range(NCORES)], axis=1))
    for nm in ["o_sdc", "o_sdl", "o_slc", "o_slr", "o_sgl", "o_sfc"]:
        outs.append(np.concatenate([r[c][nm] for c in range(NCORES)], axis=1))
    return tuple(np.ascontiguousarray(o, dtype=np.float32) for o in outs)
```
